# Optimizing a Trainium2 kernel written in Bass

```python
import math
import jax
import jax.numpy as jnp
from jax import lax
import numpy as np

D_MODEL = 1024
BATCH = 16
SEQ = 256
DEPTH = 4
DEC_BATCH = 8
DEC_SEQ = 1024
PAST_LEN = 512

GRID_W = 64
HEAD_DIM = 64
BRANCH_WIDTH = 512
N_BRANCH = 4
A_HEADS = BRANCH_WIDTH // HEAD_DIM
A_KV_HEADS = 2
A_GROUP = A_HEADS // A_KV_HEADS
A_WINDOW = 128
A_BLOCK = 128
Q_BLOCK = 128
B_HEADS = BRANCH_WIDTH // HEAD_DIM
DELTA_CHUNK = 64
CONV_K = 5
C_HEADS = BRANCH_WIDTH // HEAD_DIM
HGRN_CHUNK = 16
D_HEADS = BRANCH_WIDTH // HEAD_DIM
NH_ROWS = 8
NH_COLS = 16
NH_QCOLS = 16
NH_KCOLS = 32
D_FF = 4 * D_MODEL
ROPE_BASE = 10000.0
ATTN_SCALE = HEAD_DIM ** -0.5
EPS = 1e-6
NEG = -1e30
IN_SPLITS = (
    A_HEADS * HEAD_DIM, A_KV_HEADS * HEAD_DIM, A_KV_HEADS * HEAD_DIM,
    BRANCH_WIDTH, BRANCH_WIDTH, BRANCH_WIDTH, BRANCH_WIDTH, 2 * B_HEADS, 2 * B_HEADS,
    BRANCH_WIDTH, 2 * BRANCH_WIDTH, BRANCH_WIDTH, BRANCH_WIDTH,
    BRANCH_WIDTH, BRANCH_WIDTH, BRANCH_WIDTH,
    N_BRANCH * D_MODEL,
)
N_IN = sum(IN_SPLITS)

kernel_name = 'hybrid_flow_backbone_step'


def rmsnorm(x, w):
    xf = x.astype(jnp.float32)
    y = xf * lax.rsqrt(jnp.mean(xf * xf, axis=-1, keepdims=True) + EPS)
    return (y * w.astype(jnp.float32)).astype(x.dtype)


def l2norm(x):
    xf = x.astype(jnp.float32)
    return xf * lax.rsqrt(jnp.sum(xf * xf, axis=-1, keepdims=True) + EPS)


def adaln(cvec, w, b):
    return jax.nn.silu(cvec) @ w + b


def flip(t):
    return jnp.flip(t, axis=1)


def rope_2d(x):
    L = x.shape[1]
    t = jnp.arange(L)
    half = HEAD_DIM // 2
    quarter = half // 2
    inv = ROPE_BASE ** (-jnp.arange(quarter, dtype=jnp.float32) / quarter)

    def rot(xa, pos):
        ang = pos.astype(jnp.float32)[:, None] * inv[None, :]
        cos = jnp.cos(ang)[None, :, None, :].astype(x.dtype)
        sin = jnp.sin(ang)[None, :, None, :].astype(x.dtype)
        x1, x2 = xa[..., :quarter], xa[..., quarter:]
        return jnp.concatenate([x1 * cos - x2 * sin, x2 * cos + x1 * sin], axis=-1)

    return jnp.concatenate([rot(x[..., :half], t // GRID_W), rot(x[..., half:], t % GRID_W)], axis=-1)


def softmax_with_sink(s, sink):
    m = jnp.max(s, axis=-1, keepdims=True)
    if sink is not None:
        m = jnp.maximum(m, sink)
    p = jnp.exp(s - m)
    den = jnp.sum(p, axis=-1, keepdims=True)
    if sink is not None:
        den = den + jnp.exp(sink - m)
    return p / den


def dense_attn(q, k, v, sink):
    b, lq, nkv, ng, dh = q.shape
    nb = lq // Q_BLOCK
    qb = jnp.moveaxis(q.reshape(b, nb, Q_BLOCK, nkv, ng, dh), 1, 0)
    sink_b = None if sink is None else sink.astype(jnp.float32)[None, :, :, None, None]

    def block(qi):
        s = jnp.einsum('bqkgd,bskd->bkgqs', qi, k, preferred_element_type=jnp.float32) * ATTN_SCALE
        p = softmax_with_sink(s, sink_b)
        return jnp.einsum('bkgqs,bskd->bqkgd', p.astype(v.dtype), v)

    o = lax.map(block, qb)
    return jnp.moveaxis(o, 0, 1).reshape(b, lq, nkv, ng, dh)


def window_attn_latent(q, k, v, ck, cv, sink):
    b, L, nkv, ng, dh = q.shape
    nb = L // A_BLOCK
    pad = ((0, 0), (A_BLOCK, A_BLOCK), (0, 0), (0, 0))
    kp = jnp.pad(k, pad)
    vp = jnp.pad(v, pad)
    sink_b = sink.astype(jnp.float32)[None, :, :, None, None]
    cv = cv.astype(v.dtype)

    def block(i):
        start = i * A_BLOCK
        qi = lax.dynamic_slice_in_dim(q, start, A_BLOCK, axis=1)
        ki = lax.dynamic_slice_in_dim(kp, start, 3 * A_BLOCK, axis=1)
        vi = lax.dynamic_slice_in_dim(vp, start, 3 * A_BLOCK, axis=1)
        qpos = start + jnp.arange(A_BLOCK)
        kpos = start - A_BLOCK + jnp.arange(3 * A_BLOCK)
        ok = (kpos[None, :] >= 0) & (kpos[None, :] < L) & (jnp.abs(qpos[:, None] - kpos[None, :]) <= A_WINDOW)
        s_loc = jnp.einsum('bqkgd,bskd->bkgqs', qi, ki, preferred_element_type=jnp.float32) * ATTN_SCALE
        s_loc = jnp.where(ok, s_loc, NEG)
        s_ctx = jnp.einsum('bqkgd,bskd->bkgqs', qi, ck, preferred_element_type=jnp.float32) * ATTN_SCALE
        p = softmax_with_sink(jnp.concatenate([s_loc, s_ctx], axis=-1), sink_b)
        vv = jnp.concatenate([vi, cv], axis=1)
        return jnp.einsum('bkgqs,bskd->bqkgd', p.astype(v.dtype), vv)

    o = lax.map(block, jnp.arange(nb))
    return jnp.moveaxis(o, 0, 1).reshape(b, L, nkv, ng, dh)


def neighbourhood_attn_latent(q, k, v, ck, cv, rpb):
    b, L, nh, dh = q.shape
    rows = L // GRID_W
    kh = min(NH_ROWS, rows)
    ncb = GRID_W // NH_QCOLS
    qcol = np.arange(GRID_W).reshape(ncb, NH_QCOLS)
    cs0 = np.clip(qcol[:, 0] - NH_COLS // 2, 0, GRID_W - NH_KCOLS)
    kcol = cs0[:, None] + np.arange(NH_KCOLS)
    wstart = np.clip(qcol - NH_COLS // 2, 0, GRID_W - NH_COLS)
    col_ok = (kcol[:, None, :] >= wstart[:, :, None]) & (kcol[:, None, :] < wstart[:, :, None] + NH_COLS)
    dc_idx = np.clip(kcol[:, None, :] - qcol[:, :, None] + NH_COLS - 1, 0, 2 * NH_COLS - 2)
    rpb_c = rpb.astype(jnp.float32)[:, :, dc_idx]
    qg = q.reshape(b, rows, ncb, NH_QCOLS, nh, dh)
    kg = k.reshape(b, rows, GRID_W, nh, dh)
    vg = v.reshape(b, rows, GRID_W, nh, dh)
    cv = cv.astype(v.dtype)
    n_loc = kh * NH_KCOLS

    def row(r):
        rs = jnp.clip(r - kh // 2, 0, rows - kh)
        qr = lax.dynamic_index_in_dim(qg, r, axis=1, keepdims=False)
        kb = lax.dynamic_slice_in_dim(kg, rs, kh, axis=1)[:, :, kcol]
        vb = lax.dynamic_slice_in_dim(vg, rs, kh, axis=1)[:, :, kcol]
        s_loc = jnp.einsum('bcqhd,bicjhd->bhcqij', qr, kb, preferred_element_type=jnp.float32) * ATTN_SCALE
        dr_idx = rs + jnp.arange(kh) - r + NH_ROWS - 1
        bias = jnp.transpose(jnp.take(rpb_c, dr_idx, axis=1), (0, 2, 3, 1, 4))
        s_loc = jnp.where(col_ok[:, :, None, :], s_loc + bias, NEG).reshape(b, nh, ncb, NH_QCOLS, n_loc)
        s_ctx = jnp.einsum('bcqhd,bshd->bhcqs', qr, ck, preferred_element_type=jnp.float32) * ATTN_SCALE
        p = softmax_with_sink(jnp.concatenate([s_loc, s_ctx], axis=-1), None).astype(v.dtype)
        p_loc = p[..., :n_loc].reshape(b, nh, ncb, NH_QCOLS, kh, NH_KCOLS)
        o = jnp.einsum('bhcqij,bicjhd->bcqhd', p_loc, vb)
        return o + jnp.einsum('bhcqs,bshd->bcqhd', p[..., n_loc:], cv)

    o = lax.map(row, jnp.arange(rows))
    return jnp.moveaxis(o, 0, 1).reshape(b, L, nh, dh)


def short_conv(x, w):
    y = lax.conv_general_dilated(x, w[:, None, :].astype(x.dtype), window_strides=(1,),
                                 padding=[(CONV_K // 2, CONV_K // 2)],
                                 dimension_numbers=('NWC', 'WIO', 'NWC'),
                                 feature_group_count=x.shape[-1])
    return jax.nn.silu(y)


def gated_delta_chunked(q, k, v, g, beta, s0):
    b, L, h, dk = q.shape
    dv = v.shape[-1]
    C = DELTA_CHUNK
    n = L // C

    def chunks(t):
        return jnp.swapaxes(t.astype(jnp.float32).reshape((b, n, C) + t.shape[2:]), 2, 3)

    qc, kc, vc, bc = chunks(q), chunks(k), chunks(v), chunks(beta)
    gc = jnp.cumsum(chunks(g), axis=-1)
    tri_incl = np.tril(np.ones((C, C), dtype=bool))
    tri_strict = np.tril(np.ones((C, C), dtype=bool), k=-1)
    decay = jnp.exp(jnp.where(tri_incl, gc[..., :, None] - gc[..., None, :], NEG))
    a_mat = jnp.where(tri_strict, bc[..., :, None] * jnp.einsum('bnhtd,bnhjd->bnhtj', kc, kc) * decay, 0.0)
    rhs = jnp.concatenate([bc[..., None] * vc, (bc * jnp.exp(gc))[..., None] * kc], axis=-1)
    sol = lax.linalg.triangular_solve(a_mat + jnp.eye(C, dtype=jnp.float32), rhs,
                                      left_side=True, lower=True, unit_diagonal=True)
    w_v, w_k = sol[..., :dv], sol[..., dv:]
    p_qk = jnp.einsum('bnhtd,bnhjd->bnhtj', qc, kc) * decay
    q_dec = qc * jnp.exp(gc)[..., None]
    k_dec = kc * jnp.exp(gc[..., -1:] - gc)[..., None]
    chunk_decay = jnp.exp(gc[..., -1])

    def step(s, xs):
        wv, wk, pqk, qd, kd, cd = xs
        u = wv - jnp.einsum('bhtk,bhkv->bhtv', wk, s)
        o = jnp.einsum('bhtk,bhkv->bhtv', qd, s) + jnp.einsum('bhtj,bhjv->bhtv', pqk, u)
        s = cd[..., None, None] * s + jnp.einsum('bhtk,bhtv->bhkv', kd, u)
        return s, o

    xs = tuple(jnp.moveaxis(t, 1, 0) for t in (w_v, w_k, p_qk, q_dec, k_dec, chunk_decay))
    s_final, o = lax.scan(step, s0.astype(jnp.float32), xs)
    o = jnp.transpose(o, (1, 0, 3, 2, 4)).reshape(b, L, h, dv)
    return o, s_final


def hgrn2_chunked(q, k, v, log_f, s0):
    b, L, h, dk = q.shape
    dv = v.shape[-1]
    C = HGRN_CHUNK
    n = L // C

    def chunks(t):
        return jnp.swapaxes(t.astype(jnp.float32).reshape((b, n, C) + t.shape[2:]), 2, 3)

    qc, kc, vc = chunks(q), chunks(k), chunks(v)
    bcum = jnp.cumsum(chunks(log_f), axis=-2)
    tri_incl = np.tril(np.ones((C, C), dtype=bool))
    pair = jnp.exp(jnp.where(tri_incl[:, :, None], bcum[..., :, None, :] - bcum[..., None, :, :], NEG))
    att = jnp.einsum('bnhtd,bnhtjd,bnhjd->bnhtj', qc, pair, kc)
    o_intra = jnp.einsum('bnhtj,bnhjv->bnhtv', att, vc)
    q_dec = qc * jnp.exp(bcum)
    k_dec = kc * jnp.exp(bcum[..., -1:, :] - bcum)
    chunk_decay = jnp.exp(bcum[..., -1, :])

    def step(s, xs):
        kd, vv, cd = xs
        return cd[..., None] * s + jnp.einsum('bhtk,bhtv->bhkv', kd, vv), s

    xs = tuple(jnp.moveaxis(t, 1, 0) for t in (k_dec, vc, chunk_decay))
    s_final, s_start = lax.scan(step, s0.astype(jnp.float32), xs)
    s_start = jnp.moveaxis(s_start, 0, 1)
    o = o_intra + jnp.einsum('bnhtk,bnhkv->bnhtv', q_dec, s_start)
    o = jnp.transpose(o, (0, 1, 3, 2, 4)).reshape(b, L, h, dv)
    return o, s_final


def trunk_layer(x, mod, lp, ctx):
    b, L, _ = x.shape
    f32 = jnp.float32
    shift1, scale1, gate1, shift2, scale2, gate2 = jnp.split(mod[:, None, :], 6, axis=-1)
    h = rmsnorm(x, lp['norm_w'][0]) * (1.0 + scale1) + shift1
    splits = [int(o) for o in np.cumsum(IN_SPLITS)[:-1]]
    (a_q, a_k, a_v, b_q, b_k, b_v, b_z, b_a, b_b, c_q, c_f, c_i, c_g,
     d_q, d_k, d_v, g_logit) = jnp.split(h @ lp['w_in'], splits, axis=-1)

    a_q = a_q.reshape(b, L, A_HEADS, HEAD_DIM)
    a_k = a_k.reshape(b, L, A_KV_HEADS, HEAD_DIM)
    a_v = a_v.reshape(b, L, A_KV_HEADS, HEAD_DIM)
    sink = lp['attn_sink'].reshape(A_KV_HEADS, A_GROUP)
    if ctx is None:
        y_a = dense_attn(a_q.reshape(b, L, A_KV_HEADS, A_GROUP, HEAD_DIM), a_k, a_v, sink)
    else:
        y_a = window_attn_latent(rope_2d(a_q).reshape(b, L, A_KV_HEADS, A_GROUP, HEAD_DIM),
                                 rope_2d(a_k), a_v, ctx[0], ctx[1], sink)
    y_a = y_a.reshape(b, L, BRANCH_WIDTH)

    qkv = short_conv(jnp.concatenate([b_q, b_k, b_v], axis=-1), lp['delta_conv'])
    b_q, b_k, b_v = jnp.split(qkv, 3, axis=-1)
    b_q = l2norm(b_q.reshape(b, L, B_HEADS, HEAD_DIM)) * ATTN_SCALE
    b_k = l2norm(b_k.reshape(b, L, B_HEADS, HEAD_DIM))
    b_v = b_v.reshape(b, L, B_HEADS, HEAD_DIM)
    b_a = b_a.reshape(b, L, 2, B_HEADS).astype(f32)
    b_b = b_b.reshape(b, L, 2, B_HEADS).astype(f32)
    g = -jnp.exp(lp['delta_a_log'].astype(f32)) * jax.nn.softplus(b_a + lp['delta_dt_bias'].astype(f32))
    beta = jax.nn.sigmoid(b_b)
    if ctx is None:
        s0_f = jnp.zeros((b, B_HEADS, HEAD_DIM, HEAD_DIM), f32)
        s0_b = s0_f
    else:
        s0_f, s0_b = ctx[4][:, 0], ctx[4][:, 1]
    o_f, sd_f = gated_delta_chunked(b_q, b_k, b_v, g[:, :, 0], beta[:, :, 0], s0_f)
    o_b, sd_b = gated_delta_chunked(flip(b_q), flip(b_k), flip(b_v), flip(g[:, :, 1]), flip(beta[:, :, 1]), s0_b)
    y_b = rmsnorm(o_f + flip(o_b), lp['delta_norm_w']) * jax.nn.silu(b_z.reshape(b, L, B_HEADS, HEAD_DIM))
    y_b = y_b.reshape(b, L, BRANCH_WIDTH)

    lb = lp['hgrn_lb']
    c_f = c_f.reshape(b, L, 2, C_HEADS, HEAD_DIM).astype(f32)
    log_f = jnp.logaddexp(jnp.log(lb), jnp.log1p(-lb) + jax.nn.log_sigmoid(c_f))
    c_k = (1.0 - lb) * jax.nn.sigmoid(-c_f)
    c_q = jax.nn.silu(c_q).reshape(b, L, C_HEADS, HEAD_DIM)
    c_i = c_i.reshape(b, L, C_HEADS, HEAD_DIM)
    if ctx is None:
        h0_f = jnp.zeros((b, C_HEADS, HEAD_DIM, HEAD_DIM), f32)
        h0_b = h0_f
    else:
        h0_f, h0_b = ctx[5][:, 0], ctx[5][:, 1]
    o_f, sc_f = hgrn2_chunked(c_q, c_k[:, :, 0], c_i, log_f[:, :, 0], h0_f)
    o_b, sc_b = hgrn2_chunked(flip(c_q), flip(c_k[:, :, 1]), flip(c_i), flip(log_f[:, :, 1]), h0_b)
    y_c = rmsnorm((o_f + flip(o_b)) * jax.nn.sigmoid(c_g.reshape(b, L, C_HEADS, HEAD_DIM)), lp['hgrn_norm_w'])
    y_c = y_c.reshape(b, L, BRANCH_WIDTH)

    d_q = d_q.reshape(b, L, D_HEADS, HEAD_DIM)
    d_k = d_k.reshape(b, L, D_HEADS, HEAD_DIM)
    d_v = d_v.reshape(b, L, D_HEADS, HEAD_DIM)
    if ctx is None:
        y_d = dense_attn(d_q.reshape(b, L, D_HEADS, 1, HEAD_DIM), d_k, d_v, None)
    else:
        y_d = neighbourhood_attn_latent(d_q, d_k, d_v, ctx[2], ctx[3], lp['na_rpb'])
    y_d = y_d.reshape(b, L, BRANCH_WIDTH)

    ys = jnp.stack([y_a.astype(x.dtype), y_b.astype(x.dtype), y_c.astype(x.dtype), y_d.astype(x.dtype)], axis=2)
    y_proj = jnp.einsum('blkw,kwd->blkd', ys, lp['w_branch'])
    gates = jax.nn.sigmoid(g_logit.reshape(b, L, N_BRANCH, D_MODEL))
    merged = jnp.sum(gates * y_proj, axis=2)
    x = x + gate1 * (merged @ lp['w_out'])

    h2 = rmsnorm(x, lp['norm_w'][1]) * (1.0 + scale2) + shift2
    x = x + gate2 * (jnp.square(jax.nn.relu(h2 @ lp['mlp_w1'])) @ lp['mlp_w2'])

    if ctx is None:
        new_ctx = (a_k, a_v, d_k, d_v,
                   jnp.stack([sd_f, sd_b], axis=1).astype(x.dtype),
                   jnp.stack([sc_f, sc_b], axis=1).astype(x.dtype))
        return x, new_ctx
    return x, None


def setup_inputs(seed: int = 0) -> dict:
    key = jax.random.key(seed)
    ks = jax.random.split(key, 32)
    D = D_MODEL

    def nrm(k, shape, s):
        return jax.random.normal(k, shape, jnp.float32) * s

    dt = jnp.exp(jax.random.uniform(ks[16], (DEPTH, 2, B_HEADS), minval=math.log(1e-3), maxval=math.log(1e-1)))
    return {
        'x_prompt': nrm(ks[0], (BATCH, SEQ, D), 1.0),
        'x_sample': nrm(ks[1], (DEC_BATCH, DEC_SEQ, D), 1.0),
        'cache_attn_k': nrm(ks[2], (DEC_BATCH, DEPTH, PAST_LEN, A_KV_HEADS, HEAD_DIM), 1.0),
        'cache_attn_v': nrm(ks[3], (DEC_BATCH, DEPTH, PAST_LEN, A_KV_HEADS, HEAD_DIM), 1.0),
        'cache_na_k': nrm(ks[4], (DEC_BATCH, DEPTH, PAST_LEN, D_HEADS, HEAD_DIM), 1.0),
        'cache_na_v': nrm(ks[5], (DEC_BATCH, DEPTH, PAST_LEN, D_HEADS, HEAD_DIM), 1.0),
        'state_delta': nrm(ks[6], (DEC_BATCH, DEPTH, 2, B_HEADS, HEAD_DIM, HEAD_DIM), HEAD_DIM ** -0.5),
        'state_hgrn': nrm(ks[7], (DEC_BATCH, DEPTH, 2, C_HEADS, HEAD_DIM, HEAD_DIM), HEAD_DIM ** -0.5),
        'c': nrm(ks[8], (DEC_BATCH, D), 1.0),
        'c_ctx': nrm(ks[9], (D,), 1.0),
        'norm_w': 1.0 + nrm(ks[10], (DEPTH, 2, D), 0.02),
        'ada_w': nrm(ks[11], (DEPTH, D, 6 * D), 0.5 * D ** -0.5),
        'ada_b': nrm(ks[12], (DEPTH, 6 * D), 0.02),
        'w_in': nrm(ks[13], (DEPTH, D, N_IN), D ** -0.5),
        'attn_sink': nrm(ks[14], (DEPTH, A_HEADS), 1.0),
        'delta_conv': nrm(ks[15], (DEPTH, CONV_K, 3 * BRANCH_WIDTH), CONV_K ** -0.5),
        'delta_a_log': jnp.log(jax.random.uniform(ks[17], (DEPTH, 2, B_HEADS), minval=1.0, maxval=16.0)),
        'delta_dt_bias': dt + jnp.log(-jnp.expm1(-dt)),
        'delta_norm_w': 1.0 + nrm(ks[18], (DEPTH, HEAD_DIM), 0.02),
        'hgrn_lb': nrm(ks[19], (DEPTH, 2, C_HEADS * HEAD_DIM), 1.0),
        'hgrn_norm_w': 1.0 + nrm(ks[20], (DEPTH, HEAD_DIM), 0.02),
        'na_rpb': nrm(ks[21], (DEPTH, D_HEADS, 2 * NH_ROWS - 1, 2 * NH_COLS - 1), 0.1),
        'w_branch': nrm(ks[22], (DEPTH, N_BRANCH, BRANCH_WIDTH, D), BRANCH_WIDTH ** -0.5),
        'w_out': nrm(ks[23], (DEPTH, D, D), D ** -0.5),
        'mlp_w1': nrm(ks[24], (DEPTH, D, D_FF), D ** -0.5),
        'mlp_w2': nrm(ks[25], (DEPTH, D_FF, D), D_FF ** -0.5),
        'final_norm_w': 1.0 + nrm(ks[26], (D,), 0.02),
    }


def reference(x_prompt, x_sample, cache_attn_k, cache_attn_v, cache_na_k, cache_na_v, state_delta, state_hgrn,
              c, c_ctx, norm_w, ada_w, ada_b, w_in, attn_sink, delta_conv, delta_a_log, delta_dt_bias,
              delta_norm_w, hgrn_lb, hgrn_norm_w, na_rpb, w_branch, w_out, mlp_w1, mlp_w2, final_norm_w):
    lb = jnp.cumsum(jax.nn.softmax(hgrn_lb.astype(jnp.float32), axis=0), axis=0)
    lb = lb - lb[:1]
    xp, xs = x_prompt, x_sample
    ak_l, av_l, nk_l, nv_l, sd_l, sh_l = [], [], [], [], [], []
    for l in range(DEPTH):
        lp = {
            'norm_w': norm_w[l], 'w_in': w_in[l], 'attn_sink': attn_sink[l],
            'delta_conv': delta_conv[l], 'delta_a_log': delta_a_log[l], 'delta_dt_bias': delta_dt_bias[l],
            'delta_norm_w': delta_norm_w[l], 'hgrn_lb': lb[l].reshape(2, C_HEADS, HEAD_DIM),
            'hgrn_norm_w': hgrn_norm_w[l], 'na_rpb': na_rpb[l], 'w_branch': w_branch[l],
            'w_out': w_out[l], 'mlp_w1': mlp_w1[l], 'mlp_w2': mlp_w2[l],
        }
        mod_ctx = adaln(c_ctx[None, :], ada_w[l], ada_b[l])
        mod_lat = adaln(c, ada_w[l], ada_b[l])
        xp, (ak, av, nk, nv, sd, sh) = trunk_layer(xp, mod_ctx, lp, None)
        ak_l.append(ak)
        av_l.append(av)
        nk_l.append(nk)
        nv_l.append(nv)
        sd_l.append(sd)
        sh_l.append(sh)
        ctx = (cache_attn_k[:, l], cache_attn_v[:, l], cache_na_k[:, l], cache_na_v[:, l],
               state_delta[:, l], state_hgrn[:, l])
        xs, _ = trunk_layer(xs, mod_lat, lp, ctx)
    y_prompt = rmsnorm(xp, final_norm_w)
    y_sample = rmsnorm(xs, final_norm_w)
    new_attn_k = jnp.stack(ak_l, axis=1)
    new_attn_v = jnp.stack(av_l, axis=1)
    new_na_k = jnp.stack(nk_l, axis=1)
    new_na_v = jnp.stack(nv_l, axis=1)
    new_state_delta = jnp.stack(sd_l, axis=1)
    new_state_hgrn = jnp.stack(sh_l, axis=1)
    return (y_prompt, y_sample, new_attn_k, new_attn_v, new_na_k, new_na_v, new_state_delta, new_state_hgrn)
```

```python
import math
from contextlib import ExitStack, contextmanager
import numpy as np
import ml_dtypes
import concourse.bass as bass
import concourse.mybir as mybir
from concourse.bass_utils import run_bass_kernel_spmd

F32 = mybir.dt.float32
BF16 = mybir.dt.bfloat16
ALU = mybir.AluOpType
AF = mybir.ActivationFunctionType

D = 1024
DEPTH = 4
NCORES = 8
NP_ = 512
NS_ = 1024
PAST = 512
EPS = 1e-6
NEGM = -30000.0
O_AQ, O_AK, O_AV = 0, 512, 640
O_BQ, O_BK, O_BV, O_BZ, O_BA, O_BB = 768, 1280, 1792, 2304, 2816, 2832
O_CQ, O_CF, O_CI, O_CG = 2848, 3360, 4384, 4896
O_DQ, O_DK, O_DV, O_G = 5408, 5920, 6432, 6944
N_IN = 11040

C_ID, C_ONE = 0, 128
C_U = (256, 320)
C_W2 = (384, 448)
C_NM = (512, 576)
C_NMT = (640, 704)
C_SL = (768, 832)
C_MT = (896, 960)
C_ROPE, C_COLM, C_MPREV, C_MNEXT = 1024, 1088, 1152, 1280
C_ZERO = 1408
NCST = 1472
MID = (32, 31)


def make_consts():
    c = np.zeros((128, NCST), np.float32)
    c[:, C_ID:C_ID + 128] = np.eye(128)
    c[:, C_ONE:C_ONE + 128] = 1.0
    t = np.arange(64)
    for d in range(2):
        if d == 0:
            U = (t[:, None] <= t[None, :]).astype(np.float32)
        else:
            U = (t[:, None] >= t[None, :]).astype(np.float32)
        c[:64, C_U[d]:C_U[d] + 64] = U
        c[:64, C_W2[d]:C_W2[d] + 64] = 1.0 - U
        incl = U.T
        c[:64, C_NM[d]:C_NM[d] + 64] = np.where(incl > 0, 0.0, NEGM)
        c[:64, C_NMT[d]:C_NMT[d] + 64] = np.where(incl.T > 0, 0.0, NEGM)
        c[:64, C_SL[d]:C_SL[d] + 64] = incl - np.eye(64)
        c[:64, C_MT[d]:C_MT[d] + 64] = incl.T
    P = np.zeros((64, 64), np.float32)
    for half in (0, 32):
        for i in range(16):
            P[half + i, half + i + 16] = -1.0
            P[half + 16 + i, half + i] = 1.0
    c[:64, C_ROPE:C_ROPE + 64] = P.T
    qc = np.arange(64)
    ws = np.clip(qc - 8, 0, 48)
    kc = np.arange(64)
    c[:64, C_COLM:C_COLM + 64] = ((kc[:, None] >= ws[None, :]) & (kc[:, None] < ws[None, :] + 16)).astype(np.float32)
    k = np.arange(128)
    c[:, C_MPREV:C_MPREV + 128] = (k[:, None] >= k[None, :]).astype(np.float32)
    c[:, C_MNEXT:C_MNEXT + 128] = (k[:, None] <= k[None, :]).astype(np.float32)
    return c


def make_rope_tab():
    tt = np.arange(NS_)
    inv = (10000.0 ** (-np.arange(16, dtype=np.float32) / 16)).astype(np.float32)
    tab = np.zeros((64, 2, NS_), np.float32)
    for half, pos in ((0, tt // 64), (32, tt % 64)):
        ang = pos.astype(np.float32)[None, :] * inv[:, None]
        cs, sn = np.cos(ang).astype(np.float32), np.sin(ang).astype(np.float32)
        tab[half:half + 16, 0], tab[half + 16:half + 32, 0] = cs, cs
        tab[half:half + 16, 1], tab[half + 16:half + 32, 1] = sn, sn
    return tab


class V:
    __slots__ = ("ap", "toks")

    def __init__(self, ap, toks=()):
        self.ap = ap
        self.toks = toks

    def __getitem__(self, idx):
        return V(self.ap[idx], self.toks)

    def bc(self, shape):
        return V(self.ap.broadcast_to(list(shape)), self.toks)

    def un(self, axis):
        return V(self.ap.unsqueeze(axis), self.toks)

    def re(self, pat, **kw):
        return V(self.ap.rearrange(pat, **kw), self.toks)

    def bitcast(self, dt):
        return V(self.ap.bitcast(dt), self.toks)


class Slot:
    def __init__(self, key, sem):
        self.key, self.sem, self.cnt = key, sem, 0


class KB:
    def __init__(self, nc, es):
        self.nc = nc
        self.es = es
        self.eng = {"pe": nc.tensor, "act": nc.scalar, "dve": nc.vector, "pool": nc.gpsimd, "sp": nc.sync}
        self.semh = {}
        self.cnt = {}
        for e in ("pe", "act", "dve", "pool"):
            self.semh[e] = es.enter_context(nc.semaphore("s_" + e))
            self.cnt[e] = 0
        self.seen = {e: {} for e in self.eng}
        self.lastw = {}
        self.readers = {}
        self.slots = []
        self.ntile = 0
        self.scopes = []

    def tile(self, shape, dt, name=None):
        self.ntile += 1
        nm = "%s_%d" % (name or "t", self.ntile)
        st = self.scopes[-1] if self.scopes else self.es
        h = st.enter_context(self.nc.sbuf_tensor(nm, list(shape), dt))
        v = V(h.ap(), (nm,))
        self.memset("pool", v, 0.0)
        return v

    def slot(self):
        s = Slot("d%d" % len(self.slots), self.es.enter_context(self.nc.semaphore("sd%d" % len(self.slots))))
        self.semh[s.key] = s.sem
        self.slots.append(s)
        return s

    @contextmanager
    def scope(self):
        st = ExitStack()
        self.scopes.append(st)
        try:
            yield
        finally:
            self.barrier()
            self.scopes.pop()
            st.close()

    def barrier(self):
        marks = [(e, c) for e, c in self.cnt.items() if c > 0] + [(s.key, s.cnt) for s in self.slots if s.cnt > 0]
        for e in ("pe", "act", "dve", "pool", "sp"):
            self._wait(e, marks, True)

    def _wait(self, e, marks, full=False):
        need = {}
        for (k, v) in marks:
            if k == e and (e == "pe" or full):
                continue
            if need.get(k, 0) < v:
                need[k] = v
        for k, v in need.items():
            if self.seen[e].get(k, 0) < v:
                self.eng[e].wait_ge(self.semh[k], v)
                self.seen[e][k] = v

    def _deps(self, reads, writes):
        marks = []
        for t in reads:
            if t in self.lastw:
                marks.append(self.lastw[t])
        for t in writes:
            if t in self.lastw:
                marks.append(self.lastw[t])
            marks.extend(self.readers.get(t, {}).items())
        return marks

    def _record(self, mark, reads, writes):
        for t in reads:
            r = self.readers.setdefault(t, {})
            if r.get(mark[0], 0) < mark[1]:
                r[mark[0]] = mark[1]
        for t in writes:
            self.lastw[t] = mark
            self.readers[t] = {}

    def op(self, e, fn, ins, outs, sig=True):
        reads = [t for v in ins for t in v.toks]
        writes = [t for v in outs for t in v.toks]
        writes = writes + [t for t in reads if t.startswith("ps") and t not in writes]
        self._wait(e, self._deps(reads, writes))
        inst = fn()
        if sig:
            self.cnt[e] += 1
            inst.then_inc(self.semh[e], 1)
            mark = (e, self.cnt[e])
        else:
            mark = (e, self.cnt[e] + 1)
        self._record(mark, reads, writes)

    def dma(self, q, out, in_, slot, **kw):
        reads, writes = list(in_.toks), list(out.toks)
        if isinstance(slot, list):
            slot.append(slot.pop(0))
            slot = slot[-1]
        marks = self._deps(reads, writes)
        if slot.cnt > 0:
            marks.append((slot.key, slot.cnt))
        self._wait(q, marks)
        inst = self.eng[q].dma_start(out=out.ap, in_=in_.ap, **kw)
        slot.cnt += 16
        inst.then_inc(slot.sem, 16)
        self._record((slot.key, slot.cnt), reads, writes)

    def mm(self, out, lhsT, rhs, start=True, stop=True, sig=True):
        self.op("pe", lambda: self.nc.tensor.matmul(out.ap, lhsT.ap, rhs.ap, start=start, stop=stop),
                [lhsT, rhs] + ([] if start else [out]), [out], sig)

    def tr(self, out, in_, ident, sig=True):
        self.mm(out, in_, ident, sig=sig)

    def act(self, out, in_, func, bias=0.0, scale=1.0):
        ins = [in_] + [x for x in (bias, scale) if isinstance(x, V)]
        b = bias.ap if isinstance(bias, V) else bias
        s = scale.ap if isinstance(scale, V) else scale
        self.op("act", lambda: self.nc.scalar.activation(out.ap, in_.ap, func, bias=b, scale=s), ins, [out])

    def _ve(self, e):
        return self.nc.vector if e == "dve" else self.nc.gpsimd

    def tt(self, e, out, in0, in1, op):
        self.op(e, lambda: self._ve(e).tensor_tensor(out.ap, in0.ap, in1.ap, op), [in0, in1], [out])

    def ts(self, e, out, in0, s1, s2=None, op0=ALU.mult, op1=None):
        ins = [in0] + [x for x in (s1, s2) if isinstance(x, V)]
        a = s1.ap if isinstance(s1, V) else s1
        b = s2.ap if isinstance(s2, V) else s2
        if op1 is None:
            self.op(e, lambda: self._ve(e).tensor_scalar(out.ap, in0.ap, a, None, op0), ins, [out])
        else:
            self.op(e, lambda: self._ve(e).tensor_scalar(out.ap, in0.ap, a, b, op0, op1), ins, [out])

    def stt(self, e, out, in0, sc, in1, op0, op1):
        ins = [in0, in1] + ([sc] if isinstance(sc, V) else [])
        a = sc.ap if isinstance(sc, V) else sc
        self.op(e, lambda: self._ve(e).scalar_tensor_tensor(out.ap, in0.ap, a, in1.ap, op0, op1), ins, [out])

    def cp(self, e, out, in_):
        if e == "act":
            self.op("act", lambda: self.nc.scalar.copy(out.ap, in_.ap), [in_], [out])
        else:
            self.op(e, lambda: self._ve(e).tensor_copy(out.ap, in_.ap), [in_], [out])

    def memset(self, e, out, val):
        self.op(e, lambda: self._ve(e).memset(out.ap, val), [], [out])

    def recip(self, out, in_):
        self.op("dve", lambda: self.nc.vector.reciprocal(out.ap, in_.ap), [in_], [out])

    def finish(self):
        marks = [(e, c) for e, c in self.cnt.items() if c > 0] + [(s.key, s.cnt) for s in self.slots if s.cnt > 0]
        self._wait("sp", marks, True)


class StopBuild(Exception):
    pass


class Ring:
    def __init__(self, K, n, shape, dt, name="r"):
        self.t = [K.tile(shape, dt, name) for _ in range(n)]
        self.i = 0

    def next(self):
        v = self.t[self.i % len(self.t)]
        self.i += 1
        return v


def build_program(nlayers=DEPTH, stage=99):
    holder = {}
    try:
        return _build_program(nlayers, stage, holder)
    except StopBuild:
        holder["K"].finish()
        return holder["nc"]


def _build_program(nlayers, stage, holder):
    nc = bass.Bass("TRN2", target_bir_lowering=False)

    def din(name, shape, dt=F32):
        return V(nc.dram_tensor(name, list(shape), dt, kind="ExternalInput").ap())

    def dout(name, shape, dt=F32):
        return V(nc.dram_tensor(name, list(shape), dt, kind="ExternalOutput").ap())

    x_p = din("x_p", [NP_, D])
    x_s = din("x_s", [NS_, D])
    cak = din("cak", [DEPTH, PAST, 128])
    cav = din("cav", [DEPTH, PAST, 128])
    cnk = din("cnk", [DEPTH, PAST, 512])
    cnv = din("cnv", [DEPTH, PAST, 512])
    sd_in = din("sd", [DEPTH, 2, 8, 64, 64])
    sh_in = din("sh", [DEPTH, 2, 8, 64, 64])
    cvec = din("cvec", [2, D])
    norm_w = din("norm_w", [DEPTH, 2, D])
    ada_w = din("ada_w", [DEPTH, D, 6 * D])
    ada_b = din("ada_b", [DEPTH, 6 * D])
    w_in = din("w_in", [DEPTH, D, N_IN])
    attn_sink = din("attn_sink", [DEPTH, 8])
    delta_conv = din("delta_conv", [DEPTH, 5, 1536])
    delta_a_log = din("delta_a_log", [DEPTH, 16])
    delta_dt_bias = din("delta_dt_bias", [DEPTH, 16])
    delta_norm_w = din("delta_norm_w", [DEPTH, 64])
    hgrn_lb = din("hgrn_lb", [DEPTH, 1024])
    hgrn_norm_w = din("hgrn_norm_w", [DEPTH, 64])
    rpbx = din("rpbx", [DEPTH, 8, 64, 15 * 64])
    w_branch = din("w_branch", [DEPTH, 4, 512, D])
    w_out = din("w_out", [DEPTH, D, D])
    mlp_w1 = din("mlp_w1", [DEPTH, D, 4 * D])
    mlp_w2 = din("mlp_w2", [DEPTH, 4 * D, D])
    final_norm_w = din("final_norm_w", [D])
    consts = din("consts", [128, NCST])
    ropetab = din("ropetab", [64, 2, NS_])

    y_p = dout("y_p", [NP_, D])
    y_s = dout("y_s", [NS_, D])
    nak = dout("nak", [2, DEPTH, 256, 128])
    nav = dout("nav", [2, DEPTH, 256, 128])
    nnk = dout("nnk", [2, DEPTH, 256, 512])
    nnv = dout("nnv", [2, DEPTH, 256, 512])
    nsd = dout("nsd", [2, DEPTH, 2, 8, 64, 64])
    nsh = dout("nsh", [2, DEPTH, 2, 8, 64, 64])

    es = ExitStack()
    with es:
        nc_np = es.enter_context(nc.allow_non_contiguous_dma(reason="small param layouts"))
        K = KB(nc, es)
        holder["K"], holder["nc"] = K, nc
        chk_i = [0]

        def chk(name):
            chk_i[0] += 1
            holder.setdefault("chk", []).append((chk_i[0], name, nc.n_instructions()))
            if chk_i[0] == stage - 100:
                print("STOP at checkpoint", chk_i[0], name)
                raise StopBuild()
        banks = []
        for i in range(8):
            h = es.enter_context(nc.psum_tensor("ps%d" % i, [128, 512], F32))
            banks.append(V(h.ap(), ("ps%d" % i,)))
        bank_i = [0]

        def psum():
            v = banks[bank_i[0] % 6]
            bank_i[0] += 1
            return v

        cst = K.tile([128, NCST], F32, "cst")
        cstb = K.tile([128, NCST], BF16, "cstb")
        s_c = K.slot()
        K.dma("sp", cst, consts, s_c)
        K.cp("dve", cstb, cst)
        if stage == 1:
            K.finish()
            return nc
        identf = cst[:, C_ID:C_ID + 128]
        identb = cstb[:, C_ID:C_ID + 128]
        onesb = cstb[:, C_ONE:C_ONE + 128]
        onesf = cst[:, C_ONE:C_ONE + 128]

        xT = {"P": [K.tile([128, 8, 512], F32, "xP")], "S": [K.tile([128, 8, 512], F32, "xS%d" % i) for i in range(2)]}
        _hS = [K.tile([128, 8, 512], BF16, "hS%d" % i) for i in range(2)]
        hT = {"P": [_hS[0]], "S": _hS}
        NT = {"P": 1, "S": 2}
        TT = {"P": NP_, "S": NS_}
        SEQS = {"P": [(0, 256), (256, 256)], "S": [(0, 1024)]}

        WN = 4
        wbuf = [K.tile([128, 4096], BF16, "w") for _ in range(WN)]
        wslot = [K.slot() for _ in range(WN)]
        w_i = [0]

        def wload(src2d, kc, cols, pat="(kc p) c -> p kc c", q="pool"):
            i = w_i[0] % WN
            w_i[0] += 1
            assert kc * cols <= 4096
            dst = wbuf[i][:, 0:kc * cols].re("p (k c) -> p k c", c=cols)
            rows = src2d.ap.shape[0] // kc
            K.dma(q, dst[0:rows], src2d.re(pat, kc=kc), wslot[i])
            return dst[0:rows]

        stage_slot = [K.slot() for _ in range(4)]
        misc_slot = [K.slot() for _ in range(3)]
        io_slot = [K.slot() for _ in range(6)]

        csf = K.tile([128, 8, 2], F32, "csf")
        csT = K.tile([128, 8, 2], BF16, "csT")
        for gi_ in range(2):
            K.dma("sp", csf[:, :, gi_], cvec[gi_].re("(kc p) -> p kc", p=128), misc_slot)
        K.act(csT, csf, AF.Silu)

        if stage == 2:
            K.finish()
            return nc
        with K.scope():
            xin = Ring(K, 2, [128, D], F32, "xin")
            xs_ = [K.slot(), K.slot()] if False else [stage_slot[0], stage_slot[1]]
            n = 0
            for gname, src in (("P", x_p), ("S", x_s)):
                for tile_i in range(TT[gname] // 128):
                    xt = xin.next()
                    K.dma("sp", xt, src[tile_i * 128:(tile_i + 1) * 128, :], xs_[n % 2])
                    n += 1
                    for half in range(2):
                        ps = psum()
                        for kc4 in range(4):
                            kc = half * 4 + kc4
                            K.tr(ps[:, kc4 * 128:(kc4 + 1) * 128], xt[:, kc * 128:(kc + 1) * 128], identf)
                        tt_i, off = divmod(tile_i * 128, 512)
                        K.cp("act" if half else "dve",
                             xT[gname][tt_i][:, half * 4:half * 4 + 4, off:off + 128],
                             ps.re("p (k c) -> p k c", c=128))

        if stage == 3:
            K.finish()
            return nc
        def rms_mod(g, wm, sh):
            for tt_i in range(NT[g]):
                x = xT[g][tt_i]
                ps = psum()
                for kc in range(8):
                    sq = sqring.next()
                    K.act(sq, x[:, kc, :], AF.Square)
                    K.mm(ps, onesb, sq, start=(kc == 0), stop=(kc == 7))
                rstd = rsring.next()
                K.act(rstd, ps, AF.Sqrt, bias=epsc[:, 0:1], scale=1.0 / D)
                K.recip(rstd, rstd)
                for kc in range(8):
                    tmp = tmpring.next()
                    K.tt("dve" if kc % 2 else "pool", tmp, x[:, kc, :], rstd, ALU.mult)
                    K.act(hT[g][tt_i][:, kc, :], tmp, AF.Identity, bias=sh[:, kc:kc + 1], scale=wm[:, kc:kc + 1])

        epsc = K.tile([128, 1], F32, "eps")
        K.memset("dve", epsc, EPS)
        onec = K.tile([128, 1], F32, "onec")
        K.memset("dve", onec, 1.0)
        sqring = Ring(K, 2, [128, 512], BF16, "sq")
        rsring = Ring(K, 2, [128, 512], F32, "rstd")
        tmpring = Ring(K, 3, [128, 512], F32, "tmp")
        pring = Ring(K, 3, [128, 512], BF16, "pT")

        def proj_fm64(w, c0, rhs_h, out_ps):
            for kc in range(8):
                K.mm(out_ps, w[:, kc, c0:c0 + 64], rhs_h[:, kc, :], start=(kc == 0), stop=(kc == 7), sig=(kc == 7))

        def attention(qv, N, keytiles, outv, esink=None, b3=None):
            def r(v):
                return v if b3 is None else v.re("p (a b) -> p a b", b=b3)
            psn, psd = banks[6], banks[7]
            nkt = len(keytiles)
            for i, (kt, vv, mask, rng, nk) in enumerate(keytiles):
                c0, c1 = rng if rng is not None else (0, N)
                pss = psum()
                qs = qv if rng is None else qv[:, c0:c1]
                K.mm(r(pss[0:nk, c0:c1]), kt, qs)
                pT = pring.next()
                K.act(pT[0:nk, c0:c1], pss[0:nk, c0:c1], AF.Exp, scale=0.125)
                if mask is not None:
                    K.tt("pool", r(pT[0:nk, c0:c1]), r(pT[0:nk, c0:c1]), mask, ALU.mult)
                K.mm(psn[0:64, c0:c1], vv, pT[0:nk, c0:c1], start=(i == 0), stop=(i == nkt - 1))
                K.mm(psd[0:64, c0:c1], onesb[0:nk, 0:64], pT[0:nk, c0:c1], start=(i == 0), stop=(i == nkt - 1))
            den = tmpring.next()
            if esink is not None:
                K.tt("dve", r(den[0:64, 0:N]), r(psd[0:64, 0:N]), esink, ALU.add)
                K.recip(den[0:64, 0:N], den[0:64, 0:N])
            else:
                K.recip(den[0:64, 0:N], psd[0:64, 0:N])
            K.tt("dve", outv, r(psn[0:64, 0:N]), r(den[0:64, 0:N]), ALU.mult)

        def branch_merge(l, g, k, yT, gate1):
            with K.scope():
                mk = K.tile([128, 8, 512], BF16, "mk")
                gt = Ring(K, 2, [128, 512], F32, "gt")
                for tt_i in range(NT[g]):
                    wb = [wload(w_branch[l, k, :, half * 512:(half + 1) * 512], 8, 512) for half in range(2)]
                    wg = [wload(w_in[l, :, O_G + k * D + half * 512:O_G + k * D + (half + 1) * 512], 8, 512) for half in range(2)]
                    for oc in range(8):
                        psy, psg = psum(), psum()
                        for h in range(8):
                            K.mm(psy, wb[oc // 4][:, h, (oc % 4) * 128:(oc % 4 + 1) * 128], yT[:, h, tt_i * 512:(tt_i + 1) * 512],
                                 start=(h == 0), stop=(h == 7), sig=(h == 7))
                        for kc in range(8):
                            K.mm(psg, wg[oc // 4][:, kc, (oc % 4) * 128:(oc % 4 + 1) * 128], hT[g][tt_i][:, kc, :],
                                 start=(kc == 0), stop=(kc == 7), sig=(kc == 7))
                        gg = gt.next()
                        K.act(gg, psg, AF.Sigmoid)
                        K.tt("dve", mk[:, oc, :], psy, gg, ALU.mult)
                    wo = [wload(w_out[l, :, half * 512:(half + 1) * 512], 8, 512) for half in range(2)]
                    for oc2 in range(8):
                        ps = psum()
                        for oc in range(8):
                            K.mm(ps, wo[oc2 // 4][:, oc, (oc2 % 4) * 128:(oc2 % 4 + 1) * 128], mk[:, oc, :],
                                 start=(oc == 0), stop=(oc == 7), sig=(oc == 7))
                        xv = xTn[g][tt_i][:, oc2, :]
                        K.stt("dve", xv, ps, gate1[:, oc2:oc2 + 1], xv, ALU.mult, ALU.add)

        xTn = xT

        for l in range(nlayers):
            with K.scope():
                nw = K.tile([128, 2, 8], F32, "nw")
                for j_ in range(2):
                    K.dma("sp", nw[:, j_, :], norm_w[l, j_].re("(kc p) -> p kc", p=128), misc_slot)
                adab = K.tile([128, 48], F32, "adab")
                K.dma("sp", adab, ada_b[l].re("(c p) -> p c", p=128), misc_slot)
                esk = K.tile([64, 8], F32, "esk")
                K.dma("sp", esk, V(attn_sink.ap[l].partition_broadcast(64)), misc_slot)
                K.act(esk, esk, AF.Exp)
                convw = K.tile([64, 5, 24], F32, "convw")
                for j_ in range(5):
                    K.dma("sp", convw[:, j_, :], delta_conv[l, j_].re("(n d) -> d n", d=64), misc_slot)
                nea = K.tile([64, 16], F32, "nea")
                K.dma("sp", nea, V(delta_a_log.ap[l].partition_broadcast(64)), misc_slot)
                K.act(nea, nea, AF.Exp)
                K.ts("dve", nea, nea, -1.0)
                dtb = K.tile([64, 16], F32, "dtb")
                K.dma("sp", dtb, V(delta_dt_bias.ap[l].partition_broadcast(64)), misc_slot)
                dnw = K.tile([64, 1], F32, "dnw")
                K.dma("sp", dnw, delta_norm_w[l].re("(d o) -> d o", o=1), misc_slot)
                hnw = K.tile([64, 1], F32, "hnw")
                K.dma("sp", hnw, hgrn_norm_w[l].re("(d o) -> d o", o=1), misc_slot)
                def compute_lb(dst, rowbc):
                    shp = [64, 4, 1024] if rowbc else [64, 4, 16]
                    with K.scope():
                        raw = K.tile(shp, F32, "lbraw")
                        for ll in range(4):
                            if not rowbc:
                                K.dma("sp", raw[:, ll, :], hgrn_lb[ll].re("(j d) -> d j", d=64), misc_slot)
                            else:
                                K.dma("sp", raw[:, ll, :], V(hgrn_lb.ap[ll].partition_broadcast(64)), misc_slot)
                        K.act(raw, raw, AF.Exp)
                        tot = K.tile(shp[0:1] + shp[2:], F32, "lbtot")
                        K.tt("dve", tot, raw[:, 0, :], raw[:, 1, :], ALU.add)
                        K.tt("dve", tot, tot, raw[:, 2, :], ALU.add)
                        K.tt("dve", tot, tot, raw[:, 3, :], ALU.add)
                        K.recip(tot, tot)
                        if l == 0:
                            K.memset("dve", dst, 0.0)
                        else:
                            acc = K.tile(shp[0:1] + shp[2:], F32, "lbacc")
                            K.cp("dve", acc, raw[:, 1, :])
                            for ll in range(2, l + 1):
                                K.tt("dve", acc, acc, raw[:, ll, :], ALU.add)
                            K.tt("dve", dst, acc, tot, ALU.mult)

                lbT = K.tile([64, 16], F32, "lbT")
                compute_lb(lbT, False)
                omlbT = K.tile([64, 16], F32, "omlbT")
                K.ts("dve", omlbT, lbT, -1.0, 1.0, ALU.mult, ALU.add)
                chk("params")
                mod = K.tile([128, 48, 2], F32, "mod")
                for cb in range(12):
                    w = wload(ada_w[l, :, cb * 512:(cb + 1) * 512], 8, 512)
                    ps = psum()
                    for cc in range(4):
                        for kc in range(8):
                            K.mm(ps[:, cc * 2:cc * 2 + 2], w[:, kc, cc * 128:(cc + 1) * 128], csT[:, kc, :],
                                 start=(kc == 0), stop=(kc == 7), sig=(kc == 7 and cc == 3))
                    K.tt("dve", mod[:, cb * 4:cb * 4 + 4, :], ps[:, 0:8].re("p (c g) -> p c g", g=2),
                         adab[:, cb * 4:cb * 4 + 4].un(2).bc([128, 4, 2]), ALU.add)
                wm1, wm2, sh1, sh2, g1, g2 = {}, {}, {}, {}, {}, {}
                for gi, g in enumerate(("P", "S")):
                    for (dst, sc_i, nwi) in ((wm1, 1, 0), (wm2, 4, 1)):
                        t = K.tile([128, 8], F32, "wm")
                        K.stt("dve", t, mod[:, sc_i * 8:sc_i * 8 + 8, gi], 1.0, nw[:, nwi, :], ALU.add, ALU.mult)
                        dst[g] = t
                    sh1[g] = mod[:, 0:8, gi]
                    g1[g] = mod[:, 16:24, gi]
                    sh2[g] = mod[:, 24:32, gi]
                    g2[g] = mod[:, 40:48, gi]

                chk("adaln")
                for g in ("P", "S"):
                    T = TT[g]
                    sample = (g == "S")
                    rms_mod(g, wm1[g], sh1[g])
                    chk("rms" + g)
                    with K.scope():
                        qT = K.tile([64, 8, T], BF16, "aq")
                        kT = K.tile([64, 2, T], BF16, "ak")
                        vtm = K.tile([128, T // 128, 128], BF16, "av")
                        yT = K.tile([64, 8, T], BF16, "ya")
                        wq = wload(w_in[l, :, O_AQ:O_AQ + 512], 8, 512)
                        wkv = wload(w_in[l, :, O_AK:O_AK + 256], 8, 256)
                        stg = Ring(K, 2, [128, 256], F32, "stg")
                        if sample:
                            rtab = K.tile([64, 2, NS_], F32, "rtab")
                            K.dma("sp", rtab, ropetab, misc_slot)
                            xbr = Ring(K, 2, [64, 512], BF16, "xb")
                        for tt_i in range(NT[g]):
                            tsl = slice(tt_i * 512, (tt_i + 1) * 512)
                            for hh in range(10):
                                ps = psum()
                                if hh < 8:
                                    proj_fm64(wq, hh * 64, hT[g][tt_i], ps[0:64, :])
                                    dst = qT[:, hh, tsl]
                                else:
                                    proj_fm64(wkv, (hh - 8) * 64, hT[g][tt_i], ps[0:64, :])
                                    dst = kT[:, hh - 8, tsl]
                                if not sample:
                                    K.cp("act", dst, ps[0:64, :])
                                else:
                                    xb = xbr.next()
                                    K.cp("act", xb, ps[0:64, :])
                                    ps2 = psum()
                                    K.mm(ps2[0:64, :], cstb[0:64, C_ROPE:C_ROPE + 64], xb)
                                    t1, t2 = tmpring.next(), tmpring.next()
                                    K.tt("dve", t1[0:64, :], ps2[0:64, :], rtab[:, 1, tsl], ALU.mult)
                                    K.tt("pool", t2[0:64, :], xb, rtab[:, 0, tsl], ALU.mult)
                                    K.tt("dve", dst, t1[0:64, :], t2[0:64, :], ALU.add)
                            chk("Afm" + g)
                            for j in range(4):
                                ps = psum()
                                for kc in range(8):
                                    K.mm(ps[:, 0:256], hT[g][tt_i][:, kc, j * 128:(j + 1) * 128], wkv[:, kc, :],
                                         start=(kc == 0), stop=(kc == 7), sig=(kc == 7))
                                tile_i = tt_i * 4 + j
                                K.cp("dve", vtm[:, tile_i, :], ps[:, 128:256])
                                chk("Atm_mm" + g)
                                if not sample:
                                    st = stg.next()
                                    K.cp("dve", st, ps[:, 0:256])
                                    chk("Atm_cp" + g)
                                    s_, r0 = divmod(tile_i * 128, 256)
                                    K.dma("sp", nak[s_, l, r0:r0 + 128, :], st[:, 0:128], io_slot)
                                    K.dma("sp", nav[s_, l, r0:r0 + 128, :], st[:, 128:256], io_slot)
                        chk("Aproj" + g)
                        if sample:
                            ctm = K.tile([128, 4, 128], BF16, "ctk")
                            cvv = K.tile([128, 4, 128], BF16, "ctv")
                            ckT = K.tile([64, 2, PAST], BF16, "ckT")
                            K.dma("pool", ctm, cak[l].re("(j p) c -> p j c", p=128), stage_slot[2])
                            K.dma("pool", cvv, cav[l].re("(j p) c -> p j c", p=128), stage_slot[3])
                            for kv in range(2):
                                ps = psum()
                                psb = ps
                                for j in range(4):
                                    K.tr(psb[0:64, j * 128:(j + 1) * 128], ctm[:, j, kv * 64:(kv + 1) * 64], identb)
                                K.cp("dve", ckT[:, kv, :], psb[0:64, 0:512])
                        for (t0, L) in SEQS[g]:
                            nblk = L // 128
                            for kv in range(2):
                                for bi in range(nblk):
                                    kts = []
                                    if sample:
                                        for j in range(4):
                                            kts.append((ckT[:, kv, j * 128:(j + 1) * 128], cvv[:, j, kv * 64:(kv + 1) * 64], None, None, 128))
                                        blks = [(bi - 1, C_MPREV), (bi, None), (bi + 1, C_MNEXT)]
                                    else:
                                        blks = [(b, None) for b in range(nblk)]
                                    for (b, mc) in blks:
                                        if b < 0 or b >= nblk:
                                            continue
                                        k0 = t0 + b * 128
                                        m = None if mc is None else cstb[:, mc:mc + 128].un(1).bc([128, 4, 128])
                                        kts.append((kT[:, kv, k0:k0 + 128], vtm[:, k0 // 128, kv * 64:(kv + 1) * 64], m, None, 128))
                                    q0 = t0 + bi * 128
                                    attention(qT[:, 4 * kv:4 * kv + 4, q0:q0 + 128], 512, kts,
                                              yT[:, 4 * kv:4 * kv + 4, q0:q0 + 128],
                                              esk[:, 4 * kv:4 * kv + 4].un(2).bc([64, 4, 128]), b3=128)
                        chk("Aattn" + g)
                        branch_merge(l, g, 0, yT, g1[g])
                        chk("Amerge" + g)
                    with K.scope():
                        yT = K.tile([64, 8, T], BF16, "yd")
                        wdq = wload(w_in[l, :, O_DQ:O_DQ + 512], 8, 512)
                        wdk = wload(w_in[l, :, O_DK:O_DK + 512], 8, 512)
                        wdv = wload(w_in[l, :, O_DV:O_DV + 512], 8, 512)
                        chk("Dw" + g)
                        if sample:
                            vrow = K.tile([64, 16, 512], BF16, "vrow")
                            for r_ in range(16):
                                ps = psum()
                                for kc in range(8):
                                    K.mm(ps[0:64, :], hT[g][r_ // 8][:, kc, (r_ % 8) * 64:(r_ % 8 + 1) * 64], wdv[:, kc, :],
                                         start=(kc == 0), stop=(kc == 7), sig=(kc == 7))
                                K.cp("act", vrow[:, r_, :], ps[0:64, :])
                            cktm = K.tile([128, 4, 512], BF16, "cktm")
                            cvtm = K.tile([128, 4, 512], BF16, "cvtm")
                            K.dma("pool", cktm, cnk[l].re("(j p) c -> p j c", p=128), stage_slot[2])
                            K.dma("pool", cvtm, cnv[l].re("(j p) c -> p j c", p=128), stage_slot[3])
                            ckr = Ring(K, 2, [64, 512], BF16, "ckr")
                            Er = Ring(K, 2, [64, 960], BF16, "Er")
                            Ef = Ring(K, 2, [64, 960], F32, "Ef")
                            colm = cst[0:64, C_COLM:C_COLM + 64]
                        else:
                            stg = Ring(K, 2, [128, 512], F32, "stgd")
                            vtm = K.tile([128, T // 128, 512], BF16, "dvtm")
                            for tile_i in range(T // 128):
                                tt_i, j = divmod(tile_i, 4)
                                for which, w in ((0, wdk), (1, wdv)):
                                    ps = psum()
                                    for kc in range(8):
                                        K.mm(ps, hT[g][tt_i][:, kc, j * 128:(j + 1) * 128], w[:, kc, :],
                                             start=(kc == 0), stop=(kc == 7), sig=(kc == 7))
                                    chk("Dmm" + g)
                                    st = stg.next()
                                    K.cp("act", st, ps)
                                    chk("Dcp" + g)
                                    s_, r0 = divmod(tile_i * 128, 256)
                                    K.dma("sp", (nnk if which == 0 else nnv)[s_, l, r0:r0 + 128, :], st, io_slot)
                                    chk("Ddma" + g)
                                    if which == 1:
                                        K.cp("dve", vtm[:, tile_i, :], ps)
                        chk("Dtm" + g)
                        qh = Ring(K, 2, [64, T], BF16, "dq")
                        kh = Ring(K, 2, [64, T], BF16, "dk")
                        for h in range(8):
                            q_h, k_h = qh.next(), kh.next()
                            for tt_i in range(NT[g]):
                                for (w, dst) in ((wdq, q_h), (wdk, k_h)):
                                    ps = psum()
                                    proj_fm64(w, h * 64, hT[g][tt_i], ps[0:64, :])
                                    K.cp("act", dst[:, tt_i * 512:(tt_i + 1) * 512], ps[0:64, :])
                            chk("Dproj%d" % h + g)
                            if sample:
                                ck = ckr.next()
                                ps = psum()
                                psb = ps
                                for j in range(4):
                                    K.tr(psb[0:64, j * 128:(j + 1) * 128], cktm[:, j, h * 64:(h + 1) * 64], identb)
                                K.cp("dve", ck, psb[0:64, 0:512])
                                ef = Ef.next()
                                K.dma("sp", ef, rpbx[l, h], stage_slot[h % 2])
                                K.act(ef, ef, AF.Exp)
                                E = Er.next()
                                K.tt("dve", E.re("p (e c) -> p e c", c=64), ef.re("p (e c) -> p e c", c=64),
                                     colm.un(1).bc([64, 15, 64]), ALU.mult)
                                for hf_ in range(2):
                                    kts = [(ck[:, j * 128:(j + 1) * 128], cvtm[:, j, h * 64:(h + 1) * 64], None, None, 128) for j in range(4)]
                                    for kr in range(16):
                                        qs = [r_ for r_ in range(8 * hf_, 8 * hf_ + 8)
                                              if min(max(r_ - 4, 0), 8) <= kr < min(max(r_ - 4, 0), 8) + 8]
                                        if not qs:
                                            continue
                                        c0, c1 = (qs[0] - 8 * hf_) * 64, (qs[-1] + 1 - 8 * hf_) * 64
                                        e0 = qs[0] - kr + 7
                                        kts.append((k_h[:, kr * 64:(kr + 1) * 64], vrow[:, kr, h * 64:(h + 1) * 64],
                                                    E[:, e0 * 64:(e0 + len(qs)) * 64], (c0, c1), 64))
                                    attention(q_h[:, hf_ * 512:(hf_ + 1) * 512], 512, kts, yT[:, h, hf_ * 512:(hf_ + 1) * 512])
                            else:
                                for (t0, L) in SEQS[g]:
                                    kts = [(k_h[:, t0 + j * 128:t0 + (j + 1) * 128], vtm[:, (t0 + j * 128) // 128, h * 64:(h + 1) * 64],
                                            None, None, 128) for j in range(L // 128)]
                                    attention(q_h[:, t0:t0 + L], L, kts, yT[:, h, t0:t0 + L])
                                    chk("Dattn%d" % h + g)
                        branch_merge(l, g, 3, yT, g1[g])
                        chk("D" + g)

                    def r3(v):
                        return v.re("p (h c) -> p h c", c=64)

                    nch = T // 64
                    with K.scope():
                        yT = K.tile([64, 8, T], BF16, "yb")
                        gb = K.tile([64, nch, 32], F32, "gb")
                        wab = wload(w_in[l, :, O_BA:O_BA + 32], 8, 32)
                        for c in range(nch):
                            tt_i, off = divmod(c * 64, 512)
                            ps = psum()
                            for kc in range(8):
                                K.mm(ps[0:64, 0:32], hT[g][tt_i][:, kc, off:off + 64], wab[:, kc, :],
                                     start=(kc == 0), stop=(kc == 7), sig=(kc == 7))
                            K.cp("act", gb[:, c, :], ps[0:64, 0:32])
                        gv, bv = gb[:, :, 0:16], gb[:, :, 16:32]
                        K.tt("dve", gv, gv, dtb.un(1).bc([64, nch, 16]), ALU.add)
                        K.act(gv, gv, AF.Exp)
                        K.act(gv, gv, AF.Ln, bias=onec[0:64, 0:1])
                        K.tt("dve", gv, gv, nea.un(1).bc([64, nch, 16]), ALU.mult)
                        K.act(bv, bv, AF.Sigmoid)
                        for hg in range(2):
                            with K.scope():
                                qkv = K.tile([64, 12, T], BF16, "qkv")
                                oT = K.tile([64, 4, T], F32, "oTb")
                                w3 = [wload(w_in[l, :, o_ + hg * 256:o_ + hg * 256 + 256], 8, 256) for o_ in (O_BQ, O_BK, O_BV)]
                                nsq = len(SEQS[g])
                                Ls = SEQS[g][0][1]
                                with K.scope():
                                    rawr = Ring(K, 2, [64, nsq, Ls + 4], F32, "raw")
                                    for rt in rawr.t:
                                        K.memset("pool", rt, 0.0)
                                    cvr = Ring(K, 2, [64, T], F32, "cv")
                                    f64 = Ring(K, 4, [64, 512], F32, "f64")
                                    for which in range(3):
                                        for hi in range(4):
                                            h = hg * 4 + hi
                                            raw = rawr.next()
                                            for tt_i in range(NT[g]):
                                                ps = psum()
                                                proj_fm64(w3[which], hi * 64, hT[g][tt_i], ps[0:64, :])
                                                if sample:
                                                    K.cp("act", raw[:, 0, 2 + tt_i * 512:2 + (tt_i + 1) * 512], ps[0:64, :])
                                                else:
                                                    for s_ in range(2):
                                                        K.cp("act", raw[:, s_, 2:2 + 256], ps[0:64, s_ * 256:(s_ + 1) * 256])
                                            cv = cvr.next()
                                            ch = which * 8 + h
                                            for s_, (t0, L) in enumerate(SEQS[g]):
                                                K.ts("dve", cv[:, t0:t0 + L], raw[:, s_, 0:L], convw[:, 0, ch:ch + 1])
                                                for j in range(1, 5):
                                                    K.stt("dve", cv[:, t0:t0 + L], raw[:, s_, j:j + L],
                                                          convw[:, j, ch:ch + 1], cv[:, t0:t0 + L], ALU.mult, ALU.add)
                                            K.act(cv, cv, AF.Silu)
                                            if which < 2:
                                                for tt_i in range(NT[g]):
                                                    tsl = slice(tt_i * 512, (tt_i + 1) * 512)
                                                    sq = f64.next()
                                                    K.act(sq, cv[:, tsl], AF.Square)
                                                    ps = psum()
                                                    K.mm(ps[0:64, :], onesf[0:64, 0:64], sq)
                                                    rn = f64.next()
                                                    K.act(rn, ps[0:64, :], AF.Sqrt, bias=epsc[0:64, 0:1], scale=1.0)
                                                    K.recip(rn, rn)
                                                    K.stt("dve", qkv[:, which * 4 + hi, tsl], cv[:, tsl],
                                                          (0.125 if which == 0 else 1.0), rn, ALU.mult, ALU.mult)
                                            else:
                                                K.cp("pool", qkv[:, 8 + hi, :], cv)
                                with K.scope():
                                    Lr = Ring(K, 20, [64, 256], F32, "bL")
                                    Nr = Ring(K, 6, [64, 256], F32, "bN")
                                    Sst = [K.tile([64, 256], F32, "Sb%d" % d_) for d_ in range(2)]
                                    idf = cst[0:64, C_ID:C_ID + 64]
                                    idb = cstb[0:64, C_ID:C_ID + 64]
                                    one64 = onesf[0:64, 0:64]
                                    HS = [slice(hi * 64, (hi + 1) * 64) for hi in range(4)]
                                    for si, (t0, L) in enumerate(SEQS[g]):
                                        ncs = L // 64
                                        for d_ in range(2):
                                            if sample:
                                                K.dma("sp", r3(Sst[d_]), sd_in[l, d_, hg * 4:hg * 4 + 4].re("h k v -> k h v"), stage_slot[d_])
                                            else:
                                                K.memset("pool", Sst[d_], 0.0)
                                        visited = set()
                                        for i in range(ncs):
                                            for d_ in range(2):
                                                c = i if d_ == 0 else ncs - 1 - i
                                                tok0 = t0 + c * 64
                                                cg = tok0 // 64
                                                csl = slice(tok0, tok0 + 64)
                                                U_ = cst[0:64, C_U[d_]:C_U[d_] + 64]
                                                nm_ = cst[0:64, C_NM[d_]:C_NM[d_] + 64]
                                                nmT_ = cst[0:64, C_NMT[d_]:C_NMT[d_] + 64]
                                                SL_ = cst[0:64, C_SL[d_]:C_SL[d_] + 64]
                                                gcol = gb[:, cg, d_ * 8 + hg * 4:d_ * 8 + hg * 4 + 4]
                                                bcol = gb[:, cg, 16 + d_ * 8 + hg * 4:16 + d_ * 8 + hg * 4 + 4]
                                                S = Sst[d_]
                                                ps = psum()
                                                psb = ps
                                                for hi in range(4):
                                                    K.tr(psb[0:64, hi * 64:(hi + 1) * 64], qkv[:, 4 + hi, csl], idb, sig=False)
                                                    K.tr(psb[0:64, 256 + hi * 64:256 + (hi + 1) * 64], qkv[:, 8 + hi, csl], idb, sig=(hi == 3))
                                                Ktm, Vtm = Lr.next(), Lr.next()
                                                K.cp("act", Ktm, psb[0:64, 0:256])
                                                K.cp("dve", Vtm, psb[0:64, 256:512])
                                                gU = Lr.next()
                                                K.tt("pool", r3(gU), U_.un(1).bc([64, 4, 64]), gcol.un(2).bc([64, 4, 64]), ALU.mult)
                                                ps1 = psum()
                                                K.mm(ps1[0:64, 0:256], one64, gU, sig=False)
                                                K.mm(ps1[0:64, 256:260], U_, gcol, sig=False)
                                                K.mm(ps1[0:64, 260:264], one64, gcol)
                                                sm = Lr.next()
                                                K.cp("dve", sm[:, 0:8], ps1[0:64, 256:264])
                                                gc, gl = sm[:, 0:4], sm[:, 4:8]
                                                K.act(sm[:, 8:12], gc, AF.Exp)
                                                K.tt("dve", sm[:, 12:16], gl, gc, ALU.subtract)
                                                K.act(sm[:, 12:16], sm[:, 12:16], AF.Exp)
                                                K.act(sm[:, 16:20], gl, AF.Exp)
                                                K.tt("dve", sm[:, 20:24], sm[:, 8:12], bcol, ALU.mult)
                                                be, ekd, cd = sm[:, 20:24], sm[:, 12:16], sm[:, 16:20]
                                                gcrow = Lr.next()
                                                K.cp("act", gcrow, ps1[0:64, 0:256])
                                                G = Lr.next()
                                                K.tt("dve", r3(G), nm_.un(1).bc([64, 4, 64]), r3(gcrow), ALU.subtract)
                                                K.tt("dve", r3(G), r3(G), gc.un(2).bc([64, 4, 64]), ALU.add)
                                                K.act(G, G, AF.Exp)
                                                GT = Lr.next()
                                                K.tt("pool", r3(GT), r3(gcrow), nmT_.un(1).bc([64, 4, 64]), ALU.add)
                                                K.tt("pool", r3(GT), r3(GT), gc.un(2).bc([64, 4, 64]), ALU.subtract)
                                                K.act(GT, GT, AF.Exp)
                                                Erow = Lr.next()
                                                K.act(Erow, gcrow, AF.Exp)
                                                psk = psum()
                                                for hi in range(4):
                                                    K.mm(psk[0:64, HS[hi]], qkv[:, 4 + hi, csl], qkv[:, 4 + hi, csl], sig=(hi == 3))
                                                bsl = Lr.next()
                                                K.tt("pool", r3(bsl), SL_.un(1).bc([64, 4, 64]), bcol.un(2).bc([64, 4, 64]), ALU.mult)
                                                A = Lr.next()
                                                K.tt("dve", A, psk[0:64, 0:256], G, ALU.mult)
                                                K.tt("dve", A, A, bsl, ALU.mult)
                                                pst = psum()
                                                for hi in range(4):
                                                    K.tr(pst[0:64, HS[hi]], A[:, HS[hi]], idf, sig=(hi == 3))
                                                AT = Lr.next()
                                                K.cp("act", AT, pst[0:64, 0:256])
                                                X = Nr.next()
                                                K.tt("dve", r3(X), idf.un(1).bc([64, 4, 64]), r3(pst[0:64, 0:256]), ALU.subtract)
                                                P_, PT = A, AT
                                                for kk in range(1, 6):
                                                    psp = psum()
                                                    for hi in range(4):
                                                        K.mm(psp[0:64, HS[hi]], PT[:, HS[hi]], P_[:, HS[hi]], sig=(hi == 3))
                                                    Pn = Nr.next()
                                                    K.cp("act", Pn, psp[0:64, 0:256])
                                                    PTn = None
                                                    if kk < 5:
                                                        pspt = psum()
                                                        for hi in range(4):
                                                            K.mm(pspt[0:64, HS[hi]], P_[:, HS[hi]], PT[:, HS[hi]], sig=(hi == 3))
                                                        PTn = Nr.next()
                                                        K.cp("pool", PTn, pspt[0:64, 0:256]) if False else K.cp("dve", PTn, pspt[0:64, 0:256])
                                                    psx = psum()
                                                    for hi in range(4):
                                                        K.mm(psx[0:64, HS[hi]], Pn[:, HS[hi]], X[:, HS[hi]], sig=(hi == 3))
                                                    Xn = Nr.next() if kk < 5 else Lr.next()
                                                    K.tt("dve", Xn, psx[0:64, 0:256], X, ALU.add)
                                                    P_, PT, X = Pn, PTn, Xn
                                                TT_ = X
                                                Rk, Rv, Kd = Lr.next(), Lr.next(), Lr.next()
                                                K.tt("pool", r3(Rk), r3(Ktm), be.un(2).bc([64, 4, 64]), ALU.mult)
                                                K.tt("pool", r3(Rv), r3(Vtm), bcol.un(2).bc([64, 4, 64]), ALU.mult)
                                                K.tt("pool", r3(Kd), r3(Ktm), ekd.un(2).bc([64, 4, 64]), ALU.mult)
                                                psw = psum()
                                                for hi in range(4):
                                                    K.mm(psw[0:64, HS[hi]], Rk[:, HS[hi]], TT_[:, HS[hi]], sig=(hi == 3))
                                                nWkT = Lr.next()
                                                K.ts("dve", nWkT, psw[0:64, 0:256], -1.0)
                                                psq = psum()
                                                for hi in range(4):
                                                    K.mm(psq[0:64, HS[hi]], qkv[:, 4 + hi, csl], qkv[:, hi, csl], sig=(hi == 3))
                                                PqkT = Lr.next()
                                                K.tt("dve", PqkT, psq[0:64, 0:256], GT, ALU.mult)
                                                QdT = Lr.next()
                                                K.tt("pool", r3(QdT), qkv[:, 0:4, csl], r3(Erow), ALU.mult)
                                                psu = psum()
                                                for hi in range(4):
                                                    K.mm(psu[0:64, HS[hi]], TT_[:, HS[hi]], Rv[:, HS[hi]], start=True, stop=False, sig=False)
                                                    K.mm(psu[0:64, HS[hi]], nWkT[:, HS[hi]], S[:, HS[hi]], start=False, stop=True, sig=(hi == 3))
                                                Uc = Lr.next()
                                                K.cp("act", Uc, psu[0:64, 0:256])
                                                pso = psum()
                                                for hi in range(4):
                                                    K.mm(pso[0:64, HS[hi]], S[:, HS[hi]], QdT[:, HS[hi]], start=True, stop=False, sig=False)
                                                    K.mm(pso[0:64, HS[hi]], Uc[:, HS[hi]], PqkT[:, HS[hi]], start=False, stop=True, sig=(hi == 3))
                                                ov = oT[:, :, csl]
                                                if c not in visited:
                                                    K.cp("act", ov, r3(pso[0:64, 0:256]))
                                                    visited.add(c)
                                                else:
                                                    K.tt("dve", ov, ov, r3(pso[0:64, 0:256]), ALU.add)
                                                psm = psum()
                                                for hi in range(4):
                                                    K.mm(psm[0:64, HS[hi]], Kd[:, HS[hi]], Uc[:, HS[hi]], sig=(hi == 3))
                                                K.tt("dve", r3(S), r3(S), cd.un(2).bc([64, 4, 64]), ALU.mult)
                                                K.tt("dve", S, S, psm[0:64, 0:256], ALU.add)
                                        if not sample:
                                            for d_ in range(2):
                                                K.dma("sp", nsd[si, l, d_, hg * 4:hg * 4 + 4].re("h k v -> k h v"), r3(Sst[d_]), io_slot)
                                with K.scope():
                                    f64 = Ring(K, 4, [64, 512], F32, "f64z")
                                    wz = wload(w_in[l, :, O_BZ + hg * 256:O_BZ + hg * 256 + 256], 8, 256)
                                    for hi in range(4):
                                        for tt_i in range(NT[g]):
                                            tsl = slice(tt_i * 512, (tt_i + 1) * 512)
                                            sq = f64.next()
                                            K.act(sq, oT[:, hi, tsl], AF.Square)
                                            ps = psum()
                                            K.mm(ps[0:64, :], one64, sq)
                                            rn = f64.next()
                                            K.act(rn, ps[0:64, :], AF.Sqrt, bias=epsc[0:64, 0:1], scale=1.0 / 64)
                                            K.recip(rn, rn)
                                            psz = psum()
                                            proj_fm64(wz, hi * 64, hT[g][tt_i], psz[0:64, :])
                                            sz = f64.next()
                                            K.act(sz, psz[0:64, :], AF.Silu)
                                            K.tt("dve", rn, oT[:, hi, tsl], rn, ALU.mult)
                                            K.stt("dve", yT[:, hg * 4 + hi, tsl], rn, dnw[:, 0:1], sz, ALU.mult, ALU.mult)
                        branch_merge(l, g, 1, yT, g1[g])
                        chk("B" + g)

                    with K.scope():
                        yT = K.tile([64, 8, T], BF16, "yc")
                        lbR = K.tile([64, 1024], F32, "lbR")
                        omlbR = K.tile([64, 1024], F32, "omlbR")
                        compute_lb(lbR, True)
                        K.ts("dve", omlbR, lbR, -1.0, 1.0, ALU.mult, ALU.add)
                        for hg in range(2):
                            with K.scope():
                                qT = K.tile([64, 4, T], BF16, "cq")
                                kT = K.tile([64, 2, 4, T], BF16, "ckk")
                                oT = K.tile([64, 4, T], F32, "oTc")
                                wcq = wload(w_in[l, :, O_CQ + hg * 256:O_CQ + hg * 256 + 256], 8, 256)
                                wcf = [wload(w_in[l, :, O_CF + d_ * 512 + hg * 256:O_CF + d_ * 512 + hg * 256 + 256], 8, 256) for d_ in range(2)]
                                wci = wload(w_in[l, :, O_CI + hg * 256:O_CI + hg * 256 + 256], 8, 256)
                                with K.scope():
                                    f64 = Ring(K, 4, [64, 512], F32, "f64c")
                                    for hi in range(4):
                                        for tt_i in range(NT[g]):
                                            tsl = slice(tt_i * 512, (tt_i + 1) * 512)
                                            ps = psum()
                                            proj_fm64(wcq, hi * 64, hT[g][tt_i], ps[0:64, :])
                                            K.act(qT[:, hi, tsl], ps[0:64, :], AF.Silu)
                                            for d_ in range(2):
                                                ps = psum()
                                                proj_fm64(wcf[d_], hi * 64, hT[g][tt_i], ps[0:64, :])
                                                sg = f64.next()
                                                K.act(sg, ps[0:64, :], AF.Sigmoid, scale=-1.0)
                                                col = d_ * 8 + hg * 4 + hi
                                                K.ts("dve", kT[:, d_, hi, tsl], sg, omlbT[:, col:col + 1])
                                with K.scope():
                                    Lr = Ring(K, 18, [64, 256], F32, "cL")
                                    Sst = [K.tile([64, 256], F32, "Sc%d" % d_) for d_ in range(2)]
                                    HS = [slice(hi * 64, (hi + 1) * 64) for hi in range(4)]
                                    for si, (t0, L) in enumerate(SEQS[g]):
                                        ncs = L // 64
                                        for d_ in range(2):
                                            if sample:
                                                K.dma("sp", r3(Sst[d_]), sh_in[l, d_, hg * 4:hg * 4 + 4].re("h k v -> k h v"), stage_slot[d_])
                                            else:
                                                K.memset("pool", Sst[d_], 0.0)
                                        visited = set()
                                        for i in range(ncs):
                                            for d_ in range(2):
                                                c = i if d_ == 0 else ncs - 1 - i
                                                tok0 = t0 + c * 64
                                                csl = slice(tok0, tok0 + 64)
                                                tt_i, off = divmod(tok0, 512)
                                                U_ = cst[0:64, C_U[d_]:C_U[d_] + 64]
                                                W2_ = cst[0:64, C_W2[d_]:C_W2[d_] + 64]
                                                MT_ = cst[0:64, C_MT[d_]:C_MT[d_] + 64]
                                                mid = MID[d_]
                                                last = 63 if d_ == 0 else 0
                                                S = Sst[d_]
                                                psf, psv = psum(), psum()
                                                for kc in range(8):
                                                    K.mm(psf[0:64, 0:256], hT[g][tt_i][:, kc, off:off + 64], wcf[d_][:, kc, :],
                                                         start=(kc == 0), stop=(kc == 7), sig=(kc == 7))
                                                for kc in range(8):
                                                    K.mm(psv[0:64, 0:256], hT[g][tt_i][:, kc, off:off + 64], wci[:, kc, :],
                                                         start=(kc == 0), stop=(kc == 7), sig=(kc == 7))
                                                Vt = Lr.next()
                                                K.cp("act", Vt, psv[0:64, 0:256])
                                                f = Lr.next()
                                                K.act(f, psf[0:64, 0:256], AF.Sigmoid)
                                                cs_ = slice(d_ * 512 + hg * 256, d_ * 512 + hg * 256 + 256)
                                                K.tt("dve", f, f, omlbR[:, cs_], ALU.mult)
                                                K.tt("dve", f, f, lbR[:, cs_], ALU.add)
                                                lf = Lr.next()
                                                K.act(lf, f, AF.Ln)
                                                ktm = Lr.next()
                                                K.ts("pool", ktm, f, -1.0, 1.0, ALU.mult, ALU.add)
                                                psb_ = psum()
                                                for hi in range(4):
                                                    K.mm(psb_[0:64, HS[hi]], lf[:, HS[hi]], U_, sig=(hi == 3))
                                                psw2 = psum()
                                                K.mm(psw2[0:64, 0:256], W2_, lf)
                                                bT = Lr.next()
                                                K.cp("dve", bT, psb_[0:64, 0:256])
                                                Eb = Lr.next()
                                                K.act(Eb, bT, AF.Exp)
                                                bp = Lr.next()
                                                K.tt("dve", r3(bp), r3(bT), r3(bT)[:, :, mid:mid + 1].bc([64, 4, 64]), ALU.subtract)
                                                Ebp, Ebn = Lr.next(), Lr.next()
                                                K.act(Ebp, bp, AF.Exp)
                                                K.act(Ebn, bp, AF.Exp, scale=-1.0)
                                                QiT, KiT, QdT = Lr.next(), Lr.next(), Lr.next()
                                                K.tt("pool", r3(QiT), qT[:, :, csl], r3(Ebp), ALU.mult)
                                                K.tt("pool", r3(KiT), kT[:, d_, :, csl], r3(Ebn), ALU.mult)
                                                K.tt("dve", r3(QdT), qT[:, :, csl], r3(Eb), ALU.mult)
                                                Ekd = Lr.next()
                                                K.act(Ekd, psw2[0:64, 0:256], AF.Exp)
                                                Kd = Lr.next()
                                                K.tt("dve", Kd, ktm, Ekd, ALU.mult)
                                                psa = psum()
                                                if d_ == 0:
                                                    f_t, p_t, p_j, z_j = (32, 64), (0, 32), (0, 32), (32, 64)
                                                else:
                                                    f_t, p_t, p_j, z_j = (0, 32), (32, 64), (32, 64), (0, 32)
                                                zer = cst[0:64, C_ZERO:C_ZERO + 32]
                                                for hi in range(4):
                                                    b0 = hi * 64
                                                    K.mm(psa[0:64, b0 + f_t[0]:b0 + f_t[1]], KiT[:, HS[hi]], QiT[:, b0 + f_t[0]:b0 + f_t[1]], sig=False)
                                                    K.mm(psa[p_j[0]:p_j[1], b0 + p_t[0]:b0 + p_t[1]], KiT[:, b0 + p_j[0]:b0 + p_j[1]],
                                                         QiT[:, b0 + p_t[0]:b0 + p_t[1]], sig=False)
                                                    K.mm(psa[z_j[0]:z_j[1], b0 + p_t[0]:b0 + p_t[1]], zer, QiT[:, b0 + p_t[0]:b0 + p_t[1]], sig=(hi == 3))
                                                att = Lr.next()
                                                K.tt("dve", r3(att), r3(psa[0:64, 0:256]), MT_.un(1).bc([64, 4, 64]), ALU.mult)
                                                pso = psum()
                                                for hi in range(4):
                                                    K.mm(pso[0:64, HS[hi]], Vt[:, HS[hi]], att[:, HS[hi]], start=True, stop=False, sig=False)
                                                    K.mm(pso[0:64, HS[hi]], S[:, HS[hi]], QdT[:, HS[hi]], start=False, stop=True, sig=(hi == 3))
                                                ov = oT[:, :, csl]
                                                if c not in visited:
                                                    K.cp("act", ov, r3(pso[0:64, 0:256]))
                                                    visited.add(c)
                                                else:
                                                    K.tt("dve", ov, ov, r3(pso[0:64, 0:256]), ALU.add)
                                                psm = psum()
                                                for hi in range(4):
                                                    K.mm(psm[0:64, HS[hi]], Kd[:, HS[hi]], Vt[:, HS[hi]], sig=(hi == 3))
                                                K.tt("dve", r3(S), r3(S), r3(Eb)[:, :, last:last + 1].bc([64, 4, 64]), ALU.mult)
                                                K.tt("dve", S, S, psm[0:64, 0:256], ALU.add)
                                        if not sample:
                                            for d_ in range(2):
                                                K.dma("sp", nsh[si, l, d_, hg * 4:hg * 4 + 4].re("h k v -> k h v"), r3(Sst[d_]), io_slot)
                                with K.scope():
                                    f64 = Ring(K, 4, [64, 512], F32, "f64g")
                                    wcg = wload(w_in[l, :, O_CG + hg * 256:O_CG + hg * 256 + 256], 8, 256)
                                    for hi in range(4):
                                        for tt_i in range(NT[g]):
                                            tsl = slice(tt_i * 512, (tt_i + 1) * 512)
                                            psg = psum()
                                            proj_fm64(wcg, hi * 64, hT[g][tt_i], psg[0:64, :])
                                            sg = f64.next()
                                            K.act(sg, psg[0:64, :], AF.Sigmoid)
                                            K.tt("dve", sg, oT[:, hi, tsl], sg, ALU.mult)
                                            sq = f64.next()
                                            K.act(sq, sg, AF.Square)
                                            ps = psum()
                                            K.mm(ps[0:64, :], onesf[0:64, 0:64], sq)
                                            rn = f64.next()
                                            K.act(rn, ps[0:64, :], AF.Sqrt, bias=epsc[0:64, 0:1], scale=1.0 / 64)
                                            K.recip(rn, rn)
                                            K.stt("dve", yT[:, hg * 4 + hi, tsl], sg, hnw[:, 0:1], rn, ALU.mult, ALU.mult)
                        branch_merge(l, g, 2, yT, g1[g])
                        chk("C" + g)
                    rms_mod(g, wm2[g], sh2[g])
                    with K.scope():
                        uT = K.tile([128, 4, 512], BF16, "uT")
                        rr = Ring(K, 2, [128, 512], F32, "relu")
                        for hb in range(8):
                            w1 = wload(mlp_w1[l, :, hb * 512:(hb + 1) * 512], 8, 512)
                            w2 = [wload(mlp_w2[l, hb * 512:(hb + 1) * 512, half * 512:(half + 1) * 512], 4, 512) for half in range(2)]
                            for tt_i in range(NT[g]):
                                for hc in range(4):
                                    ps = psum()
                                    for kc in range(8):
                                        K.mm(ps, w1[:, kc, hc * 128:(hc + 1) * 128], hT[g][tt_i][:, kc, :],
                                             start=(kc == 0), stop=(kc == 7), sig=(kc == 7))
                                    r = rr.next()
                                    K.act(r, ps, AF.Relu)
                                    K.tt("pool", uT[:, hc, :], r, r, ALU.mult)
                                for oc in range(8):
                                    ps = psum()
                                    for hc in range(4):
                                        K.mm(ps, w2[oc // 4][:, hc, (oc % 4) * 128:(oc % 4 + 1) * 128], uT[:, hc, :],
                                             start=(hc == 0), stop=(hc == 3), sig=(hc == 3))
                                    xv = xT[g][tt_i][:, oc, :]
                                    K.stt("dve", xv, ps, g2[g][:, oc:oc + 1], xv, ALU.mult, ALU.add)

        chk("layers")
        with K.scope():
            fw = K.tile([128, 8], F32, "fw")
            K.dma("sp", fw, final_norm_w.re("(kc p) -> p kc", p=128), misc_slot)
            zero8 = K.tile([128, 8], F32, "z8")
            K.memset("dve", zero8, 0.0)
            yst = Ring(K, 2, [128, D], F32, "yst")
            hf = {"P": [K.tile([128, 8, 512], F32, "hfP")], "S": [K.tile([128, 8, 512], F32, "hfS%d" % i) for i in range(2)]}
            for g, dst in (("P", y_p), ("S", y_s)):
                for tt_i in range(NT[g]):
                    x = xT[g][tt_i]
                    ps = psum()
                    for kc in range(8):
                        sq = sqring.next()
                        K.act(sq, x[:, kc, :], AF.Square)
                        K.mm(ps, onesb, sq, start=(kc == 0), stop=(kc == 7))
                    rstd = rsring.next()
                    K.act(rstd, ps, AF.Sqrt, bias=epsc[:, 0:1], scale=1.0 / D)
                    K.recip(rstd, rstd)
                    for kc in range(8):
                        K.stt("dve", hf[g][tt_i][:, kc, :], x[:, kc, :], fw[:, kc:kc + 1], rstd, ALU.mult, ALU.mult)
                    for j in range(4):
                        yt = yst.next()
                        for half in range(2):
                            ps = psum()
                            for kc4 in range(4):
                                kc = half * 4 + kc4
                                K.tr(ps[:, kc4 * 128:(kc4 + 1) * 128], hf[g][tt_i][:, kc, j * 128:(j + 1) * 128], identf)
                            K.cp("act" if half else "dve", yt[:, half * 512:(half + 1) * 512], ps)
                        r0 = tt_i * 512 + j * 128
                        K.dma("sp", dst[r0:r0 + 128, :], yt, io_slot)
        K.finish()
    return nc


_CACHE = {}


def kernel(**inp):
    f = lambda k: np.ascontiguousarray(np.asarray(inp[k], dtype=np.float32))
    if "nc" not in _CACHE:
        _CACHE["nc"] = build_program()
    nc = _CACHE["nc"]
    rpb = f("na_rpb")
    kc = np.arange(64)[:, None, None]
    e = np.arange(15)[None, :, None]
    qc = np.arange(64)[None, None, :]
    dc = np.clip(kc - qc + 15, 0, 30) + 0 * e
    rr = (14 - e) + 0 * dc
    rpbx = np.ascontiguousarray(rpb[:, :, rr, dc]).reshape(DEPTH, 8, 64, 15 * 64)
    shared = {
        "norm_w": f("norm_w"), "ada_w": f("ada_w"), "ada_b": f("ada_b"), "w_in": f("w_in"),
        "attn_sink": f("attn_sink"), "delta_conv": f("delta_conv"),
        "delta_a_log": f("delta_a_log").reshape(DEPTH, 16), "delta_dt_bias": f("delta_dt_bias").reshape(DEPTH, 16),
        "delta_norm_w": f("delta_norm_w"), "hgrn_lb": f("hgrn_lb").reshape(DEPTH, 1024),
        "hgrn_norm_w": f("hgrn_norm_w"), "rpbx": rpbx, "w_branch": f("w_branch"), "w_out": f("w_out"),
        "mlp_w1": f("mlp_w1"), "mlp_w2": f("mlp_w2"), "final_norm_w": f("final_norm_w"),
        "consts": make_consts(), "ropetab": make_rope_tab(),
    }
    xp, xs = f("x_prompt"), f("x_sample")
    cak, cav, cnk, cnv = f("cache_attn_k"), f("cache_attn_v"), f("cache_na_k"), f("cache_na_v")
    sd, sh, c, cctx = f("state_delta"), f("state_hgrn"), f("c"), f("c_ctx")
    in_maps = []
    for i in range(NCORES):
        m = dict(shared)
        m["x_p"] = xp[2 * i:2 * i + 2].reshape(NP_, D)
        m["x_s"] = xs[i]
        m["cak"] = cak[i].reshape(DEPTH, PAST, 128)
        m["cav"] = cav[i].reshape(DEPTH, PAST, 128)
        m["cnk"] = cnk[i].reshape(DEPTH, PAST, 512)
        m["cnv"] = cnv[i].reshape(DEPTH, PAST, 512)
        m["sd"] = sd[i]
        m["sh"] = sh[i]
        m["cvec"] = np.stack([cctx, c[i]], 0)
        in_maps.append(m)
    res = run_bass_kernel_spmd(nc, in_maps, core_ids=list(range(NCORES)))
    R = res.results
    cat = lambda k: np.concatenate([np.asarray(r[k]) for r in R], 0)
    y_prompt = cat("y_p").reshape(16, 256, D)
    y_sample = cat("y_s").reshape(8, 1024, D)
    nak = cat("nak").reshape(16, DEPTH, 256, 2, 64)
    nav = cat("nav").reshape(16, DEPTH, 256, 2, 64)
    nnk = cat("nnk").reshape(16, DEPTH, 256, 8, 64)
    nnv = cat("nnv").reshape(16, DEPTH, 256, 8, 64)
    nsd = cat("nsd").reshape(16, DEPTH, 2, 8, 64, 64)
    nsh = cat("nsh").reshape(16, DEPTH, 2, 8, 64, 64)
    return tuple(np.ascontiguousarray(a, dtype=np.float32) for a in (y_prompt, y_sample, nak, nav, nnk, nnv, nsd, nsh))
```

```python
import math
from contextlib import ExitStack, contextmanager
import numpy as np
import ml_dtypes
import concourse.bass as bass
import concourse.mybir as mybir
from concourse.bass_utils import run_bass_kernel_spmd

F32 = mybir.dt.float32
BF16 = mybir.dt.bfloat16
ALU = mybir.AluOpType
AF = mybir.ActivationFunctionType

D = 1024
DEPTH = 4
NCORES = 8
NP_ = 512
NS_ = 1024
PAST = 512
EPS = 1e-6
NEGM = -30000.0
O_AQ, O_AK, O_AV = 0, 512, 640
O_BQ, O_BK, O_BV, O_BZ, O_BA, O_BB = 768, 1280, 1792, 2304, 2816, 2832
O_CQ, O_CF, O_CI, O_CG = 2848, 3360, 4384, 4896
O_DQ, O_DK, O_DV, O_G = 5408, 5920, 6432, 6944
N_IN = 11040

C_ID, C_ONE = 0, 128
C_U = (256, 320)
C_W2 = (384, 448)
C_NM = (512, 576)
C_NMT = (640, 704)
C_SL = (768, 832)
C_MT = (896, 960)
C_ROPE, C_COLM, C_MPREV, C_MNEXT = 1024, 1088, 1152, 1280
C_ZERO = 1408
NCST = 1472
MID = (32, 31)


def make_consts():
    c = np.zeros((128, NCST), np.float32)
    c[:, C_ID:C_ID + 128] = np.eye(128)
    c[:, C_ONE:C_ONE + 128] = 1.0
    t = np.arange(64)
    for d in range(2):
        if d == 0:
            U = (t[:, None] <= t[None, :]).astype(np.float32)
        else:
            U = (t[:, None] >= t[None, :]).astype(np.float32)
        c[:64, C_U[d]:C_U[d] + 64] = U
        c[:64, C_W2[d]:C_W2[d] + 64] = 1.0 - U
        incl = U.T
        c[:64, C_NM[d]:C_NM[d] + 64] = np.where(incl > 0, 0.0, NEGM)
        c[:64, C_NMT[d]:C_NMT[d] + 64] = np.where(incl.T > 0, 0.0, NEGM)
        c[:64, C_SL[d]:C_SL[d] + 64] = incl - np.eye(64)
        c[:64, C_MT[d]:C_MT[d] + 64] = incl.T
    P = np.zeros((64, 64), np.float32)
    for half in (0, 32):
        for i in range(16):
            P[half + i, half + i + 16] = -1.0
            P[half + 16 + i, half + i] = 1.0
    c[:64, C_ROPE:C_ROPE + 64] = P.T
    qc = np.arange(64)
    ws = np.clip(qc - 8, 0, 48)
    kc = np.arange(64)
    c[:64, C_COLM:C_COLM + 64] = ((kc[:, None] >= ws[None, :]) & (kc[:, None] < ws[None, :] + 16)).astype(np.float32)
    k = np.arange(128)
    c[:, C_MPREV:C_MPREV + 128] = (k[:, None] >= k[None, :]).astype(np.float32)
    c[:, C_MNEXT:C_MNEXT + 128] = (k[:, None] <= k[None, :]).astype(np.float32)
    return c


def make_rope_tab():
    tt = np.arange(NS_)
    inv = (10000.0 ** (-np.arange(16, dtype=np.float32) / 16)).astype(np.float32)
    tab = np.zeros((64, 2, NS_), np.float32)
    for half, pos in ((0, tt // 64), (32, tt % 64)):
        ang = pos.astype(np.float32)[None, :] * inv[:, None]
        cs, sn = np.cos(ang).astype(np.float32), np.sin(ang).astype(np.float32)
        tab[half:half + 16, 0], tab[half + 16:half + 32, 0] = cs, cs
        tab[half:half + 16, 1], tab[half + 16:half + 32, 1] = sn, sn
    return tab


class V:
    __slots__ = ("ap", "toks")

    def __init__(self, ap, toks=()):
        self.ap = ap
        self.toks = toks

    def __getitem__(self, idx):
        return V(self.ap[idx], self.toks)

    def bc(self, shape):
        return V(self.ap.broadcast_to(list(shape)), self.toks)

    def un(self, axis):
        return V(self.ap.unsqueeze(axis), self.toks)

    def re(self, pat, **kw):
        return V(self.ap.rearrange(pat, **kw), self.toks)

    def bitcast(self, dt):
        return V(self.ap.bitcast(dt), self.toks)


class Slot:
    def __init__(self, key, sem):
        self.key, self.sem, self.cnt = key, sem, 0


class KB:
    def __init__(self, nc, es):
        self.nc = nc
        self.es = es
        self.eng = {"pe": nc.tensor, "act": nc.scalar, "dve": nc.vector, "pool": nc.gpsimd, "sp": nc.sync}
        self.semh = {}
        self.cnt = {}
        for e in ("pe", "act", "dve", "pool"):
            self.semh[e] = es.enter_context(nc.semaphore("s_" + e))
            self.cnt[e] = 0
        self.seen = {e: {} for e in self.eng}
        self.lastw = {}
        self.readers = {}
        self.slots = []
        self.ntile = 0
        self.scopes = []

    def tile(self, shape, dt, name=None):
        self.ntile += 1
        nm = "%s_%d" % (name or "t", self.ntile)
        st = self.scopes[-1] if self.scopes else self.es
        h = st.enter_context(self.nc.sbuf_tensor(nm, list(shape), dt))
        v = V(h.ap(), (nm,))
        self.memset("pool", v, 0.0)
        return v

    def slot(self):
        s = Slot("d%d" % len(self.slots), self.es.enter_context(self.nc.semaphore("sd%d" % len(self.slots))))
        self.semh[s.key] = s.sem
        self.slots.append(s)
        return s

    @contextmanager
    def scope(self):
        st = ExitStack()
        self.scopes.append(st)
        try:
            yield
        finally:
            self.barrier()
            self.scopes.pop()
            st.close()

    def barrier(self):
        marks = [(e, c) for e, c in self.cnt.items() if c > 0] + [(s.key, s.cnt) for s in self.slots if s.cnt > 0]
        for e in ("pe", "act", "dve", "pool", "sp"):
            self._wait(e, marks, True)

    def _wait(self, e, marks, full=False):
        need = {}
        for (k, v) in marks:
            if k == e and e == "pe":
                continue
            if need.get(k, 0) < v:
                need[k] = v
        for k, v in need.items():
            if self.seen[e].get(k, 0) < v:
                self.eng[e].wait_ge(self.semh[k], v)
                self.seen[e][k] = v

    def _deps(self, reads, writes):
        marks = []
        for t in reads:
            if t in self.lastw:
                marks.append(self.lastw[t])
        for t in writes:
            if t in self.lastw:
                marks.append(self.lastw[t])
            marks.extend(self.readers.get(t, {}).items())
        return marks

    def _record(self, mark, reads, writes):
        for t in reads:
            r = self.readers.setdefault(t, {})
            if r.get(mark[0], 0) < mark[1]:
                r[mark[0]] = mark[1]
        for t in writes:
            self.lastw[t] = mark
            self.readers[t] = {}

    def op(self, e, fn, ins, outs, sig=True):
        reads = [t for v in ins for t in v.toks]
        writes = [t for v in outs for t in v.toks]
        writes = writes + [t for t in reads if t.startswith("ps") and t not in writes]
        self._wait(e, self._deps(reads, writes))
        inst = fn()
        if sig:
            self.cnt[e] += 1
            inst.then_inc(self.semh[e], 1)
            mark = (e, self.cnt[e])
        else:
            mark = (e, self.cnt[e] + 1)
        self._record(mark, reads, writes)

    def dma(self, q, out, in_, slot, **kw):
        reads, writes = list(in_.toks), list(out.toks)
        if isinstance(slot, list):
            slot.append(slot.pop(0))
            slot = slot[-1]
        marks = self._deps(reads, writes)
        if slot.cnt > 0:
            marks.append((slot.key, slot.cnt))
        self._wait(q, marks)
        inst = self.eng[q].dma_start(out=out.ap, in_=in_.ap, **kw)
        slot.cnt += 16
        inst.then_inc(slot.sem, 16)
        self._record((slot.key, slot.cnt), reads, writes)

    def mm(self, out, lhsT, rhs, start=True, stop=True, sig=True):
        self.op("pe", lambda: self.nc.tensor.matmul(out.ap, lhsT.ap, rhs.ap, start=start, stop=stop),
                [lhsT, rhs] + ([] if start else [out]), [out], sig)

    def tr(self, out, in_, ident, sig=True):
        self.mm(out, in_, ident, sig=sig)

    def act(self, out, in_, func, bias=0.0, scale=1.0):
        ins = [in_] + [x for x in (bias, scale) if isinstance(x, V)]
        b = bias.ap if isinstance(bias, V) else bias
        s = scale.ap if isinstance(scale, V) else scale
        self.op("act", lambda: self.nc.scalar.activation(out.ap, in_.ap, func, bias=b, scale=s), ins, [out])

    def _ve(self, e):
        return self.nc.vector if e == "dve" else self.nc.gpsimd

    def tt(self, e, out, in0, in1, op):
        self.op(e, lambda: self._ve(e).tensor_tensor(out.ap, in0.ap, in1.ap, op), [in0, in1], [out])

    def ts(self, e, out, in0, s1, s2=None, op0=ALU.mult, op1=None):
        ins = [in0] + [x for x in (s1, s2) if isinstance(x, V)]
        a = s1.ap if isinstance(s1, V) else s1
        b = s2.ap if isinstance(s2, V) else s2
        if op1 is None:
            self.op(e, lambda: self._ve(e).tensor_scalar(out.ap, in0.ap, a, None, op0), ins, [out])
        else:
            self.op(e, lambda: self._ve(e).tensor_scalar(out.ap, in0.ap, a, b, op0, op1), ins, [out])

    def stt(self, e, out, in0, sc, in1, op0, op1):
        ins = [in0, in1] + ([sc] if isinstance(sc, V) else [])
        a = sc.ap if isinstance(sc, V) else sc
        self.op(e, lambda: self._ve(e).scalar_tensor_tensor(out.ap, in0.ap, a, in1.ap, op0, op1), ins, [out])

    def cp(self, e, out, in_):
        if e == "act":
            self.op("act", lambda: self.nc.scalar.copy(out.ap, in_.ap), [in_], [out])
        else:
            self.op(e, lambda: self._ve(e).tensor_copy(out.ap, in_.ap), [in_], [out])

    def memset(self, e, out, val):
        self.op(e, lambda: self._ve(e).memset(out.ap, val), [], [out])

    def recip(self, out, in_):
        self.op("dve", lambda: self.nc.vector.reciprocal(out.ap, in_.ap), [in_], [out])

    def finish(self):
        marks = [(e, c) for e, c in self.cnt.items() if c > 0] + [(s.key, s.cnt) for s in self.slots if s.cnt > 0]
        self._wait("sp", marks, True)


class StopBuild(Exception):
    pass


class Ring:
    def __init__(self, K, n, shape, dt, name="r"):
        self.t = [K.tile(shape, dt, name) for _ in range(n)]
        self.i = 0

    def next(self):
        v = self.t[self.i % len(self.t)]
        self.i += 1
        return v


def build_program(nlayers=DEPTH, stage=99):
    holder = {}
    try:
        return _build_program(nlayers, stage, holder)
    except StopBuild:
        holder["K"].finish()
        return holder["nc"]


def _build_program(nlayers, stage, holder):
    nc = bass.Bass("TRN2", target_bir_lowering=False)

    def din(name, shape, dt=F32):
        return V(nc.dram_tensor(name, list(shape), dt, kind="ExternalInput").ap())

    def dout(name, shape, dt=F32):
        return V(nc.dram_tensor(name, list(shape), dt, kind="ExternalOutput").ap())

    x_p = din("x_p", [NP_, D])
    x_s = din("x_s", [NS_, D])
    cak = din("cak", [DEPTH, PAST, 128])
    cav = din("cav", [DEPTH, PAST, 128])
    cnk = din("cnk", [DEPTH, PAST, 512])
    cnv = din("cnv", [DEPTH, PAST, 512])
    sd_in = din("sd", [DEPTH, 2, 8, 64, 64])
    sh_in = din("sh", [DEPTH, 2, 8, 64, 64])
    cvec = din("cvec", [2, D])
    norm_w = din("norm_w", [DEPTH, 2, D])
    ada_w = din("ada_w", [DEPTH, D, 6 * D])
    ada_b = din("ada_b", [DEPTH, 6 * D])
    w_in = din("w_in", [DEPTH, D, N_IN])
    attn_sink = din("attn_sink", [DEPTH, 8])
    delta_conv = din("delta_conv", [DEPTH, 5, 1536])
    delta_a_log = din("delta_a_log", [DEPTH, 16])
    delta_dt_bias = din("delta_dt_bias", [DEPTH, 16])
    delta_norm_w = din("delta_norm_w", [DEPTH, 64])
    hgrn_lb = din("hgrn_lb", [DEPTH, 1024])
    hgrn_norm_w = din("hgrn_norm_w", [DEPTH, 64])
    rpbx = din("rpbx", [DEPTH, 8, 64, 15 * 64])
    w_branch = din("w_branch", [DEPTH, 4, 512, D])
    w_out = din("w_out", [DEPTH, D, D])
    mlp_w1 = din("mlp_w1", [DEPTH, D, 4 * D])
    mlp_w2 = din("mlp_w2", [DEPTH, 4 * D, D])
    final_norm_w = din("final_norm_w", [D])
    consts = din("consts", [128, NCST])
    ropetab = din("ropetab", [64, 2, NS_])

    y_p = dout("y_p", [NP_, D])
    y_s = dout("y_s", [NS_, D])
    nak = dout("nak", [2, DEPTH, 256, 128])
    nav = dout("nav", [2, DEPTH, 256, 128])
    nnk = dout("nnk", [2, DEPTH, 256, 512])
    nnv = dout("nnv", [2, DEPTH, 256, 512])
    nsd = dout("nsd", [2, DEPTH, 2, 8, 64, 64])
    nsh = dout("nsh", [2, DEPTH, 2, 8, 64, 64])

    es = ExitStack()
    with es:
        nc_np = es.enter_context(nc.allow_non_contiguous_dma(reason="small param layouts"))
        K = KB(nc, es)
        holder["K"], holder["nc"] = K, nc
        chk_i = [0]

        def chk(name):
            chk_i[0] += 1
            holder.setdefault("chk", []).append((chk_i[0], name, nc.n_instructions()))
            if chk_i[0] == stage - 100:
                print("STOP at checkpoint", chk_i[0], name)
                raise StopBuild()
        banks = []
        for i in range(8):
            h = es.enter_context(nc.psum_tensor("ps%d" % i, [128, 512], F32))
            banks.append(V(h.ap(), ("ps%d" % i,)))
        bank_i = [0]

        def psum():
            v = banks[bank_i[0] % 6]
            bank_i[0] += 1
            return v

        cst = K.tile([128, NCST], F32, "cst")
        cstb = K.tile([128, NCST], BF16, "cstb")
        s_c = K.slot()
        K.dma("sp", cst, consts, s_c)
        K.cp("dve", cstb, cst)
        if stage == 1:
            K.finish()
            return nc
        identf = cst[:, C_ID:C_ID + 128]
        identb = cstb[:, C_ID:C_ID + 128]
        onesb = cstb[:, C_ONE:C_ONE + 128]
        onesf = cst[:, C_ONE:C_ONE + 128]

        xT = {"P": [K.tile([128, 8, 512], F32, "xP")], "S": [K.tile([128, 8, 512], F32, "xS%d" % i) for i in range(2)]}
        _hS = [K.tile([128, 8, 512], BF16, "hS%d" % i) for i in range(2)]
        hT = {"P": [_hS[0]], "S": _hS}
        NT = {"P": 1, "S": 2}
        TT = {"P": NP_, "S": NS_}
        SEQS = {"P": [(0, 256), (256, 256)], "S": [(0, 1024)]}

        WN = 4
        wbuf = [K.tile([128, 4096], BF16, "w") for _ in range(WN)]
        wslot = [K.slot() for _ in range(WN)]
        w_i = [0]

        def wload(src2d, kc, cols, pat="(kc p) c -> p kc c", q="pool"):
            i = w_i[0] % WN
            w_i[0] += 1
            assert kc * cols <= 4096
            dst = wbuf[i][:, 0:kc * cols].re("p (k c) -> p k c", c=cols)
            rows = src2d.ap.shape[0] // kc
            K.dma(q, dst[0:rows], src2d.re(pat, kc=kc), wslot[i])
            return dst[0:rows]

        stage_slot = [K.slot() for _ in range(4)]
        misc_slot = [K.slot() for _ in range(3)]
        io_slot = [K.slot() for _ in range(6)]

        csf = K.tile([128, 8, 2], F32, "csf")
        csT = K.tile([128, 8, 2], BF16, "csT")
        for gi_ in range(2):
            K.dma("sp", csf[:, :, gi_], cvec[gi_].re("(kc p) -> p kc", p=128), misc_slot)
        K.act(csT, csf, AF.Silu)

        if stage == 2:
            K.finish()
            return nc
        with K.scope():
            xin = Ring(K, 2, [128, D], F32, "xin")
            xs_ = [K.slot(), K.slot()] if False else [stage_slot[0], stage_slot[1]]
            n = 0
            for gname, src in (("P", x_p), ("S", x_s)):
                for tile_i in range(TT[gname] // 128):
                    xt = xin.next()
                    K.dma("sp", xt, src[tile_i * 128:(tile_i + 1) * 128, :], xs_[n % 2])
                    n += 1
                    for half in range(2):
                        ps = psum()
                        for kc4 in range(4):
                            kc = half * 4 + kc4
                            K.tr(ps[:, kc4 * 128:(kc4 + 1) * 128], xt[:, kc * 128:(kc + 1) * 128], identf)
                        tt_i, off = divmod(tile_i * 128, 512)
                        K.cp("act" if half else "dve",
                             xT[gname][tt_i][:, half * 4:half * 4 + 4, off:off + 128],
                             ps.re("p (k c) -> p k c", c=128))

        if stage == 3:
            K.finish()
            return nc
        def rms_mod(g, wm, sh):
            for tt_i in range(NT[g]):
                x = xT[g][tt_i]
                ps = psum()
                for kc in range(8):
                    sq = sqring.next()
                    K.act(sq, x[:, kc, :], AF.Square)
                    K.mm(ps, onesb, sq, start=(kc == 0), stop=(kc == 7))
                rstd = rsring.next()
                K.act(rstd, ps, AF.Sqrt, bias=epsc[:, 0:1], scale=1.0 / D)
                K.recip(rstd, rstd)
                for kc in range(8):
                    tmp = tmpring.next()
                    K.tt("dve" if kc % 2 else "pool", tmp, x[:, kc, :], rstd, ALU.mult)
                    K.act(hT[g][tt_i][:, kc, :], tmp, AF.Identity, bias=sh[:, kc:kc + 1], scale=wm[:, kc:kc + 1])

        epsc = K.tile([128, 1], F32, "eps")
        K.memset("dve", epsc, EPS)
        onec = K.tile([128, 1], F32, "onec")
        K.memset("dve", onec, 1.0)
        sqring = Ring(K, 2, [128, 512], BF16, "sq")
        rsring = Ring(K, 2, [128, 512], F32, "rstd")
        tmpring = Ring(K, 3, [128, 512], F32, "tmp")
        pring = Ring(K, 3, [128, 512], BF16, "pT")

        def proj_fm64(w, c0, rhs_h, out_ps):
            for kc in range(8):
                K.mm(out_ps, w[:, kc, c0:c0 + 64], rhs_h[:, kc, :], start=(kc == 0), stop=(kc == 7), sig=(kc == 7))

        def attention(qv, N, keytiles, outv, esink=None, b3=None):
            def r(v):
                return v if b3 is None else v.re("p (a b) -> p a b", b=b3)
            psn, psd = banks[6], banks[7]
            nkt = len(keytiles)
            for i, (kt, vv, mask, rng, nk) in enumerate(keytiles):
                c0, c1 = rng if rng is not None else (0, N)
                pss = psum()
                qs = qv if rng is None else qv[:, c0:c1]
                K.mm(r(pss[0:nk, c0:c1]), kt, qs)
                pT = pring.next()
                K.act(pT[0:nk, c0:c1], pss[0:nk, c0:c1], AF.Exp, scale=0.125)
                if mask is not None:
                    K.tt("pool", r(pT[0:nk, c0:c1]), r(pT[0:nk, c0:c1]), mask, ALU.mult)
                K.mm(psn[0:64, c0:c1], vv, pT[0:nk, c0:c1], start=(i == 0), stop=(i == nkt - 1))
                K.mm(psd[0:64, c0:c1], onesb[0:nk, 0:64], pT[0:nk, c0:c1], start=(i == 0), stop=(i == nkt - 1))
            den = tmpring.next()
            if esink is not None:
                K.tt("dve", r(den[0:64, 0:N]), r(psd[0:64, 0:N]), esink, ALU.add)
                K.recip(den[0:64, 0:N], den[0:64, 0:N])
            else:
                K.recip(den[0:64, 0:N], psd[0:64, 0:N])
            K.tt("dve", outv, r(psn[0:64, 0:N]), r(den[0:64, 0:N]), ALU.mult)

        def branch_merge(l, g, k, yT, gate1):
            with K.scope():
                mk = K.tile([128, 8, 512], BF16, "mk")
                gt = Ring(K, 2, [128, 512], F32, "gt")
                for tt_i in range(NT[g]):
                    wb = [wload(w_branch[l, k, :, half * 512:(half + 1) * 512], 8, 512) for half in range(2)]
                    wg = [wload(w_in[l, :, O_G + k * D + half * 512:O_G + k * D + (half + 1) * 512], 8, 512) for half in range(2)]
                    for oc in range(8):
                        psy, psg = psum(), psum()
                        for h in range(8):
                            K.mm(psy, wb[oc // 4][:, h, (oc % 4) * 128:(oc % 4 + 1) * 128], yT[:, h, tt_i * 512:(tt_i + 1) * 512],
                                 start=(h == 0), stop=(h == 7), sig=(h == 7))
                        for kc in range(8):
                            K.mm(psg, wg[oc // 4][:, kc, (oc % 4) * 128:(oc % 4 + 1) * 128], hT[g][tt_i][:, kc, :],
                                 start=(kc == 0), stop=(kc == 7), sig=(kc == 7))
                        gg = gt.next()
                        K.act(gg, psg, AF.Sigmoid)
                        K.tt("dve", mk[:, oc, :], psy, gg, ALU.mult)
                    wo = [wload(w_out[l, :, half * 512:(half + 1) * 512], 8, 512) for half in range(2)]
                    for oc2 in range(8):
                        ps = psum()
                        for oc in range(8):
                            K.mm(ps, wo[oc2 // 4][:, oc, (oc2 % 4) * 128:(oc2 % 4 + 1) * 128], mk[:, oc, :],
                                 start=(oc == 0), stop=(oc == 7), sig=(oc == 7))
                        xv = xTn[g][tt_i][:, oc2, :]
                        K.stt("dve", xv, ps, gate1[:, oc2:oc2 + 1], xv, ALU.mult, ALU.add)

        xTn = xT

        for l in range(nlayers):
            with K.scope():
                nw = K.tile([128, 2, 8], F32, "nw")
                for j_ in range(2):
                    K.dma("sp", nw[:, j_, :], norm_w[l, j_].re("(kc p) -> p kc", p=128), misc_slot)
                adab = K.tile([128, 48], F32, "adab")
                K.dma("sp", adab, ada_b[l].re("(c p) -> p c", p=128), misc_slot)
                esk = K.tile([64, 8], F32, "esk")
                K.dma("sp", esk, V(attn_sink.ap[l].partition_broadcast(64)), misc_slot)
                K.act(esk, esk, AF.Exp)
                convw = K.tile([64, 5, 24], F32, "convw")
                for j_ in range(5):
                    K.dma("sp", convw[:, j_, :], delta_conv[l, j_].re("(n d) -> d n", d=64), misc_slot)
                nea = K.tile([64, 16], F32, "nea")
                K.dma("sp", nea, V(delta_a_log.ap[l].partition_broadcast(64)), misc_slot)
                K.act(nea, nea, AF.Exp)
                K.ts("dve", nea, nea, -1.0)
                dtb = K.tile([64, 16], F32, "dtb")
                K.dma("sp", dtb, V(delta_dt_bias.ap[l].partition_broadcast(64)), misc_slot)
                dnw = K.tile([64, 1], F32, "dnw")
                K.dma("sp", dnw, delta_norm_w[l].re("(d o) -> d o", o=1), misc_slot)
                hnw = K.tile([64, 1], F32, "hnw")
                K.dma("sp", hnw, hgrn_norm_w[l].re("(d o) -> d o", o=1), misc_slot)
                def compute_lb(dst, rowbc):
                    shp = [64, 4, 1024] if rowbc else [64, 4, 16]
                    with K.scope():
                        raw = K.tile(shp, F32, "lbraw")
                        for ll in range(4):
                            if not rowbc:
                                K.dma("sp", raw[:, ll, :], hgrn_lb[ll].re("(j d) -> d j", d=64), misc_slot)
                            else:
                                K.dma("sp", raw[:, ll, :], V(hgrn_lb.ap[ll].partition_broadcast(64)), misc_slot)
                        K.act(raw, raw, AF.Exp)
                        tot = K.tile(shp[0:1] + shp[2:], F32, "lbtot")
                        K.tt("dve", tot, raw[:, 0, :], raw[:, 1, :], ALU.add)
                        K.tt("dve", tot, tot, raw[:, 2, :], ALU.add)
                        K.tt("dve", tot, tot, raw[:, 3, :], ALU.add)
                        K.recip(tot, tot)
                        if l == 0:
                            K.memset("dve", dst, 0.0)
                        else:
                            acc = K.tile(shp[0:1] + shp[2:], F32, "lbacc")
                            K.cp("dve", acc, raw[:, 1, :])
                            for ll in range(2, l + 1):
                                K.tt("dve", acc, acc, raw[:, ll, :], ALU.add)
                            K.tt("dve", dst, acc, tot, ALU.mult)

                lbT = K.tile([64, 16], F32, "lbT")
                compute_lb(lbT, False)
                omlbT = K.tile([64, 16], F32, "omlbT")
                K.ts("dve", omlbT, lbT, -1.0, 1.0, ALU.mult, ALU.add)
                chk("params")
                mod = K.tile([128, 48, 2], F32, "mod")
                for cb in range(12):
                    w = wload(ada_w[l, :, cb * 512:(cb + 1) * 512], 8, 512)
                    ps = psum()
                    for cc in range(4):
                        for kc in range(8):
                            K.mm(ps[:, cc * 2:cc * 2 + 2], w[:, kc, cc * 128:(cc + 1) * 128], csT[:, kc, :],
                                 start=(kc == 0), stop=(kc == 7), sig=(kc == 7 and cc == 3))
                    K.tt("dve", mod[:, cb * 4:cb * 4 + 4, :], ps[:, 0:8].re("p (c g) -> p c g", g=2),
                         adab[:, cb * 4:cb * 4 + 4].un(2).bc([128, 4, 2]), ALU.add)
                wm1, wm2, sh1, sh2, g1, g2 = {}, {}, {}, {}, {}, {}
                for gi, g in enumerate(("P", "S")):
                    for (dst, sc_i, nwi) in ((wm1, 1, 0), (wm2, 4, 1)):
                        t = K.tile([128, 8], F32, "wm")
                        K.stt("dve", t, mod[:, sc_i * 8:sc_i * 8 + 8, gi], 1.0, nw[:, nwi, :], ALU.add, ALU.mult)
                        dst[g] = t
                    sh1[g] = mod[:, 0:8, gi]
                    g1[g] = mod[:, 16:24, gi]
                    sh2[g] = mod[:, 24:32, gi]
                    g2[g] = mod[:, 40:48, gi]

                chk("adaln")
                for g in ("P", "S"):
                    T = TT[g]
                    sample = (g == "S")
                    rms_mod(g, wm1[g], sh1[g])
                    chk("rms" + g)
                    with K.scope():
                        qT = K.tile([64, 8, T], BF16, "aq")
                        kT = K.tile([64, 2, T], BF16, "ak")
                        vtm = K.tile([128, T // 128, 128], BF16, "av")
                        yT = K.tile([64, 8, T], BF16, "ya")
                        wq = wload(w_in[l, :, O_AQ:O_AQ + 512], 8, 512)
                        wkv = wload(w_in[l, :, O_AK:O_AK + 256], 8, 256)
                        stg = Ring(K, 2, [128, 256], F32, "stg")
                        if sample:
                            rtab = K.tile([64, 2, NS_], F32, "rtab")
                            K.dma("sp", rtab, ropetab, misc_slot)
                            xbr = Ring(K, 2, [64, 512], BF16, "xb")
                        for tt_i in range(NT[g]):
                            tsl = slice(tt_i * 512, (tt_i + 1) * 512)
                            for hh in range(10):
                                ps = psum()
                                if hh < 8:
                                    proj_fm64(wq, hh * 64, hT[g][tt_i], ps[0:64, :])
                                    dst = qT[:, hh, tsl]
                                else:
                                    proj_fm64(wkv, (hh - 8) * 64, hT[g][tt_i], ps[0:64, :])
                                    dst = kT[:, hh - 8, tsl]
                                if not sample:
                                    K.cp("act", dst, ps[0:64, :])
                                else:
                                    xb = xbr.next()
                                    K.cp("act", xb, ps[0:64, :])
                                    ps2 = psum()
                                    K.mm(ps2[0:64, :], cstb[0:64, C_ROPE:C_ROPE + 64], xb)
                                    t1, t2 = tmpring.next(), tmpring.next()
                                    K.tt("dve", t1[0:64, :], ps2[0:64, :], rtab[:, 1, tsl], ALU.mult)
                                    K.tt("pool", t2[0:64, :], xb, rtab[:, 0, tsl], ALU.mult)
                                    K.tt("dve", dst, t1[0:64, :], t2[0:64, :], ALU.add)
                            chk("Afm" + g)
                            for j in range(4):
                                ps = psum()
                                for kc in range(8):
                                    K.mm(ps[:, 0:256], hT[g][tt_i][:, kc, j * 128:(j + 1) * 128], wkv[:, kc, :],
                                         start=(kc == 0), stop=(kc == 7), sig=(kc == 7))
                                tile_i = tt_i * 4 + j
                                K.cp("dve", vtm[:, tile_i, :], ps[:, 128:256])
                                chk("Atm_mm" + g)
                                if not sample:
                                    st = stg.next()
                                    K.cp("dve", st, ps[:, 0:256])
                                    chk("Atm_cp" + g)
                                    s_, r0 = divmod(tile_i * 128, 256)
                                    K.dma("sp", nak[s_, l, r0:r0 + 128, :], st[:, 0:128], io_slot)
                                    K.dma("sp", nav[s_, l, r0:r0 + 128, :], st[:, 128:256], io_slot)
                        chk("Aproj" + g)
                        if sample:
                            ctm = K.tile([128, 4, 128], BF16, "ctk")
                            cvv = K.tile([128, 4, 128], BF16, "ctv")
                            ckT = K.tile([64, 2, PAST], BF16, "ckT")
                            K.dma("pool", ctm, cak[l].re("(j p) c -> p j c", p=128), stage_slot[2])
                            K.dma("pool", cvv, cav[l].re("(j p) c -> p j c", p=128), stage_slot[3])
                            for kv in range(2):
                                ps = psum()
                                psb = ps
                                for j in range(4):
                                    K.tr(psb[0:64, j * 128:(j + 1) * 128], ctm[:, j, kv * 64:(kv + 1) * 64], identb)
                                K.cp("dve", ckT[:, kv, :], psb[0:64, 0:512])
                        for (t0, L) in SEQS[g]:
                            nblk = L // 128
                            for kv in range(2):
                                for bi in range(nblk):
                                    kts = []
                                    if sample:
                                        for j in range(4):
                                            kts.append((ckT[:, kv, j * 128:(j + 1) * 128], cvv[:, j, kv * 64:(kv + 1) * 64], None, None, 128))
                                        blks = [(bi - 1, C_MPREV), (bi, None), (bi + 1, C_MNEXT)]
                                    else:
                                        blks = [(b, None) for b in range(nblk)]
                                    for (b, mc) in blks:
                                        if b < 0 or b >= nblk:
                                            continue
                                        k0 = t0 + b * 128
                                        m = None if mc is None else cstb[:, mc:mc + 128].un(1).bc([128, 4, 128])
                                        kts.append((kT[:, kv, k0:k0 + 128], vtm[:, k0 // 128, kv * 64:(kv + 1) * 64], m, None, 128))
                                    q0 = t0 + bi * 128
                                    attention(qT[:, 4 * kv:4 * kv + 4, q0:q0 + 128], 512, kts,
                                              yT[:, 4 * kv:4 * kv + 4, q0:q0 + 128],
                                              esk[:, 4 * kv:4 * kv + 4].un(2).bc([64, 4, 128]), b3=128)
                        chk("Aattn" + g)
                        branch_merge(l, g, 0, yT, g1[g])
                        chk("Amerge" + g)
                    with K.scope():
                        yT = K.tile([64, 8, T], BF16, "yd")
                        wdq = wload(w_in[l, :, O_DQ:O_DQ + 512], 8, 512)
                        wdk = wload(w_in[l, :, O_DK:O_DK + 512], 8, 512)
                        wdv = wload(w_in[l, :, O_DV:O_DV + 512], 8, 512)
                        chk("Dw" + g)
                        if sample:
                            vrow = K.tile([64, 16, 512], BF16, "vrow")
                            for r_ in range(16):
                                ps = psum()
                                for kc in range(8):
                                    K.mm(ps[0:64, :], hT[g][r_ // 8][:, kc, (r_ % 8) * 64:(r_ % 8 + 1) * 64], wdv[:, kc, :],
                                         start=(kc == 0), stop=(kc == 7), sig=(kc == 7))
                                K.cp("act", vrow[:, r_, :], ps[0:64, :])
                            cktm = K.tile([128, 4, 512], BF16, "cktm")
                            cvtm = K.tile([128, 4, 512], BF16, "cvtm")
                            K.dma("pool", cktm, cnk[l].re("(j p) c -> p j c", p=128), stage_slot[2])
                            K.dma("pool", cvtm, cnv[l].re("(j p) c -> p j c", p=128), stage_slot[3])
                            ckr = Ring(K, 2, [64, 512], BF16, "ckr")
                            Er = Ring(K, 2, [64, 960], BF16, "Er")
                            Ef = Ring(K, 2, [64, 960], F32, "Ef")
                            colm = cst[0:64, C_COLM:C_COLM + 64]
                        else:
                            stg = Ring(K, 2, [128, 512], F32, "stgd")
                            vtm = K.tile([128, T // 128, 512], BF16, "dvtm")
                            for tile_i in range(T // 128):
                                tt_i, j = divmod(tile_i, 4)
                                for which, w in ((0, wdk), (1, wdv)):
                                    ps = psum()
                                    for kc in range(8):
                                        K.mm(ps, hT[g][tt_i][:, kc, j * 128:(j + 1) * 128], w[:, kc, :],
                                             start=(kc == 0), stop=(kc == 7), sig=(kc == 7))
                                    chk("Dmm" + g)
                                    st = stg.next()
                                    K.cp("act", st, ps)
                                    chk("Dcp" + g)
                                    s_, r0 = divmod(tile_i * 128, 256)
                                    K.dma("sp", (nnk if which == 0 else nnv)[s_, l, r0:r0 + 128, :], st, io_slot)
                                    chk("Ddma" + g)
                                    if which == 1:
                                        K.cp("dve", vtm[:, tile_i, :], ps)
                        chk("Dtm" + g)
                        qh = Ring(K, 2, [64, T], BF16, "dq")
                        kh = Ring(K, 2, [64, T], BF16, "dk")
                        for h in range(8):
                            q_h, k_h = qh.next(), kh.next()
                            for tt_i in range(NT[g]):
                                for (w, dst) in ((wdq, q_h), (wdk, k_h)):
                                    ps = psum()
                                    proj_fm64(w, h * 64, hT[g][tt_i], ps[0:64, :])
                                    K.cp("act", dst[:, tt_i * 512:(tt_i + 1) * 512], ps[0:64, :])
                            chk("Dproj%d" % h + g)
                            if sample:
                                ck = ckr.next()
                                ps = psum()
                                psb = ps
                                for j in range(4):
                                    K.tr(psb[0:64, j * 128:(j + 1) * 128], cktm[:, j, h * 64:(h + 1) * 64], identb)
                                K.cp("dve", ck, psb[0:64, 0:512])
                                ef = Ef.next()
                                K.dma("sp", ef, rpbx[l, h], stage_slot[h % 2])
                                K.act(ef, ef, AF.Exp)
                                E = Er.next()
                                K.tt("dve", E.re("p (e c) -> p e c", c=64), ef.re("p (e c) -> p e c", c=64),
                                     colm.un(1).bc([64, 15, 64]), ALU.mult)
                                for hf_ in range(2):
                                    kts = [(ck[:, j * 128:(j + 1) * 128], cvtm[:, j, h * 64:(h + 1) * 64], None, None, 128) for j in range(4)]
                                    for kr in range(16):
                                        qs = [r_ for r_ in range(8 * hf_, 8 * hf_ + 8)
                                              if min(max(r_ - 4, 0), 8) <= kr < min(max(r_ - 4, 0), 8) + 8]
                                        if not qs:
                                            continue
                                        c0, c1 = (qs[0] - 8 * hf_) * 64, (qs[-1] + 1 - 8 * hf_) * 64
                                        e0 = qs[0] - kr + 7
                                        kts.append((k_h[:, kr * 64:(kr + 1) * 64], vrow[:, kr, h * 64:(h + 1) * 64],
                                                    E[:, e0 * 64:(e0 + len(qs)) * 64], (c0, c1), 64))
                                    attention(q_h[:, hf_ * 512:(hf_ + 1) * 512], 512, kts, yT[:, h, hf_ * 512:(hf_ + 1) * 512])
                            else:
                                for (t0, L) in SEQS[g]:
                                    kts = [(k_h[:, t0 + j * 128:t0 + (j + 1) * 128], vtm[:, (t0 + j * 128) // 128, h * 64:(h + 1) * 64],
                                            None, None, 128) for j in range(L // 128)]
                                    attention(q_h[:, t0:t0 + L], L, kts, yT[:, h, t0:t0 + L])
                                    chk("Dattn%d" % h + g)
                        branch_merge(l, g, 3, yT, g1[g])
                        chk("D" + g)

                    def r3(v):
                        return v.re("p (h c) -> p h c", c=64)

                    def bfv(v):
                        return v.bitcast(BF16)[:, 0:256]

                    nch = T // 64
                    with K.scope():
                        yT = K.tile([64, 8, T], BF16, "yb")
                        gb = K.tile([64, nch, 32], F32, "gb")
                        wab = wload(w_in[l, :, O_BA:O_BA + 32], 8, 32)
                        for c in range(nch):
                            tt_i, off = divmod(c * 64, 512)
                            ps = psum()
                            for kc in range(8):
                                K.mm(ps[0:64, 0:32], hT[g][tt_i][:, kc, off:off + 64], wab[:, kc, :],
                                     start=(kc == 0), stop=(kc == 7), sig=(kc == 7))
                            K.cp("act", gb[:, c, :], ps[0:64, 0:32])
                        gv, bv = gb[:, :, 0:16], gb[:, :, 16:32]
                        K.tt("dve", gv, gv, dtb.un(1).bc([64, nch, 16]), ALU.add)
                        K.act(gv, gv, AF.Exp)
                        K.act(gv, gv, AF.Ln, bias=onec[0:64, 0:1])
                        K.tt("dve", gv, gv, nea.un(1).bc([64, nch, 16]), ALU.mult)
                        K.act(bv, bv, AF.Sigmoid)
                        for hg in range(2):
                            with K.scope():
                                qkv = K.tile([64, 12, T], BF16, "qkv")
                                oT = K.tile([64, 4, T], F32, "oTb")
                                w3 = [wload(w_in[l, :, o_ + hg * 256:o_ + hg * 256 + 256], 8, 256) for o_ in (O_BQ, O_BK, O_BV)]
                                nsq = len(SEQS[g])
                                Ls = SEQS[g][0][1]
                                with K.scope():
                                    rawr = Ring(K, 2, [64, nsq, Ls + 4], F32, "raw")
                                    for rt in rawr.t:
                                        K.memset("pool", rt, 0.0)
                                    cvr = Ring(K, 2, [64, T], F32, "cv")
                                    f64 = Ring(K, 4, [64, 512], F32, "f64")
                                    for which in range(3):
                                        for hi in range(4):
                                            h = hg * 4 + hi
                                            raw = rawr.next()
                                            for tt_i in range(NT[g]):
                                                ps = psum()
                                                proj_fm64(w3[which], hi * 64, hT[g][tt_i], ps[0:64, :])
                                                if sample:
                                                    K.cp("act", raw[:, 0, 2 + tt_i * 512:2 + (tt_i + 1) * 512], ps[0:64, :])
                                                else:
                                                    for s_ in range(2):
                                                        K.cp("act", raw[:, s_, 2:2 + 256], ps[0:64, s_ * 256:(s_ + 1) * 256])
                                            cv = cvr.next()
                                            ch = which * 8 + h
                                            for s_, (t0, L) in enumerate(SEQS[g]):
                                                K.ts("dve", cv[:, t0:t0 + L], raw[:, s_, 0:L], convw[:, 0, ch:ch + 1])
                                                for j in range(1, 5):
                                                    K.stt("dve", cv[:, t0:t0 + L], raw[:, s_, j:j + L],
                                                          convw[:, j, ch:ch + 1], cv[:, t0:t0 + L], ALU.mult, ALU.add)
                                            K.act(cv, cv, AF.Silu)
                                            if which < 2:
                                                for tt_i in range(NT[g]):
                                                    tsl = slice(tt_i * 512, (tt_i + 1) * 512)
                                                    sq = f64.next()
                                                    K.act(sq, cv[:, tsl], AF.Square)
                                                    ps = psum()
                                                    K.mm(ps[0:64, :], onesf[0:64, 0:64], sq)
                                                    rn = f64.next()
                                                    K.act(rn, ps[0:64, :], AF.Sqrt, bias=epsc[0:64, 0:1], scale=1.0)
                                                    K.recip(rn, rn)
                                                    K.stt("dve", qkv[:, which * 4 + hi, tsl], cv[:, tsl],
                                                          (0.125 if which == 0 else 1.0), rn, ALU.mult, ALU.mult)
                                            else:
                                                K.cp("pool", qkv[:, 8 + hi, :], cv)
                                with K.scope():
                                    Lr = Ring(K, 20, [64, 256], F32, "bL")
                                    Nr = Ring(K, 6, [64, 256], F32, "bN")
                                    Sst = [K.tile([64, 256], F32, "Sb%d" % d_) for d_ in range(2)]
                                    idf = cst[0:64, C_ID:C_ID + 64]
                                    idb = cstb[0:64, C_ID:C_ID + 64]
                                    one64 = onesf[0:64, 0:64]
                                    HS = [slice(hi * 64, (hi + 1) * 64) for hi in range(4)]
                                    for si, (t0, L) in enumerate(SEQS[g]):
                                        ncs = L // 64
                                        for d_ in range(2):
                                            if sample:
                                                K.dma("sp", r3(Sst[d_]), sd_in[l, d_, hg * 4:hg * 4 + 4].re("h k v -> k h v"), stage_slot[d_])
                                            else:
                                                K.memset("pool", Sst[d_], 0.0)
                                        visited = set()
                                        for i in range(ncs):
                                            for d_ in range(2):
                                                c = i if d_ == 0 else ncs - 1 - i
                                                tok0 = t0 + c * 64
                                                cg = tok0 // 64
                                                csl = slice(tok0, tok0 + 64)
                                                U_ = cst[0:64, C_U[d_]:C_U[d_] + 64]
                                                nm_ = cst[0:64, C_NM[d_]:C_NM[d_] + 64]
                                                nmT_ = cst[0:64, C_NMT[d_]:C_NMT[d_] + 64]
                                                SL_ = cst[0:64, C_SL[d_]:C_SL[d_] + 64]
                                                gcol = gb[:, cg, d_ * 8 + hg * 4:d_ * 8 + hg * 4 + 4]
                                                bcol = gb[:, cg, 16 + d_ * 8 + hg * 4:16 + d_ * 8 + hg * 4 + 4]
                                                S = Sst[d_]
                                                ps = psum()
                                                psb = ps
                                                for hi in range(4):
                                                    K.tr(psb[0:64, hi * 64:(hi + 1) * 64], qkv[:, 4 + hi, csl], idb, sig=False)
                                                    K.tr(psb[0:64, 256 + hi * 64:256 + (hi + 1) * 64], qkv[:, 8 + hi, csl], idb, sig=(hi == 3))
                                                Ktm, Vtm = Lr.next(), Lr.next()
                                                K.cp("act", Ktm, psb[0:64, 0:256])
                                                K.cp("dve", Vtm, psb[0:64, 256:512])
                                                gU = Lr.next()
                                                K.tt("pool", r3(gU), U_.un(1).bc([64, 4, 64]), gcol.un(2).bc([64, 4, 64]), ALU.mult)
                                                ps1 = psum()
                                                K.mm(ps1[0:64, 0:256], one64, gU, sig=False)
                                                K.mm(ps1[0:64, 256:260], U_, gcol, sig=False)
                                                K.mm(ps1[0:64, 260:264], one64, gcol)
                                                sm = Lr.next()
                                                K.cp("dve", sm[:, 0:8], ps1[0:64, 256:264])
                                                gc, gl = sm[:, 0:4], sm[:, 4:8]
                                                K.act(sm[:, 8:12], gc, AF.Exp)
                                                K.tt("dve", sm[:, 12:16], gl, gc, ALU.subtract)
                                                K.act(sm[:, 12:16], sm[:, 12:16], AF.Exp)
                                                K.act(sm[:, 16:20], gl, AF.Exp)
                                                K.tt("dve", sm[:, 20:24], sm[:, 8:12], bcol, ALU.mult)
                                                be, ekd, cd = sm[:, 20:24], sm[:, 12:16], sm[:, 16:20]
                                                gcrow = Lr.next()
                                                K.cp("act", gcrow, ps1[0:64, 0:256])
                                                G = Lr.next()
                                                K.tt("dve", r3(G), nm_.un(1).bc([64, 4, 64]), r3(gcrow), ALU.subtract)
                                                K.tt("dve", r3(G), r3(G), gc.un(2).bc([64, 4, 64]), ALU.add)
                                                K.act(G, G, AF.Exp)
                                                GT = Lr.next()
                                                K.tt("pool", r3(GT), r3(gcrow), nmT_.un(1).bc([64, 4, 64]), ALU.add)
                                                K.tt("pool", r3(GT), r3(GT), gc.un(2).bc([64, 4, 64]), ALU.subtract)
                                                K.act(GT, GT, AF.Exp)
                                                Erow = Lr.next()
                                                K.act(Erow, gcrow, AF.Exp)
                                                psk = psum()
                                                for hi in range(4):
                                                    K.mm(psk[0:64, HS[hi]], qkv[:, 4 + hi, csl], qkv[:, 4 + hi, csl], sig=(hi == 3))
                                                bsl = Lr.next()
                                                K.tt("pool", r3(bsl), SL_.un(1).bc([64, 4, 64]), bcol.un(2).bc([64, 4, 64]), ALU.mult)
                                                A = Lr.next()
                                                K.tt("dve", A, psk[0:64, 0:256], G, ALU.mult)
                                                K.tt("dve", A, A, bsl, ALU.mult)
                                                pst = psum()
                                                for hi in range(4):
                                                    K.tr(pst[0:64, HS[hi]], A[:, HS[hi]], idf, sig=(hi == 3))
                                                AT = Lr.next()
                                                K.cp("act", AT, pst[0:64, 0:256])
                                                X = Nr.next()
                                                K.tt("dve", r3(X), idf.un(1).bc([64, 4, 64]), r3(pst[0:64, 0:256]), ALU.subtract)
                                                P_, PT = A, AT
                                                for kk in range(1, 6):
                                                    psp = psum()
                                                    for hi in range(4):
                                                        K.mm(psp[0:64, HS[hi]], PT[:, HS[hi]], P_[:, HS[hi]], sig=(hi == 3))
                                                    Pn = Nr.next()
                                                    K.cp("act", Pn, psp[0:64, 0:256])
                                                    PTn = None
                                                    if kk < 5:
                                                        pspt = psum()
                                                        for hi in range(4):
                                                            K.mm(pspt[0:64, HS[hi]], P_[:, HS[hi]], PT[:, HS[hi]], sig=(hi == 3))
                                                        PTn = Nr.next()
                                                        K.cp("pool", PTn, pspt[0:64, 0:256]) if False else K.cp("dve", PTn, pspt[0:64, 0:256])
                                                    psx = psum()
                                                    for hi in range(4):
                                                        K.mm(psx[0:64, HS[hi]], Pn[:, HS[hi]], X[:, HS[hi]], sig=(hi == 3))
                                                    Xn = Nr.next() if kk < 5 else Lr.next()
                                                    K.tt("dve", Xn, psx[0:64, 0:256], X, ALU.add)
                                                    P_, PT, X = Pn, PTn, Xn
                                                TT_ = X
                                                Rk, Rv, Kd = Lr.next(), Lr.next(), Lr.next()
                                                K.tt("pool", r3(Rk), r3(Ktm), be.un(2).bc([64, 4, 64]), ALU.mult)
                                                K.tt("pool", r3(Rv), r3(Vtm), bcol.un(2).bc([64, 4, 64]), ALU.mult)
                                                K.tt("pool", r3(Kd), r3(Ktm), ekd.un(2).bc([64, 4, 64]), ALU.mult)
                                                psw = psum()
                                                for hi in range(4):
                                                    K.mm(psw[0:64, HS[hi]], Rk[:, HS[hi]], TT_[:, HS[hi]], sig=(hi == 3))
                                                nWkT = Lr.next()
                                                K.ts("dve", nWkT, psw[0:64, 0:256], -1.0)
                                                psq = psum()
                                                for hi in range(4):
                                                    K.mm(psq[0:64, HS[hi]], qkv[:, 4 + hi, csl], qkv[:, hi, csl], sig=(hi == 3))
                                                PqkT = Lr.next()
                                                K.tt("dve", PqkT, psq[0:64, 0:256], GT, ALU.mult)
                                                QdT = Lr.next()
                                                K.tt("pool", r3(QdT), qkv[:, 0:4, csl], r3(Erow), ALU.mult)
                                                psu = psum()
                                                for hi in range(4):
                                                    K.mm(psu[0:64, HS[hi]], TT_[:, HS[hi]], Rv[:, HS[hi]], start=True, stop=False, sig=False)
                                                    K.mm(psu[0:64, HS[hi]], nWkT[:, HS[hi]], S[:, HS[hi]], start=False, stop=True, sig=(hi == 3))
                                                Uc = Lr.next()
                                                K.cp("act", Uc, psu[0:64, 0:256])
                                                pso = psum()
                                                for hi in range(4):
                                                    K.mm(pso[0:64, HS[hi]], S[:, HS[hi]], QdT[:, HS[hi]], start=True, stop=False, sig=False)
                                                    K.mm(pso[0:64, HS[hi]], Uc[:, HS[hi]], PqkT[:, HS[hi]], start=False, stop=True, sig=(hi == 3))
                                                ov = oT[:, :, csl]
                                                if c not in visited:
                                                    K.cp("act", ov, r3(pso[0:64, 0:256]))
                                                    visited.add(c)
                                                else:
                                                    K.tt("dve", ov, ov, r3(pso[0:64, 0:256]), ALU.add)
                                                psm = psum()
                                                for hi in range(4):
                                                    K.mm(psm[0:64, HS[hi]], Kd[:, HS[hi]], Uc[:, HS[hi]], sig=(hi == 3))
                                                K.tt("dve", r3(S), r3(S), cd.un(2).bc([64, 4, 64]), ALU.mult)
                                                K.tt("dve", S, S, psm[0:64, 0:256], ALU.add)
                                        if not sample:
                                            for d_ in range(2):
                                                K.dma("sp", nsd[si, l, d_, hg * 4:hg * 4 + 4].re("h k v -> k h v"), r3(Sst[d_]), io_slot)
                                with K.scope():
                                    f64 = Ring(K, 4, [64, 512], F32, "f64z")
                                    wz = wload(w_in[l, :, O_BZ + hg * 256:O_BZ + hg * 256 + 256], 8, 256)
                                    for hi in range(4):
                                        for tt_i in range(NT[g]):
                                            tsl = slice(tt_i * 512, (tt_i + 1) * 512)
                                            sq = f64.next()
                                            K.act(sq, oT[:, hi, tsl], AF.Square)
                                            ps = psum()
                                            K.mm(ps[0:64, :], one64, sq)
                                            rn = f64.next()
                                            K.act(rn, ps[0:64, :], AF.Sqrt, bias=epsc[0:64, 0:1], scale=1.0 / 64)
                                            K.recip(rn, rn)
                                            psz = psum()
                                            proj_fm64(wz, hi * 64, hT[g][tt_i], psz[0:64, :])
                                            sz = f64.next()
                                            K.act(sz, psz[0:64, :], AF.Silu)
                                            K.tt("dve", rn, oT[:, hi, tsl], rn, ALU.mult)
                                            K.stt("dve", yT[:, hg * 4 + hi, tsl], rn, dnw[:, 0:1], sz, ALU.mult, ALU.mult)
                        branch_merge(l, g, 1, yT, g1[g])
                        chk("B" + g)

                    with K.scope():
                        yT = K.tile([64, 8, T], BF16, "yc")
                        lbR = K.tile([64, 1024], F32, "lbR")
                        omlbR = K.tile([64, 1024], F32, "omlbR")
                        compute_lb(lbR, True)
                        K.ts("dve", omlbR, lbR, -1.0, 1.0, ALU.mult, ALU.add)
                        for hg in range(2):
                            with K.scope():
                                qT = K.tile([64, 4, T], BF16, "cq")
                                kT = K.tile([64, 2, 4, T], BF16, "ckk")
                                oT = K.tile([64, 4, T], F32, "oTc")
                                wcq = wload(w_in[l, :, O_CQ + hg * 256:O_CQ + hg * 256 + 256], 8, 256)
                                wcf = [wload(w_in[l, :, O_CF + d_ * 512 + hg * 256:O_CF + d_ * 512 + hg * 256 + 256], 8, 256) for d_ in range(2)]
                                wci = wload(w_in[l, :, O_CI + hg * 256:O_CI + hg * 256 + 256], 8, 256)
                                with K.scope():
                                    f64 = Ring(K, 4, [64, 512], F32, "f64c")
                                    for hi in range(4):
                                        for tt_i in range(NT[g]):
                                            tsl = slice(tt_i * 512, (tt_i + 1) * 512)
                                            ps = psum()
                                            proj_fm64(wcq, hi * 64, hT[g][tt_i], ps[0:64, :])
                                            K.act(qT[:, hi, tsl], ps[0:64, :], AF.Silu)
                                            for d_ in range(2):
                                                ps = psum()
                                                proj_fm64(wcf[d_], hi * 64, hT[g][tt_i], ps[0:64, :])
                                                sg = f64.next()
                                                K.act(sg, ps[0:64, :], AF.Sigmoid, scale=-1.0)
                                                col = d_ * 8 + hg * 4 + hi
                                                K.ts("dve", kT[:, d_, hi, tsl], sg, omlbT[:, col:col + 1])
                                with K.scope():
                                    Lr = Ring(K, 18, [64, 256], F32, "cL")
                                    Sst = [K.tile([64, 256], F32, "Sc%d" % d_) for d_ in range(2)]
                                    HS = [slice(hi * 64, (hi + 1) * 64) for hi in range(4)]
                                    for si, (t0, L) in enumerate(SEQS[g]):
                                        ncs = L // 64
                                        for d_ in range(2):
                                            if sample:
                                                K.dma("sp", r3(Sst[d_]), sh_in[l, d_, hg * 4:hg * 4 + 4].re("h k v -> k h v"), stage_slot[d_])
                                            else:
                                                K.memset("pool", Sst[d_], 0.0)
                                        visited = set()
                                        for i in range(ncs):
                                            for d_ in range(2):
                                                c = i if d_ == 0 else ncs - 1 - i
                                                tok0 = t0 + c * 64
                                                csl = slice(tok0, tok0 + 64)
                                                tt_i, off = divmod(tok0, 512)
                                                U_ = cst[0:64, C_U[d_]:C_U[d_] + 64]
                                                W2_ = cst[0:64, C_W2[d_]:C_W2[d_] + 64]
                                                MT_ = cst[0:64, C_MT[d_]:C_MT[d_] + 64]
                                                mid = MID[d_]
                                                last = 63 if d_ == 0 else 0
                                                S = Sst[d_]
                                                psf, psv = psum(), psum()
                                                for kc in range(8):
                                                    K.mm(psf[0:64, 0:256], hT[g][tt_i][:, kc, off:off + 64], wcf[d_][:, kc, :],
                                                         start=(kc == 0), stop=(kc == 7), sig=(kc == 7))
                                                for kc in range(8):
                                                    K.mm(psv[0:64, 0:256], hT[g][tt_i][:, kc, off:off + 64], wci[:, kc, :],
                                                         start=(kc == 0), stop=(kc == 7), sig=(kc == 7))
                                                Vt = bfv(Lr.next())
                                                K.cp("act", Vt, psv[0:64, 0:256])
                                                f = Lr.next()
                                                K.act(f, psf[0:64, 0:256], AF.Sigmoid)
                                                cs_ = slice(d_ * 512 + hg * 256, d_ * 512 + hg * 256 + 256)
                                                K.tt("dve", f, f, omlbR[:, cs_], ALU.mult)
                                                K.tt("dve", f, f, lbR[:, cs_], ALU.add)
                                                lf = Lr.next()
                                                K.act(lf, f, AF.Ln)
                                                ktm = Lr.next()
                                                K.ts("pool", ktm, f, -1.0, 1.0, ALU.mult, ALU.add)
                                                psb_ = psum()
                                                for hi in range(4):
                                                    K.mm(psb_[0:64, HS[hi]], lf[:, HS[hi]], U_, sig=(hi == 3))
                                                psw2 = psum()
                                                K.mm(psw2[0:64, 0:256], W2_, lf)
                                                bT = Lr.next()
                                                K.cp("dve", bT, psb_[0:64, 0:256])
                                                Eb = Lr.next()
                                                K.act(Eb, bT, AF.Exp)
                                                bp = Lr.next()
                                                K.tt("dve", r3(bp), r3(bT), r3(bT)[:, :, mid:mid + 1].bc([64, 4, 64]), ALU.subtract)
                                                Ebp, Ebn = Lr.next(), Lr.next()
                                                K.act(Ebp, bp, AF.Exp)
                                                K.act(Ebn, bp, AF.Exp, scale=-1.0)
                                                QiT, KiT, QdT = bfv(Lr.next()), bfv(Lr.next()), Lr.next()
                                                K.tt("pool", r3(QiT), qT[:, :, csl], r3(Ebp), ALU.mult)
                                                K.tt("pool", r3(KiT), kT[:, d_, :, csl], r3(Ebn), ALU.mult)
                                                K.tt("dve", r3(QdT), qT[:, :, csl], r3(Eb), ALU.mult)
                                                Ekd = Lr.next()
                                                K.act(Ekd, psw2[0:64, 0:256], AF.Exp)
                                                Kd = bfv(Lr.next())
                                                K.tt("dve", Kd, ktm, Ekd, ALU.mult)
                                                psa = psum()
                                                if d_ == 0:
                                                    f_t, p_t, p_j, z_j = (32, 64), (0, 32), (0, 32), (32, 64)
                                                else:
                                                    f_t, p_t, p_j, z_j = (0, 32), (32, 64), (32, 64), (0, 32)
                                                zer = cstb[0:64, C_ZERO:C_ZERO + 32]
                                                for hi in range(4):
                                                    b0 = hi * 64
                                                    K.mm(psa[0:64, b0 + f_t[0]:b0 + f_t[1]], KiT[:, HS[hi]], QiT[:, b0 + f_t[0]:b0 + f_t[1]], sig=False)
                                                    K.mm(psa[p_j[0]:p_j[1], b0 + p_t[0]:b0 + p_t[1]], KiT[:, b0 + p_j[0]:b0 + p_j[1]],
                                                         QiT[:, b0 + p_t[0]:b0 + p_t[1]], sig=False)
                                                    K.mm(psa[z_j[0]:z_j[1], b0 + p_t[0]:b0 + p_t[1]], zer, QiT[:, b0 + p_t[0]:b0 + p_t[1]], sig=(hi == 3))
                                                att = bfv(Lr.next())
                                                K.tt("dve", r3(att), r3(psa[0:64, 0:256]), MT_.un(1).bc([64, 4, 64]), ALU.mult)
                                                pso = psum()
                                                for hi in range(4):
                                                    K.mm(pso[0:64, HS[hi]], Vt[:, HS[hi]], att[:, HS[hi]], start=True, stop=False, sig=False)
                                                    K.mm(pso[0:64, HS[hi]], S[:, HS[hi]], QdT[:, HS[hi]], start=False, stop=True, sig=(hi == 3))
                                                ov = oT[:, :, csl]
                                                if c not in visited:
                                                    K.cp("act", ov, r3(pso[0:64, 0:256]))
                                                    visited.add(c)
                                                else:
                                                    K.tt("dve", ov, ov, r3(pso[0:64, 0:256]), ALU.add)
                                                psm = psum()
                                                for hi in range(4):
                                                    K.mm(psm[0:64, HS[hi]], Kd[:, HS[hi]], Vt[:, HS[hi]], sig=(hi == 3))
                                                K.tt("dve", r3(S), r3(S), r3(Eb)[:, :, last:last + 1].bc([64, 4, 64]), ALU.mult)
                                                K.tt("dve", S, S, psm[0:64, 0:256], ALU.add)
                                        if not sample:
                                            for d_ in range(2):
                                                K.dma("sp", nsh[si, l, d_, hg * 4:hg * 4 + 4].re("h k v -> k h v"), r3(Sst[d_]), io_slot)
                                with K.scope():
                                    f64 = Ring(K, 4, [64, 512], F32, "f64g")
                                    wcg = wload(w_in[l, :, O_CG + hg * 256:O_CG + hg * 256 + 256], 8, 256)
                                    for hi in range(4):
                                        for tt_i in range(NT[g]):
                                            tsl = slice(tt_i * 512, (tt_i + 1) * 512)
                                            psg = psum()
                                            proj_fm64(wcg, hi * 64, hT[g][tt_i], psg[0:64, :])
                                            sg = f64.next()
                                            K.act(sg, psg[0:64, :], AF.Sigmoid)
                                            K.tt("dve", sg, oT[:, hi, tsl], sg, ALU.mult)
                                            sq = f64.next()
                                            K.act(sq, sg, AF.Square)
                                            ps = psum()
                                            K.mm(ps[0:64, :], onesf[0:64, 0:64], sq)
                                            rn = f64.next()
                                            K.act(rn, ps[0:64, :], AF.Sqrt, bias=epsc[0:64, 0:1], scale=1.0 / 64)
                                            K.recip(rn, rn)
                                            K.stt("dve", yT[:, hg * 4 + hi, tsl], sg, hnw[:, 0:1], rn, ALU.mult, ALU.mult)
                        branch_merge(l, g, 2, yT, g1[g])
                        chk("C" + g)
                    rms_mod(g, wm2[g], sh2[g])
                    with K.scope():
                        uT = K.tile([128, 4, 512], BF16, "uT")
                        rr = Ring(K, 2, [128, 512], F32, "relu")
                        for hb in range(8):
                            w1 = wload(mlp_w1[l, :, hb * 512:(hb + 1) * 512], 8, 512)
                            w2 = [wload(mlp_w2[l, hb * 512:(hb + 1) * 512, half * 512:(half + 1) * 512], 4, 512) for half in range(2)]
                            for tt_i in range(NT[g]):
                                for hc in range(4):
                                    ps = psum()
                                    for kc in range(8):
                                        K.mm(ps, w1[:, kc, hc * 128:(hc + 1) * 128], hT[g][tt_i][:, kc, :],
                                             start=(kc == 0), stop=(kc == 7), sig=(kc == 7))
                                    r = rr.next()
                                    K.act(r, ps, AF.Relu)
                                    K.tt("pool", uT[:, hc, :], r, r, ALU.mult)
                                for oc in range(8):
                                    ps = psum()
                                    for hc in range(4):
                                        K.mm(ps, w2[oc // 4][:, hc, (oc % 4) * 128:(oc % 4 + 1) * 128], uT[:, hc, :],
                                             start=(hc == 0), stop=(hc == 3), sig=(hc == 3))
                                    xv = xT[g][tt_i][:, oc, :]
                                    K.stt("dve", xv, ps, g2[g][:, oc:oc + 1], xv, ALU.mult, ALU.add)

        chk("layers")
        with K.scope():
            fw = K.tile([128, 8], F32, "fw")
            K.dma("sp", fw, final_norm_w.re("(kc p) -> p kc", p=128), misc_slot)
            zero8 = K.tile([128, 8], F32, "z8")
            K.memset("dve", zero8, 0.0)
            yst = Ring(K, 2, [128, D], F32, "yst")
            hf = {"P": [K.tile([128, 8, 512], F32, "hfP")], "S": [K.tile([128, 8, 512], F32, "hfS%d" % i) for i in range(2)]}
            for g, dst in (("P", y_p), ("S", y_s)):
                for tt_i in range(NT[g]):
                    x = xT[g][tt_i]
                    ps = psum()
                    for kc in range(8):
                        sq = sqring.next()
                        K.act(sq, x[:, kc, :], AF.Square)
                        K.mm(ps, onesb, sq, start=(kc == 0), stop=(kc == 7))
                    rstd = rsring.next()
                    K.act(rstd, ps, AF.Sqrt, bias=epsc[:, 0:1], scale=1.0 / D)
                    K.recip(rstd, rstd)
                    for kc in range(8):
                        K.stt("dve", hf[g][tt_i][:, kc, :], x[:, kc, :], fw[:, kc:kc + 1], rstd, ALU.mult, ALU.mult)
                    for j in range(4):
                        yt = yst.next()
                        for half in range(2):
                            ps = psum()
                            for kc4 in range(4):
                                kc = half * 4 + kc4
                                K.tr(ps[:, kc4 * 128:(kc4 + 1) * 128], hf[g][tt_i][:, kc, j * 128:(j + 1) * 128], identf)
                            K.cp("act" if half else "dve", yt[:, half * 512:(half + 1) * 512], ps)
                        r0 = tt_i * 512 + j * 128
                        K.dma("sp", dst[r0:r0 + 128, :], yt, io_slot)
        K.finish()
    return nc


_CACHE = {}


def kernel(**inp):
    f = lambda k: np.ascontiguousarray(np.asarray(inp[k], dtype=np.float32))
    if "nc" not in _CACHE:
        _CACHE["nc"] = build_program()
    nc = _CACHE["nc"]
    rpb = f("na_rpb")
    kc = np.arange(64)[:, None, None]
    e = np.arange(15)[None, :, None]
    qc = np.arange(64)[None, None, :]
    dc = np.clip(kc - qc + 15, 0, 30) + 0 * e
    rr = (14 - e) + 0 * dc
    rpbx = np.ascontiguousarray(rpb[:, :, rr, dc]).reshape(DEPTH, 8, 64, 15 * 64)
    shared = {
        "norm_w": f("norm_w"), "ada_w": f("ada_w"), "ada_b": f("ada_b"), "w_in": f("w_in"),
        "attn_sink": f("attn_sink"), "delta_conv": f("delta_conv"),
        "delta_a_log": f("delta_a_log").reshape(DEPTH, 16), "delta_dt_bias": f("delta_dt_bias").reshape(DEPTH, 16),
        "delta_norm_w": f("delta_norm_w"), "hgrn_lb": f("hgrn_lb").reshape(DEPTH, 1024),
        "hgrn_norm_w": f("hgrn_norm_w"), "rpbx": rpbx, "w_branch": f("w_branch"), "w_out": f("w_out"),
        "mlp_w1": f("mlp_w1"), "mlp_w2": f("mlp_w2"), "final_norm_w": f("final_norm_w"),
        "consts": make_consts(), "ropetab": make_rope_tab(),
    }
    xp, xs = f("x_prompt"), f("x_sample")
    cak, cav, cnk, cnv = f("cache_attn_k"), f("cache_attn_v"), f("cache_na_k"), f("cache_na_v")
    sd, sh, c, cctx = f("state_delta"), f("state_hgrn"), f("c"), f("c_ctx")
    in_maps = []
    for i in range(NCORES):
        m = dict(shared)
        m["x_p"] = xp[2 * i:2 * i + 2].reshape(NP_, D)
        m["x_s"] = xs[i]
        m["cak"] = cak[i].reshape(DEPTH, PAST, 128)
        m["cav"] = cav[i].reshape(DEPTH, PAST, 128)
        m["cnk"] = cnk[i].reshape(DEPTH, PAST, 512)
        m["cnv"] = cnv[i].reshape(DEPTH, PAST, 512)
        m["sd"] = sd[i]
        m["sh"] = sh[i]
        m["cvec"] = np.stack([cctx, c[i]], 0)
        in_maps.append(m)
    res = run_bass_kernel_spmd(nc, in_maps, core_ids=list(range(NCORES)))
    R = res.results
    cat = lambda k: np.concatenate([np.asarray(r[k]) for r in R], 0)
    y_prompt = cat("y_p").reshape(16, 256, D)
    y_sample = cat("y_s").reshape(8, 1024, D)
    nak = cat("nak").reshape(16, DEPTH, 256, 2, 64)
    nav = cat("nav").reshape(16, DEPTH, 256, 2, 64)
    nnk = cat("nnk").reshape(16, DEPTH, 256, 8, 64)
    nnv = cat("nnv").reshape(16, DEPTH, 256, 8, 64)
    nsd = cat("nsd").reshape(16, DEPTH, 2, 8, 64, 64)
    nsh = cat("nsh").reshape(16, DEPTH, 2, 8, 64, 64)
    return tuple(np.ascontiguousarray(a, dtype=np.float32) for a in (y_prompt, y_sample, nak, nav, nnk, nnv, nsd, nsh))
```

```python
import math
from contextlib import ExitStack, contextmanager
import numpy as np
import ml_dtypes
import concourse.bass as bass
import concourse.mybir as mybir
from concourse.bass_utils import run_bass_kernel_spmd

F32 = mybir.dt.float32
BF16 = mybir.dt.bfloat16
ALU = mybir.AluOpType
AF = mybir.ActivationFunctionType

D = 1024
DEPTH = 4
NCORES = 8
NP_ = 512
NS_ = 1024
PAST = 512
EPS = 1e-6
NEGM = -30000.0
O_AQ, O_AK, O_AV = 0, 512, 640
O_BQ, O_BK, O_BV, O_BZ, O_BA, O_BB = 768, 1280, 1792, 2304, 2816, 2832
O_CQ, O_CF, O_CI, O_CG = 2848, 3360, 4384, 4896
O_DQ, O_DK, O_DV, O_G = 5408, 5920, 6432, 6944
N_IN = 11040

C_ID, C_ONE = 0, 128
C_U = (256, 320)
C_W2 = (384, 448)
C_NM = (512, 576)
C_NMT = (640, 704)
C_SL = (768, 832)
C_MT = (896, 960)
C_ROPE, C_COLM, C_MPREV, C_MNEXT = 1024, 1088, 1152, 1280
C_ZERO = 1408
NCST = 1472
MID = (32, 31)


def make_consts():
    c = np.zeros((128, NCST), np.float32)
    c[:, C_ID:C_ID + 128] = np.eye(128)
    c[:, C_ONE:C_ONE + 128] = 1.0
    t = np.arange(64)
    for d in range(2):
        if d == 0:
            U = (t[:, None] <= t[None, :]).astype(np.float32)
        else:
            U = (t[:, None] >= t[None, :]).astype(np.float32)
        c[:64, C_U[d]:C_U[d] + 64] = U
        c[:64, C_W2[d]:C_W2[d] + 64] = 1.0 - U
        incl = U.T
        c[:64, C_NM[d]:C_NM[d] + 64] = np.where(incl > 0, 0.0, NEGM)
        c[:64, C_NMT[d]:C_NMT[d] + 64] = np.where(incl.T > 0, 0.0, NEGM)
        c[:64, C_SL[d]:C_SL[d] + 64] = incl - np.eye(64)
        c[:64, C_MT[d]:C_MT[d] + 64] = incl.T
    P = np.zeros((64, 64), np.float32)
    for half in (0, 32):
        for i in range(16):
            P[half + i, half + i + 16] = -1.0
            P[half + 16 + i, half + i] = 1.0
    c[:64, C_ROPE:C_ROPE + 64] = P.T
    qc = np.arange(64)
    ws = np.clip(qc - 8, 0, 48)
    kc = np.arange(64)
    c[:64, C_COLM:C_COLM + 64] = ((kc[:, None] >= ws[None, :]) & (kc[:, None] < ws[None, :] + 16)).astype(np.float32)
    k = np.arange(128)
    c[:, C_MPREV:C_MPREV + 128] = (k[:, None] >= k[None, :]).astype(np.float32)
    c[:, C_MNEXT:C_MNEXT + 128] = (k[:, None] <= k[None, :]).astype(np.float32)
    return c


def make_rope_tab():
    tt = np.arange(NS_)
    inv = (10000.0 ** (-np.arange(16, dtype=np.float32) / 16)).astype(np.float32)
    tab = np.zeros((64, 2, NS_), np.float32)
    for half, pos in ((0, tt // 64), (32, tt % 64)):
        ang = pos.astype(np.float32)[None, :] * inv[:, None]
        cs, sn = np.cos(ang).astype(np.float32), np.sin(ang).astype(np.float32)
        tab[half:half + 16, 0], tab[half + 16:half + 32, 0] = cs, cs
        tab[half:half + 16, 1], tab[half + 16:half + 32, 1] = sn, sn
    return tab


class V:
    __slots__ = ("ap", "toks")

    def __init__(self, ap, toks=()):
        self.ap = ap
        self.toks = toks

    def __getitem__(self, idx):
        return V(self.ap[idx], self.toks)

    def bc(self, shape):
        return V(self.ap.broadcast_to(list(shape)), self.toks)

    def un(self, axis):
        return V(self.ap.unsqueeze(axis), self.toks)

    def re(self, pat, **kw):
        return V(self.ap.rearrange(pat, **kw), self.toks)

    def bitcast(self, dt):
        return V(self.ap.bitcast(dt), self.toks)


class Slot:
    def __init__(self, key, sem):
        self.key, self.sem, self.cnt = key, sem, 0


class KB:
    def __init__(self, nc, es):
        self.nc = nc
        self.es = es
        self.eng = {"pe": nc.tensor, "act": nc.scalar, "dve": nc.vector, "pool": nc.gpsimd, "sp": nc.sync}
        self.semh = {}
        self.cnt = {}
        for e in ("pe", "act", "dve", "pool"):
            self.semh[e] = es.enter_context(nc.semaphore("s_" + e))
            self.cnt[e] = 0
        self.seen = {e: {} for e in self.eng}
        self.lastw = {}
        self.readers = {}
        self.slots = []
        self.ntile = 0
        self.scopes = []

    def tile(self, shape, dt, name=None):
        self.ntile += 1
        nm = "%s_%d" % (name or "t", self.ntile)
        st = self.scopes[-1] if self.scopes else self.es
        h = st.enter_context(self.nc.sbuf_tensor(nm, list(shape), dt))
        v = V(h.ap(), (nm,))
        self.memset("pool", v, 0.0)
        return v

    def slot(self):
        s = Slot("d%d" % len(self.slots), self.es.enter_context(self.nc.semaphore("sd%d" % len(self.slots))))
        self.semh[s.key] = s.sem
        self.slots.append(s)
        return s

    @contextmanager
    def scope(self):
        st = ExitStack()
        self.scopes.append(st)
        try:
            yield
        finally:
            self.barrier()
            self.scopes.pop()
            st.close()

    def barrier(self):
        marks = [(e, c) for e, c in self.cnt.items() if c > 0] + [(s.key, s.cnt) for s in self.slots if s.cnt > 0]
        for e in ("pe", "act", "dve", "pool", "sp"):
            self._wait(e, marks, True)

    def _wait(self, e, marks, full=False):
        need = {}
        for (k, v) in marks:
            if k == e and e == "pe":
                continue
            if need.get(k, 0) < v:
                need[k] = v
        for k, v in need.items():
            if self.seen[e].get(k, 0) < v:
                self.eng[e].wait_ge(self.semh[k], v)
                self.seen[e][k] = v

    def _deps(self, reads, writes):
        marks = []
        for t in reads:
            if t in self.lastw:
                marks.append(self.lastw[t])
        for t in writes:
            if t in self.lastw:
                marks.append(self.lastw[t])
            marks.extend(self.readers.get(t, {}).items())
        return marks

    def _record(self, mark, reads, writes):
        for t in reads:
            r = self.readers.setdefault(t, {})
            if r.get(mark[0], 0) < mark[1]:
                r[mark[0]] = mark[1]
        for t in writes:
            self.lastw[t] = mark
            self.readers[t] = {}

    def op(self, e, fn, ins, outs, sig=True):
        reads = [t for v in ins for t in v.toks]
        writes = [t for v in outs for t in v.toks]
        writes = writes + [t for t in reads if t.startswith("ps") and t not in writes]
        self._wait(e, self._deps(reads, writes))
        inst = fn()
        if sig:
            self.cnt[e] += 1
            inst.then_inc(self.semh[e], 1)
            mark = (e, self.cnt[e])
        else:
            mark = (e, self.cnt[e] + 1)
        self._record(mark, reads, writes)

    def dma(self, q, out, in_, slot, **kw):
        reads, writes = list(in_.toks), list(out.toks)
        if isinstance(slot, list):
            slot.append(slot.pop(0))
            slot = slot[-1]
        marks = self._deps(reads, writes)
        if slot.cnt > 0:
            marks.append((slot.key, slot.cnt))
        self._wait(q, marks)
        inst = self.eng[q].dma_start(out=out.ap, in_=in_.ap, **kw)
        slot.cnt += 16
        inst.then_inc(slot.sem, 16)
        self._record((slot.key, slot.cnt), reads, writes)

    def mm(self, out, lhsT, rhs, start=True, stop=True, sig=True):
        self.op("pe", lambda: self.nc.tensor.matmul(out.ap, lhsT.ap, rhs.ap, start=start, stop=stop),
                [lhsT, rhs] + ([] if start else [out]), [out], sig)

    def tr(self, out, in_, ident, sig=True):
        self.mm(out, in_, ident, sig=sig)

    def act(self, out, in_, func, bias=0.0, scale=1.0):
        ins = [in_] + [x for x in (bias, scale) if isinstance(x, V)]
        b = bias.ap if isinstance(bias, V) else bias
        s = scale.ap if isinstance(scale, V) else scale
        self.op("act", lambda: self.nc.scalar.activation(out.ap, in_.ap, func, bias=b, scale=s), ins, [out])

    def _ve(self, e):
        return self.nc.vector if e == "dve" else self.nc.gpsimd

    def tt(self, e, out, in0, in1, op):
        self.op(e, lambda: self._ve(e).tensor_tensor(out.ap, in0.ap, in1.ap, op), [in0, in1], [out])

    def ts(self, e, out, in0, s1, s2=None, op0=ALU.mult, op1=None):
        ins = [in0] + [x for x in (s1, s2) if isinstance(x, V)]
        a = s1.ap if isinstance(s1, V) else s1
        b = s2.ap if isinstance(s2, V) else s2
        if op1 is None:
            self.op(e, lambda: self._ve(e).tensor_scalar(out.ap, in0.ap, a, None, op0), ins, [out])
        else:
            self.op(e, lambda: self._ve(e).tensor_scalar(out.ap, in0.ap, a, b, op0, op1), ins, [out])

    def stt(self, e, out, in0, sc, in1, op0, op1):
        ins = [in0, in1] + ([sc] if isinstance(sc, V) else [])
        a = sc.ap if isinstance(sc, V) else sc
        self.op(e, lambda: self._ve(e).scalar_tensor_tensor(out.ap, in0.ap, a, in1.ap, op0, op1), ins, [out])

    def cp(self, e, out, in_):
        if e == "act":
            self.op("act", lambda: self.nc.scalar.copy(out.ap, in_.ap), [in_], [out])
        else:
            self.op(e, lambda: self._ve(e).tensor_copy(out.ap, in_.ap), [in_], [out])

    def memset(self, e, out, val):
        self.op(e, lambda: self._ve(e).memset(out.ap, val), [], [out])

    def recip(self, out, in_):
        self.op("dve", lambda: self.nc.vector.reciprocal(out.ap, in_.ap), [in_], [out])

    def finish(self):
        marks = [(e, c) for e, c in self.cnt.items() if c > 0] + [(s.key, s.cnt) for s in self.slots if s.cnt > 0]
        self._wait("sp", marks, True)


class StopBuild(Exception):
    pass


class Ring:
    def __init__(self, K, n, shape, dt, name="r"):
        self.t = [K.tile(shape, dt, name) for _ in range(n)]
        self.i = 0

    def next(self):
        v = self.t[self.i % len(self.t)]
        self.i += 1
        return v


def build_program(nlayers=DEPTH, stage=99):
    holder = {}
    try:
        return _build_program(nlayers, stage, holder)
    except StopBuild:
        holder["K"].finish()
        return holder["nc"]


def _build_program(nlayers, stage, holder):
    nc = bass.Bass("TRN2", target_bir_lowering=False)

    def din(name, shape, dt=F32):
        return V(nc.dram_tensor(name, list(shape), dt, kind="ExternalInput").ap())

    def dout(name, shape, dt=F32):
        return V(nc.dram_tensor(name, list(shape), dt, kind="ExternalOutput").ap())

    x_p = din("x_p", [NP_, D])
    x_s = din("x_s", [NS_, D])
    cak = din("cak", [DEPTH, PAST, 128])
    cav = din("cav", [DEPTH, PAST, 128])
    cnk = din("cnk", [DEPTH, PAST, 512])
    cnv = din("cnv", [DEPTH, PAST, 512])
    sd_in = din("sd", [DEPTH, 2, 8, 64, 64])
    sh_in = din("sh", [DEPTH, 2, 8, 64, 64])
    cvec = din("cvec", [2, D])
    norm_w = din("norm_w", [DEPTH, 2, D])
    ada_w = din("ada_w", [DEPTH, D, 6 * D])
    ada_b = din("ada_b", [DEPTH, 6 * D])
    w_in = din("w_in", [DEPTH, D, N_IN])
    attn_sink = din("attn_sink", [DEPTH, 8])
    delta_conv = din("delta_conv", [DEPTH, 5, 1536])
    delta_a_log = din("delta_a_log", [DEPTH, 16])
    delta_dt_bias = din("delta_dt_bias", [DEPTH, 16])
    delta_norm_w = din("delta_norm_w", [DEPTH, 64])
    hgrn_lb = din("hgrn_lb", [DEPTH, 1024])
    hgrn_norm_w = din("hgrn_norm_w", [DEPTH, 64])
    rpbx = din("rpbx", [DEPTH, 8, 64, 15 * 64])
    w_branch = din("w_branch", [DEPTH, 4, 512, D])
    w_out = din("w_out", [DEPTH, D, D])
    mlp_w1 = din("mlp_w1", [DEPTH, D, 4 * D])
    mlp_w2 = din("mlp_w2", [DEPTH, 4 * D, D])
    final_norm_w = din("final_norm_w", [D])
    consts = din("consts", [128, NCST])
    ropetab = din("ropetab", [64, 2, NS_])

    y_p = dout("y_p", [NP_, D])
    y_s = dout("y_s", [NS_, D])
    nak = dout("nak", [2, DEPTH, 256, 128])
    nav = dout("nav", [2, DEPTH, 256, 128])
    nnk = dout("nnk", [2, DEPTH, 256, 512])
    nnv = dout("nnv", [2, DEPTH, 256, 512])
    nsd = dout("nsd", [2, DEPTH, 2, 8, 64, 64])
    nsh = dout("nsh", [2, DEPTH, 2, 8, 64, 64])

    es = ExitStack()
    with es:
        nc_np = es.enter_context(nc.allow_non_contiguous_dma(reason="small param layouts"))
        K = KB(nc, es)
        holder["K"], holder["nc"] = K, nc
        chk_i = [0]

        def chk(name):
            chk_i[0] += 1
            holder.setdefault("chk", []).append((chk_i[0], name, nc.n_instructions()))
            if chk_i[0] == stage - 100:
                print("STOP at checkpoint", chk_i[0], name)
                raise StopBuild()
        banks = []
        for i in range(8):
            h = es.enter_context(nc.psum_tensor("ps%d" % i, [128, 512], F32))
            banks.append(V(h.ap(), ("ps%d" % i,)))
        bank_i = [0]

        def psum():
            v = banks[bank_i[0] % 6]
            bank_i[0] += 1
            return v

        cst = K.tile([128, NCST], F32, "cst")
        cstb = K.tile([128, NCST], BF16, "cstb")
        s_c = K.slot()
        K.dma("sp", cst, consts, s_c)
        K.cp("dve", cstb, cst)
        if stage == 1:
            K.finish()
            return nc
        identf = cst[:, C_ID:C_ID + 128]
        identb = cstb[:, C_ID:C_ID + 128]
        onesb = cstb[:, C_ONE:C_ONE + 128]
        onesf = cst[:, C_ONE:C_ONE + 128]

        xT = {"P": [K.tile([128, 8, 512], F32, "xP")], "S": [K.tile([128, 8, 512], F32, "xS%d" % i) for i in range(2)]}
        _hS = [K.tile([128, 8, 512], BF16, "hS%d" % i) for i in range(2)]
        hT = {"P": [_hS[0]], "S": _hS}
        NT = {"P": 1, "S": 2}
        TT = {"P": NP_, "S": NS_}
        SEQS = {"P": [(0, 256), (256, 256)], "S": [(0, 1024)]}

        WN = 4
        wbuf = [K.tile([128, 4096], BF16, "w") for _ in range(WN)]
        wslot = [K.slot() for _ in range(WN)]
        w_i = [0]

        def wload(src2d, kc, cols, pat="(kc p) c -> p kc c", q="pool"):
            i = w_i[0] % WN
            w_i[0] += 1
            assert kc * cols <= 4096
            dst = wbuf[i][:, 0:kc * cols].re("p (k c) -> p k c", c=cols)
            rows = src2d.ap.shape[0] // kc
            K.dma(q, dst[0:rows], src2d.re(pat, kc=kc), wslot[i])
            return dst[0:rows]

        stage_slot = [K.slot() for _ in range(4)]
        misc_slot = [K.slot() for _ in range(3)]
        io_slot = [K.slot() for _ in range(6)]

        csf = K.tile([128, 8, 2], F32, "csf")
        csT = K.tile([128, 8, 2], BF16, "csT")
        for gi_ in range(2):
            K.dma("sp", csf[:, :, gi_], cvec[gi_].re("(kc p) -> p kc", p=128), misc_slot)
        K.act(csT, csf, AF.Silu)

        if stage == 2:
            K.finish()
            return nc
        with K.scope():
            xin = Ring(K, 2, [128, D], F32, "xin")
            xs_ = [K.slot(), K.slot()] if False else [stage_slot[0], stage_slot[1]]
            n = 0
            for gname, src in (("P", x_p), ("S", x_s)):
                for tile_i in range(TT[gname] // 128):
                    xt = xin.next()
                    K.dma("sp", xt, src[tile_i * 128:(tile_i + 1) * 128, :], xs_[n % 2])
                    n += 1
                    for half in range(2):
                        ps = psum()
                        for kc4 in range(4):
                            kc = half * 4 + kc4
                            K.tr(ps[:, kc4 * 128:(kc4 + 1) * 128], xt[:, kc * 128:(kc + 1) * 128], identf)
                        tt_i, off = divmod(tile_i * 128, 512)
                        K.cp("act" if half else "dve",
                             xT[gname][tt_i][:, half * 4:half * 4 + 4, off:off + 128],
                             ps.re("p (k c) -> p k c", c=128))

        if stage == 3:
            K.finish()
            return nc
        def rms_mod(g, wm, sh):
            for tt_i in range(NT[g]):
                x = xT[g][tt_i]
                ps = psum()
                for kc in range(8):
                    sq = sqring.next()
                    K.act(sq, x[:, kc, :], AF.Square)
                    K.mm(ps, onesb, sq, start=(kc == 0), stop=(kc == 7))
                rstd = rsring.next()
                K.act(rstd, ps, AF.Sqrt, bias=epsc[:, 0:1], scale=1.0 / D)
                K.recip(rstd, rstd)
                for kc in range(8):
                    tmp = tmpring.next()
                    K.tt("dve" if kc % 2 else "pool", tmp, x[:, kc, :], rstd, ALU.mult)
                    K.act(hT[g][tt_i][:, kc, :], tmp, AF.Identity, bias=sh[:, kc:kc + 1], scale=wm[:, kc:kc + 1])

        epsc = K.tile([128, 1], F32, "eps")
        K.memset("dve", epsc, EPS)
        onec = K.tile([128, 1], F32, "onec")
        K.memset("dve", onec, 1.0)
        sqring = Ring(K, 2, [128, 512], BF16, "sq")
        rsring = Ring(K, 2, [128, 512], F32, "rstd")
        tmpring = Ring(K, 3, [128, 512], F32, "tmp")
        pring = Ring(K, 3, [128, 512], BF16, "pT")

        def proj_fm64(w, c0, rhs_h, out_ps):
            for kc in range(8):
                K.mm(out_ps, w[:, kc, c0:c0 + 64], rhs_h[:, kc, :], start=(kc == 0), stop=(kc == 7), sig=(kc == 7))

        def attention(qv, N, keytiles, outv, esink=None, b3=None):
            def r(v):
                return v if b3 is None else v.re("p (a b) -> p a b", b=b3)
            psn, psd = banks[6], banks[7]
            nkt = len(keytiles)
            for i, (kt, vv, mask, rng, nk) in enumerate(keytiles):
                c0, c1 = rng if rng is not None else (0, N)
                pss = psum()
                qs = qv if rng is None else qv[:, c0:c1]
                K.mm(r(pss[0:nk, c0:c1]), kt, qs)
                pT = pring.next()
                K.act(pT[0:nk, c0:c1], pss[0:nk, c0:c1], AF.Exp, scale=0.125)
                if mask is not None:
                    K.tt("pool", r(pT[0:nk, c0:c1]), r(pT[0:nk, c0:c1]), mask, ALU.mult)
                K.mm(psn[0:64, c0:c1], vv, pT[0:nk, c0:c1], start=(i == 0), stop=(i == nkt - 1))
                K.mm(psd[0:64, c0:c1], onesb[0:nk, 0:64], pT[0:nk, c0:c1], start=(i == 0), stop=(i == nkt - 1))
            den = tmpring.next()
            if esink is not None:
                K.tt("dve", r(den[0:64, 0:N]), r(psd[0:64, 0:N]), esink, ALU.add)
                K.recip(den[0:64, 0:N], den[0:64, 0:N])
            else:
                K.recip(den[0:64, 0:N], psd[0:64, 0:N])
            K.tt("dve", outv, r(psn[0:64, 0:N]), r(den[0:64, 0:N]), ALU.mult)

        def branch_merge(l, g, k, yT, gate1):
            with K.scope():
                mk = K.tile([128, 8, 512], BF16, "mk")
                gt = Ring(K, 2, [128, 512], F32, "gt")
                for tt_i in range(NT[g]):
                    wb = [wload(w_branch[l, k, :, half * 512:(half + 1) * 512], 8, 512) for half in range(2)]
                    wg = [wload(w_in[l, :, O_G + k * D + half * 512:O_G + k * D + (half + 1) * 512], 8, 512) for half in range(2)]
                    for oc in range(8):
                        psy, psg = psum(), psum()
                        for h in range(8):
                            K.mm(psy, wb[oc // 4][:, h, (oc % 4) * 128:(oc % 4 + 1) * 128], yT[:, h, tt_i * 512:(tt_i + 1) * 512],
                                 start=(h == 0), stop=(h == 7), sig=(h == 7))
                        for kc in range(8):
                            K.mm(psg, wg[oc // 4][:, kc, (oc % 4) * 128:(oc % 4 + 1) * 128], hT[g][tt_i][:, kc, :],
                                 start=(kc == 0), stop=(kc == 7), sig=(kc == 7))
                        gg = gt.next()
                        K.act(gg, psg, AF.Sigmoid)
                        K.tt("dve", mk[:, oc, :], psy, gg, ALU.mult)
                    wo = [wload(w_out[l, :, half * 512:(half + 1) * 512], 8, 512) for half in range(2)]
                    for oc2 in range(8):
                        ps = psum()
                        for oc in range(8):
                            K.mm(ps, wo[oc2 // 4][:, oc, (oc2 % 4) * 128:(oc2 % 4 + 1) * 128], mk[:, oc, :],
                                 start=(oc == 0), stop=(oc == 7), sig=(oc == 7))
                        xv = xTn[g][tt_i][:, oc2, :]
                        K.stt("dve", xv, ps, gate1[:, oc2:oc2 + 1], xv, ALU.mult, ALU.add)

        xTn = xT

        for l in range(nlayers):
            with K.scope():
                nw = K.tile([128, 2, 8], F32, "nw")
                for j_ in range(2):
                    K.dma("sp", nw[:, j_, :], norm_w[l, j_].re("(kc p) -> p kc", p=128), misc_slot)
                adab = K.tile([128, 48], F32, "adab")
                K.dma("sp", adab, ada_b[l].re("(c p) -> p c", p=128), misc_slot)
                esk = K.tile([64, 8], F32, "esk")
                K.dma("sp", esk, V(attn_sink.ap[l].partition_broadcast(64)), misc_slot)
                K.act(esk, esk, AF.Exp)
                convw = K.tile([64, 5, 24], F32, "convw")
                for j_ in range(5):
                    K.dma("sp", convw[:, j_, :], delta_conv[l, j_].re("(n d) -> d n", d=64), misc_slot)
                nea = K.tile([64, 16], F32, "nea")
                K.dma("sp", nea, V(delta_a_log.ap[l].partition_broadcast(64)), misc_slot)
                K.act(nea, nea, AF.Exp)
                K.ts("dve", nea, nea, -1.0)
                dtb = K.tile([64, 16], F32, "dtb")
                K.dma("sp", dtb, V(delta_dt_bias.ap[l].partition_broadcast(64)), misc_slot)
                dnw = K.tile([64, 1], F32, "dnw")
                K.dma("sp", dnw, delta_norm_w[l].re("(d o) -> d o", o=1), misc_slot)
                hnw = K.tile([64, 1], F32, "hnw")
                K.dma("sp", hnw, hgrn_norm_w[l].re("(d o) -> d o", o=1), misc_slot)
                def compute_lb(dst, rowbc):
                    shp = [64, 4, 1024] if rowbc else [64, 4, 16]
                    with K.scope():
                        raw = K.tile(shp, F32, "lbraw")
                        for ll in range(4):
                            if not rowbc:
                                K.dma("sp", raw[:, ll, :], hgrn_lb[ll].re("(j d) -> d j", d=64), misc_slot)
                            else:
                                K.dma("sp", raw[:, ll, :], V(hgrn_lb.ap[ll].partition_broadcast(64)), misc_slot)
                        K.act(raw, raw, AF.Exp)
                        tot = K.tile(shp[0:1] + shp[2:], F32, "lbtot")
                        K.tt("dve", tot, raw[:, 0, :], raw[:, 1, :], ALU.add)
                        K.tt("dve", tot, tot, raw[:, 2, :], ALU.add)
                        K.tt("dve", tot, tot, raw[:, 3, :], ALU.add)
                        K.recip(tot, tot)
                        if l == 0:
                            K.memset("dve", dst, 0.0)
                        else:
                            acc = K.tile(shp[0:1] + shp[2:], F32, "lbacc")
                            K.cp("dve", acc, raw[:, 1, :])
                            for ll in range(2, l + 1):
                                K.tt("dve", acc, acc, raw[:, ll, :], ALU.add)
                            K.tt("dve", dst, acc, tot, ALU.mult)

                lbT = K.tile([64, 16], F32, "lbT")
                compute_lb(lbT, False)
                omlbT = K.tile([64, 16], F32, "omlbT")
                K.ts("dve", omlbT, lbT, -1.0, 1.0, ALU.mult, ALU.add)
                chk("params")
                mod = K.tile([128, 48, 2], F32, "mod")
                for cb in range(12):
                    w = wload(ada_w[l, :, cb * 512:(cb + 1) * 512], 8, 512)
                    ps = psum()
                    for cc in range(4):
                        for kc in range(8):
                            K.mm(ps[:, cc * 2:cc * 2 + 2], w[:, kc, cc * 128:(cc + 1) * 128], csT[:, kc, :],
                                 start=(kc == 0), stop=(kc == 7), sig=(kc == 7 and cc == 3))
                    K.tt("dve", mod[:, cb * 4:cb * 4 + 4, :], ps[:, 0:8].re("p (c g) -> p c g", g=2),
                         adab[:, cb * 4:cb * 4 + 4].un(2).bc([128, 4, 2]), ALU.add)
                wm1, wm2, sh1, sh2, g1, g2 = {}, {}, {}, {}, {}, {}
                for gi, g in enumerate(("P", "S")):
                    for (dst, sc_i, nwi) in ((wm1, 1, 0), (wm2, 4, 1)):
                        t = K.tile([128, 8], F32, "wm")
                        K.stt("dve", t, mod[:, sc_i * 8:sc_i * 8 + 8, gi], 1.0, nw[:, nwi, :], ALU.add, ALU.mult)
                        dst[g] = t
                    sh1[g] = mod[:, 0:8, gi]
                    g1[g] = mod[:, 16:24, gi]
                    sh2[g] = mod[:, 24:32, gi]
                    g2[g] = mod[:, 40:48, gi]

                chk("adaln")
                for g in ("P", "S"):
                    T = TT[g]
                    sample = (g == "S")
                    rms_mod(g, wm1[g], sh1[g])
                    chk("rms" + g)
                    with K.scope():
                        qT = K.tile([64, 8, T], BF16, "aq")
                        kT = K.tile([64, 2, T], BF16, "ak")
                        vtm = K.tile([128, T // 128, 128], BF16, "av")
                        yT = K.tile([64, 8, T], BF16, "ya")
                        wq = wload(w_in[l, :, O_AQ:O_AQ + 512], 8, 512)
                        wkv = wload(w_in[l, :, O_AK:O_AK + 256], 8, 256)
                        stg = Ring(K, 2, [128, 256], F32, "stg")
                        if sample:
                            rtab = K.tile([64, 2, NS_], F32, "rtab")
                            K.dma("sp", rtab, ropetab, misc_slot)
                            xbr = Ring(K, 2, [64, 512], BF16, "xb")
                        for tt_i in range(NT[g]):
                            tsl = slice(tt_i * 512, (tt_i + 1) * 512)
                            for hh in range(10):
                                ps = psum()
                                if hh < 8:
                                    proj_fm64(wq, hh * 64, hT[g][tt_i], ps[0:64, :])
                                    dst = qT[:, hh, tsl]
                                else:
                                    proj_fm64(wkv, (hh - 8) * 64, hT[g][tt_i], ps[0:64, :])
                                    dst = kT[:, hh - 8, tsl]
                                if not sample:
                                    K.cp("act", dst, ps[0:64, :])
                                else:
                                    xb = xbr.next()
                                    K.cp("act", xb, ps[0:64, :])
                                    ps2 = psum()
                                    K.mm(ps2[0:64, :], cstb[0:64, C_ROPE:C_ROPE + 64], xb)
                                    t1, t2 = tmpring.next(), tmpring.next()
                                    K.tt("dve", t1[0:64, :], ps2[0:64, :], rtab[:, 1, tsl], ALU.mult)
                                    K.tt("pool", t2[0:64, :], xb, rtab[:, 0, tsl], ALU.mult)
                                    K.tt("dve", dst, t1[0:64, :], t2[0:64, :], ALU.add)
                            chk("Afm" + g)
                            for j in range(4):
                                ps = psum()
                                for kc in range(8):
                                    K.mm(ps[:, 0:256], hT[g][tt_i][:, kc, j * 128:(j + 1) * 128], wkv[:, kc, :],
                                         start=(kc == 0), stop=(kc == 7), sig=(kc == 7))
                                tile_i = tt_i * 4 + j
                                K.cp("dve", vtm[:, tile_i, :], ps[:, 128:256])
                                chk("Atm_mm" + g)
                                if not sample:
                                    st = stg.next()
                                    K.cp("dve", st, ps[:, 0:256])
                                    chk("Atm_cp" + g)
                                    s_, r0 = divmod(tile_i * 128, 256)
                                    K.dma("sp", nak[s_, l, r0:r0 + 128, :], st[:, 0:128], io_slot)
                                    K.dma("sp", nav[s_, l, r0:r0 + 128, :], st[:, 128:256], io_slot)
                        chk("Aproj" + g)
                        if sample:
                            ctm = K.tile([128, 4, 128], BF16, "ctk")
                            cvv = K.tile([128, 4, 128], BF16, "ctv")
                            ckT = K.tile([64, 2, PAST], BF16, "ckT")
                            K.dma("pool", ctm, cak[l].re("(j p) c -> p j c", p=128), stage_slot[2])
                            K.dma("pool", cvv, cav[l].re("(j p) c -> p j c", p=128), stage_slot[3])
                            for kv in range(2):
                                ps = psum()
                                psb = ps
                                for j in range(4):
                                    K.tr(psb[0:64, j * 128:(j + 1) * 128], ctm[:, j, kv * 64:(kv + 1) * 64], identb)
                                K.cp("dve", ckT[:, kv, :], psb[0:64, 0:512])
                        for (t0, L) in SEQS[g]:
                            nblk = L // 128
                            for kv in range(2):
                                for bi in range(nblk):
                                    kts = []
                                    if sample:
                                        for j in range(4):
                                            kts.append((ckT[:, kv, j * 128:(j + 1) * 128], cvv[:, j, kv * 64:(kv + 1) * 64], None, None, 128))
                                        blks = [(bi - 1, C_MPREV), (bi, None), (bi + 1, C_MNEXT)]
                                    else:
                                        blks = [(b, None) for b in range(nblk)]
                                    for (b, mc) in blks:
                                        if b < 0 or b >= nblk:
                                            continue
                                        k0 = t0 + b * 128
                                        m = None if mc is None else cstb[:, mc:mc + 128].un(1).bc([128, 4, 128])
                                        kts.append((kT[:, kv, k0:k0 + 128], vtm[:, k0 // 128, kv * 64:(kv + 1) * 64], m, None, 128))
                                    q0 = t0 + bi * 128
                                    attention(qT[:, 4 * kv:4 * kv + 4, q0:q0 + 128], 512, kts,
                                              yT[:, 4 * kv:4 * kv + 4, q0:q0 + 128],
                                              esk[:, 4 * kv:4 * kv + 4].un(2).bc([64, 4, 128]), b3=128)
                        chk("Aattn" + g)
                        branch_merge(l, g, 0, yT, g1[g])
                        chk("Amerge" + g)
                    with K.scope():
                        yT = K.tile([64, 8, T], BF16, "yd")
                        wdq = wload(w_in[l, :, O_DQ:O_DQ + 512], 8, 512)
                        wdk = wload(w_in[l, :, O_DK:O_DK + 512], 8, 512)
                        wdv = wload(w_in[l, :, O_DV:O_DV + 512], 8, 512)
                        chk("Dw" + g)
                        if sample:
                            vrow = K.tile([64, 16, 512], BF16, "vrow")
                            for r_ in range(16):
                                ps = psum()
                                for kc in range(8):
                                    K.mm(ps[0:64, :], hT[g][r_ // 8][:, kc, (r_ % 8) * 64:(r_ % 8 + 1) * 64], wdv[:, kc, :],
                                         start=(kc == 0), stop=(kc == 7), sig=(kc == 7))
                                K.cp("act", vrow[:, r_, :], ps[0:64, :])
                            cktm = K.tile([128, 4, 512], BF16, "cktm")
                            cvtm = K.tile([128, 4, 512], BF16, "cvtm")
                            K.dma("pool", cktm, cnk[l].re("(j p) c -> p j c", p=128), stage_slot[2])
                            K.dma("pool", cvtm, cnv[l].re("(j p) c -> p j c", p=128), stage_slot[3])
                            ckr = Ring(K, 2, [64, 512], BF16, "ckr")
                            Er = Ring(K, 2, [64, 960], BF16, "Er")
                            Ef = Ring(K, 2, [64, 960], F32, "Ef")
                            colm = cst[0:64, C_COLM:C_COLM + 64]
                        else:
                            stg = Ring(K, 2, [128, 512], F32, "stgd")
                            vtm = K.tile([128, T // 128, 512], BF16, "dvtm")
                            for tile_i in range(T // 128):
                                tt_i, j = divmod(tile_i, 4)
                                for which, w in ((0, wdk), (1, wdv)):
                                    ps = psum()
                                    for kc in range(8):
                                        K.mm(ps, hT[g][tt_i][:, kc, j * 128:(j + 1) * 128], w[:, kc, :],
                                             start=(kc == 0), stop=(kc == 7), sig=(kc == 7))
                                    chk("Dmm" + g)
                                    st = stg.next()
                                    K.cp("act", st, ps)
                                    chk("Dcp" + g)
                                    s_, r0 = divmod(tile_i * 128, 256)
                                    K.dma("sp", (nnk if which == 0 else nnv)[s_, l, r0:r0 + 128, :], st, io_slot)
                                    chk("Ddma" + g)
                                    if which == 1:
                                        K.cp("dve", vtm[:, tile_i, :], ps)
                        chk("Dtm" + g)
                        qh = Ring(K, 2, [64, T], BF16, "dq")
                        kh = Ring(K, 2, [64, T], BF16, "dk")
                        for h in range(8):
                            q_h, k_h = qh.next(), kh.next()
                            for tt_i in range(NT[g]):
                                for (w, dst) in ((wdq, q_h), (wdk, k_h)):
                                    ps = psum()
                                    proj_fm64(w, h * 64, hT[g][tt_i], ps[0:64, :])
                                    K.cp("act", dst[:, tt_i * 512:(tt_i + 1) * 512], ps[0:64, :])
                            chk("Dproj%d" % h + g)
                            if sample:
                                ck = ckr.next()
                                ps = psum()
                                psb = ps
                                for j in range(4):
                                    K.tr(psb[0:64, j * 128:(j + 1) * 128], cktm[:, j, h * 64:(h + 1) * 64], identb)
                                K.cp("dve", ck, psb[0:64, 0:512])
                                ef = Ef.next()
                                K.dma("sp", ef, rpbx[l, h], stage_slot[h % 2])
                                K.act(ef, ef, AF.Exp)
                                E = Er.next()
                                K.tt("dve", E.re("p (e c) -> p e c", c=64), ef.re("p (e c) -> p e c", c=64),
                                     colm.un(1).bc([64, 15, 64]), ALU.mult)
                                for hf_ in range(2):
                                    kts = [(ck[:, j * 128:(j + 1) * 128], cvtm[:, j, h * 64:(h + 1) * 64], None, None, 128) for j in range(4)]
                                    for kr in range(16):
                                        qs = [r_ for r_ in range(8 * hf_, 8 * hf_ + 8)
                                              if min(max(r_ - 4, 0), 8) <= kr < min(max(r_ - 4, 0), 8) + 8]
                                        if not qs:
                                            continue
                                        c0, c1 = (qs[0] - 8 * hf_) * 64, (qs[-1] + 1 - 8 * hf_) * 64
                                        e0 = qs[0] - kr + 7
                                        kts.append((k_h[:, kr * 64:(kr + 1) * 64], vrow[:, kr, h * 64:(h + 1) * 64],
                                                    E[:, e0 * 64:(e0 + len(qs)) * 64], (c0, c1), 64))
                                    attention(q_h[:, hf_ * 512:(hf_ + 1) * 512], 512, kts, yT[:, h, hf_ * 512:(hf_ + 1) * 512])
                            else:
                                for (t0, L) in SEQS[g]:
                                    kts = [(k_h[:, t0 + j * 128:t0 + (j + 1) * 128], vtm[:, (t0 + j * 128) // 128, h * 64:(h + 1) * 64],
                                            None, None, 128) for j in range(L // 128)]
                                    attention(q_h[:, t0:t0 + L], L, kts, yT[:, h, t0:t0 + L])
                                    chk("Dattn%d" % h + g)
                        branch_merge(l, g, 3, yT, g1[g])
                        chk("D" + g)

                    def r3(v):
                        return v.re("p (h c) -> p h c", c=64)

                    def bfv(v):
                        return v.bitcast(BF16)[:, 0:256]

                    nch = T // 64
                    with K.scope():
                        yT = K.tile([64, 8, T], BF16, "yb")
                        gb = K.tile([64, nch, 32], F32, "gb")
                        wab = wload(w_in[l, :, O_BA:O_BA + 32], 8, 32)
                        for c in range(nch):
                            tt_i, off = divmod(c * 64, 512)
                            ps = psum()
                            for kc in range(8):
                                K.mm(ps[0:64, 0:32], hT[g][tt_i][:, kc, off:off + 64], wab[:, kc, :],
                                     start=(kc == 0), stop=(kc == 7), sig=(kc == 7))
                            K.cp("act", gb[:, c, :], ps[0:64, 0:32])
                        gv, bv = gb[:, :, 0:16], gb[:, :, 16:32]
                        K.tt("dve", gv, gv, dtb.un(1).bc([64, nch, 16]), ALU.add)
                        K.act(gv, gv, AF.Exp)
                        K.act(gv, gv, AF.Ln, bias=onec[0:64, 0:1])
                        K.tt("dve", gv, gv, nea.un(1).bc([64, nch, 16]), ALU.mult)
                        K.act(bv, bv, AF.Sigmoid)
                        for hg in range(2):
                            with K.scope():
                                qkv = K.tile([64, 12, T], BF16, "qkv")
                                oT = K.tile([64, 4, T], F32, "oTb")
                                w3 = [wload(w_in[l, :, o_ + hg * 256:o_ + hg * 256 + 256], 8, 256) for o_ in (O_BQ, O_BK, O_BV)]
                                nsq = len(SEQS[g])
                                Ls = SEQS[g][0][1]
                                with K.scope():
                                    rawr = Ring(K, 2, [64, nsq, Ls + 4], F32, "raw")
                                    for rt in rawr.t:
                                        K.memset("pool", rt, 0.0)
                                    cvr = Ring(K, 2, [64, T], F32, "cv")
                                    f64 = Ring(K, 4, [64, 512], F32, "f64")
                                    for which in range(3):
                                        for hi in range(4):
                                            h = hg * 4 + hi
                                            raw = rawr.next()
                                            for tt_i in range(NT[g]):
                                                ps = psum()
                                                proj_fm64(w3[which], hi * 64, hT[g][tt_i], ps[0:64, :])
                                                if sample:
                                                    K.cp("act", raw[:, 0, 2 + tt_i * 512:2 + (tt_i + 1) * 512], ps[0:64, :])
                                                else:
                                                    for s_ in range(2):
                                                        K.cp("act", raw[:, s_, 2:2 + 256], ps[0:64, s_ * 256:(s_ + 1) * 256])
                                            cv = cvr.next()
                                            ch = which * 8 + h
                                            for s_, (t0, L) in enumerate(SEQS[g]):
                                                K.ts("dve", cv[:, t0:t0 + L], raw[:, s_, 0:L], convw[:, 0, ch:ch + 1])
                                                for j in range(1, 5):
                                                    K.stt("dve", cv[:, t0:t0 + L], raw[:, s_, j:j + L],
                                                          convw[:, j, ch:ch + 1], cv[:, t0:t0 + L], ALU.mult, ALU.add)
                                            K.act(cv, cv, AF.Silu)
                                            if which < 2:
                                                for tt_i in range(NT[g]):
                                                    tsl = slice(tt_i * 512, (tt_i + 1) * 512)
                                                    sq = f64.next()
                                                    K.act(sq, cv[:, tsl], AF.Square)
                                                    ps = psum()
                                                    K.mm(ps[0:64, :], onesf[0:64, 0:64], sq)
                                                    rn = f64.next()
                                                    K.act(rn, ps[0:64, :], AF.Sqrt, bias=epsc[0:64, 0:1], scale=1.0)
                                                    K.recip(rn, rn)
                                                    K.stt("dve", qkv[:, which * 4 + hi, tsl], cv[:, tsl],
                                                          (0.125 if which == 0 else 1.0), rn, ALU.mult, ALU.mult)
                                            else:
                                                K.cp("pool", qkv[:, 8 + hi, :], cv)
                                with K.scope():
                                    Lr = Ring(K, 20 if sample else 40, [64, 256], F32, "bL")
                                    Nr = Ring(K, 6 if sample else 14, [64, 256], F32, "bN")
                                    Sst = [K.tile([64, 256], F32, "Sb%d" % d_) for d_ in range(2)]
                                    idf = cst[0:64, C_ID:C_ID + 64]
                                    idb = cstb[0:64, C_ID:C_ID + 64]
                                    one64 = onesf[0:64, 0:64]
                                    HS = [slice(hi * 64, (hi + 1) * 64) for hi in range(4)]
                                    for si, (t0, L) in enumerate(SEQS[g]):
                                        ncs = L // 64
                                        for d_ in range(2):
                                            if sample:
                                                K.dma("sp", r3(Sst[d_]), sd_in[l, d_, hg * 4:hg * 4 + 4].re("h k v -> k h v"), stage_slot[d_])
                                            else:
                                                K.memset("pool", Sst[d_], 0.0)
                                        visited = set()
                                        def chunk_gen(i, d_):
                                            c = i if d_ == 0 else ncs - 1 - i
                                            tok0 = t0 + c * 64
                                            cg = tok0 // 64
                                            csl = slice(tok0, tok0 + 64)
                                            U_ = cst[0:64, C_U[d_]:C_U[d_] + 64]
                                            nm_ = cst[0:64, C_NM[d_]:C_NM[d_] + 64]
                                            nmT_ = cst[0:64, C_NMT[d_]:C_NMT[d_] + 64]
                                            SL_ = cst[0:64, C_SL[d_]:C_SL[d_] + 64]
                                            gcol = gb[:, cg, d_ * 8 + hg * 4:d_ * 8 + hg * 4 + 4]
                                            bcol = gb[:, cg, 16 + d_ * 8 + hg * 4:16 + d_ * 8 + hg * 4 + 4]
                                            S = Sst[d_]
                                            ps = psum()
                                            psb = ps
                                            for hi in range(4):
                                                K.tr(psb[0:64, hi * 64:(hi + 1) * 64], qkv[:, 4 + hi, csl], idb, sig=False)
                                                K.tr(psb[0:64, 256 + hi * 64:256 + (hi + 1) * 64], qkv[:, 8 + hi, csl], idb, sig=(hi == 3))
                                            Ktm, Vtm = Lr.next(), Lr.next()
                                            K.cp("act", Ktm, psb[0:64, 0:256])
                                            yield
                                            K.cp("dve", Vtm, psb[0:64, 256:512])
                                            gU = Lr.next()
                                            K.tt("pool", r3(gU), U_.un(1).bc([64, 4, 64]), gcol.un(2).bc([64, 4, 64]), ALU.mult)
                                            ps1 = psum()
                                            K.mm(ps1[0:64, 0:256], one64, gU, sig=False)
                                            K.mm(ps1[0:64, 256:260], U_, gcol, sig=False)
                                            K.mm(ps1[0:64, 260:264], one64, gcol)
                                            sm = Lr.next()
                                            K.cp("dve", sm[:, 0:8], ps1[0:64, 256:264])
                                            gc, gl = sm[:, 0:4], sm[:, 4:8]
                                            K.act(sm[:, 8:12], gc, AF.Exp)
                                            K.tt("dve", sm[:, 12:16], gl, gc, ALU.subtract)
                                            K.act(sm[:, 12:16], sm[:, 12:16], AF.Exp)
                                            K.act(sm[:, 16:20], gl, AF.Exp)
                                            K.tt("dve", sm[:, 20:24], sm[:, 8:12], bcol, ALU.mult)
                                            be, ekd, cd = sm[:, 20:24], sm[:, 12:16], sm[:, 16:20]
                                            gcrow = Lr.next()
                                            K.cp("act", gcrow, ps1[0:64, 0:256])
                                            yield
                                            G = Lr.next()
                                            K.tt("dve", r3(G), nm_.un(1).bc([64, 4, 64]), r3(gcrow), ALU.subtract)
                                            K.tt("dve", r3(G), r3(G), gc.un(2).bc([64, 4, 64]), ALU.add)
                                            K.act(G, G, AF.Exp)
                                            GT = Lr.next()
                                            K.tt("pool", r3(GT), r3(gcrow), nmT_.un(1).bc([64, 4, 64]), ALU.add)
                                            K.tt("pool", r3(GT), r3(GT), gc.un(2).bc([64, 4, 64]), ALU.subtract)
                                            K.act(GT, GT, AF.Exp)
                                            Erow = Lr.next()
                                            K.act(Erow, gcrow, AF.Exp)
                                            yield
                                            psk = psum()
                                            for hi in range(4):
                                                K.mm(psk[0:64, HS[hi]], qkv[:, 4 + hi, csl], qkv[:, 4 + hi, csl], sig=(hi == 3))
                                            bsl = Lr.next()
                                            K.tt("pool", r3(bsl), SL_.un(1).bc([64, 4, 64]), bcol.un(2).bc([64, 4, 64]), ALU.mult)
                                            A = Lr.next()
                                            K.tt("dve", A, psk[0:64, 0:256], G, ALU.mult)
                                            K.tt("dve", A, A, bsl, ALU.mult)
                                            pst = psum()
                                            for hi in range(4):
                                                K.tr(pst[0:64, HS[hi]], A[:, HS[hi]], idf, sig=(hi == 3))
                                            AT = Lr.next()
                                            K.cp("act", AT, pst[0:64, 0:256])
                                            yield
                                            X = Nr.next()
                                            K.tt("dve", r3(X), idf.un(1).bc([64, 4, 64]), r3(pst[0:64, 0:256]), ALU.subtract)
                                            P_, PT = A, AT
                                            for kk in range(1, 6):
                                                psp = psum()
                                                for hi in range(4):
                                                    K.mm(psp[0:64, HS[hi]], PT[:, HS[hi]], P_[:, HS[hi]], sig=(hi == 3))
                                                Pn = Nr.next()
                                                K.cp("act", Pn, psp[0:64, 0:256])
                                                PTn = None
                                                if kk < 5:
                                                    pspt = psum()
                                                    for hi in range(4):
                                                        K.mm(pspt[0:64, HS[hi]], P_[:, HS[hi]], PT[:, HS[hi]], sig=(hi == 3))
                                                    PTn = Nr.next()
                                                    K.cp("pool", PTn, pspt[0:64, 0:256]) if False else K.cp("dve", PTn, pspt[0:64, 0:256])
                                                psx = psum()
                                                for hi in range(4):
                                                    K.mm(psx[0:64, HS[hi]], Pn[:, HS[hi]], X[:, HS[hi]], sig=(hi == 3))
                                                Xn = Nr.next() if kk < 5 else Lr.next()
                                                yield
                                                K.tt("dve", Xn, psx[0:64, 0:256], X, ALU.add)
                                                P_, PT, X = Pn, PTn, Xn
                                            TT_ = X
                                            Rk, Rv, Kd = Lr.next(), Lr.next(), Lr.next()
                                            K.tt("pool", r3(Rk), r3(Ktm), be.un(2).bc([64, 4, 64]), ALU.mult)
                                            K.tt("pool", r3(Rv), r3(Vtm), bcol.un(2).bc([64, 4, 64]), ALU.mult)
                                            K.tt("pool", r3(Kd), r3(Ktm), ekd.un(2).bc([64, 4, 64]), ALU.mult)
                                            psw = psum()
                                            for hi in range(4):
                                                K.mm(psw[0:64, HS[hi]], Rk[:, HS[hi]], TT_[:, HS[hi]], sig=(hi == 3))
                                            nWkT = Lr.next()
                                            K.ts("dve", nWkT, psw[0:64, 0:256], -1.0)
                                            yield
                                            psq = psum()
                                            for hi in range(4):
                                                K.mm(psq[0:64, HS[hi]], qkv[:, 4 + hi, csl], qkv[:, hi, csl], sig=(hi == 3))
                                            PqkT = Lr.next()
                                            K.tt("dve", PqkT, psq[0:64, 0:256], GT, ALU.mult)
                                            QdT = Lr.next()
                                            K.tt("pool", r3(QdT), qkv[:, 0:4, csl], r3(Erow), ALU.mult)
                                            yield
                                            psu = psum()
                                            for hi in range(4):
                                                K.mm(psu[0:64, HS[hi]], TT_[:, HS[hi]], Rv[:, HS[hi]], start=True, stop=False, sig=False)
                                                K.mm(psu[0:64, HS[hi]], nWkT[:, HS[hi]], S[:, HS[hi]], start=False, stop=True, sig=(hi == 3))
                                            Uc = Lr.next()
                                            yield
                                            K.cp("act", Uc, psu[0:64, 0:256])
                                            pso = psum()
                                            for hi in range(4):
                                                K.mm(pso[0:64, HS[hi]], S[:, HS[hi]], QdT[:, HS[hi]], start=True, stop=False, sig=False)
                                                K.mm(pso[0:64, HS[hi]], Uc[:, HS[hi]], PqkT[:, HS[hi]], start=False, stop=True, sig=(hi == 3))
                                            ov = oT[:, :, csl]
                                            if c not in visited:
                                                K.cp("act", ov, r3(pso[0:64, 0:256]))
                                                visited.add(c)
                                            else:
                                                K.tt("dve", ov, ov, r3(pso[0:64, 0:256]), ALU.add)
                                            psm = psum()
                                            for hi in range(4):
                                                K.mm(psm[0:64, HS[hi]], Kd[:, HS[hi]], Uc[:, HS[hi]], sig=(hi == 3))
                                            K.tt("dve", r3(S), r3(S), cd.un(2).bc([64, 4, 64]), ALU.mult)
                                            K.tt("dve", S, S, psm[0:64, 0:256], ALU.add)
                                            yield
                                            yield
                                        for i in range(ncs):
                                            gens = [chunk_gen(i, 0), chunk_gen(i, 1)]
                                            if sample:
                                                for g_ in gens:
                                                    for _ in g_:
                                                        pass
                                            else:
                                                live = list(gens)
                                                while live:
                                                    for g_ in list(live):
                                                        try:
                                                            next(g_)
                                                        except StopIteration:
                                                            live.remove(g_)
                                        if not sample:
                                            for d_ in range(2):
                                                K.dma("sp", nsd[si, l, d_, hg * 4:hg * 4 + 4].re("h k v -> k h v"), r3(Sst[d_]), io_slot)
                                with K.scope():
                                    f64 = Ring(K, 4, [64, 512], F32, "f64z")
                                    wz = wload(w_in[l, :, O_BZ + hg * 256:O_BZ + hg * 256 + 256], 8, 256)
                                    for hi in range(4):
                                        for tt_i in range(NT[g]):
                                            tsl = slice(tt_i * 512, (tt_i + 1) * 512)
                                            sq = f64.next()
                                            K.act(sq, oT[:, hi, tsl], AF.Square)
                                            ps = psum()
                                            K.mm(ps[0:64, :], one64, sq)
                                            rn = f64.next()
                                            K.act(rn, ps[0:64, :], AF.Sqrt, bias=epsc[0:64, 0:1], scale=1.0 / 64)
                                            K.recip(rn, rn)
                                            psz = psum()
                                            proj_fm64(wz, hi * 64, hT[g][tt_i], psz[0:64, :])
                                            sz = f64.next()
                                            K.act(sz, psz[0:64, :], AF.Silu)
                                            K.tt("dve", rn, oT[:, hi, tsl], rn, ALU.mult)
                                            K.stt("dve", yT[:, hg * 4 + hi, tsl], rn, dnw[:, 0:1], sz, ALU.mult, ALU.mult)
                        branch_merge(l, g, 1, yT, g1[g])
                        chk("B" + g)

                    with K.scope():
                        yT = K.tile([64, 8, T], BF16, "yc")
                        lbR = K.tile([64, 1024], F32, "lbR")
                        omlbR = K.tile([64, 1024], F32, "omlbR")
                        compute_lb(lbR, True)
                        K.ts("dve", omlbR, lbR, -1.0, 1.0, ALU.mult, ALU.add)
                        for hg in range(2):
                            with K.scope():
                                qT = K.tile([64, 4, T], BF16, "cq")
                                kT = K.tile([64, 2, 4, T], BF16, "ckk")
                                oT = K.tile([64, 4, T], F32, "oTc")
                                wcq = wload(w_in[l, :, O_CQ + hg * 256:O_CQ + hg * 256 + 256], 8, 256)
                                wcf = [wload(w_in[l, :, O_CF + d_ * 512 + hg * 256:O_CF + d_ * 512 + hg * 256 + 256], 8, 256) for d_ in range(2)]
                                wci = wload(w_in[l, :, O_CI + hg * 256:O_CI + hg * 256 + 256], 8, 256)
                                with K.scope():
                                    f64 = Ring(K, 4, [64, 512], F32, "f64c")
                                    for hi in range(4):
                                        for tt_i in range(NT[g]):
                                            tsl = slice(tt_i * 512, (tt_i + 1) * 512)
                                            ps = psum()
                                            proj_fm64(wcq, hi * 64, hT[g][tt_i], ps[0:64, :])
                                            K.act(qT[:, hi, tsl], ps[0:64, :], AF.Silu)
                                            for d_ in range(2):
                                                ps = psum()
                                                proj_fm64(wcf[d_], hi * 64, hT[g][tt_i], ps[0:64, :])
                                                sg = f64.next()
                                                K.act(sg, ps[0:64, :], AF.Sigmoid, scale=-1.0)
                                                col = d_ * 8 + hg * 4 + hi
                                                K.ts("dve", kT[:, d_, hi, tsl], sg, omlbT[:, col:col + 1])
                                with K.scope():
                                    Lr = Ring(K, 18 if sample else 36, [64, 256], F32, "cL")
                                    Sst = [K.tile([64, 256], F32, "Sc%d" % d_) for d_ in range(2)]
                                    HS = [slice(hi * 64, (hi + 1) * 64) for hi in range(4)]
                                    for si, (t0, L) in enumerate(SEQS[g]):
                                        ncs = L // 64
                                        for d_ in range(2):
                                            if sample:
                                                K.dma("sp", r3(Sst[d_]), sh_in[l, d_, hg * 4:hg * 4 + 4].re("h k v -> k h v"), stage_slot[d_])
                                            else:
                                                K.memset("pool", Sst[d_], 0.0)
                                        visited = set()
                                        def chunk_gen(i, d_):
                                            c = i if d_ == 0 else ncs - 1 - i
                                            tok0 = t0 + c * 64
                                            csl = slice(tok0, tok0 + 64)
                                            tt_i, off = divmod(tok0, 512)
                                            U_ = cst[0:64, C_U[d_]:C_U[d_] + 64]
                                            W2_ = cst[0:64, C_W2[d_]:C_W2[d_] + 64]
                                            MT_ = cst[0:64, C_MT[d_]:C_MT[d_] + 64]
                                            mid = MID[d_]
                                            last = 63 if d_ == 0 else 0
                                            S = Sst[d_]
                                            psf, psv = psum(), psum()
                                            for kc in range(8):
                                                K.mm(psf[0:64, 0:256], hT[g][tt_i][:, kc, off:off + 64], wcf[d_][:, kc, :],
                                                     start=(kc == 0), stop=(kc == 7), sig=(kc == 7))
                                            for kc in range(8):
                                                K.mm(psv[0:64, 0:256], hT[g][tt_i][:, kc, off:off + 64], wci[:, kc, :],
                                                     start=(kc == 0), stop=(kc == 7), sig=(kc == 7))
                                            Vt = bfv(Lr.next())
                                            K.cp("act", Vt, psv[0:64, 0:256])
                                            yield
                                            f = Lr.next()
                                            K.act(f, psf[0:64, 0:256], AF.Sigmoid)
                                            cs_ = slice(d_ * 512 + hg * 256, d_ * 512 + hg * 256 + 256)
                                            K.tt("dve", f, f, omlbR[:, cs_], ALU.mult)
                                            K.tt("dve", f, f, lbR[:, cs_], ALU.add)
                                            lf = Lr.next()
                                            K.act(lf, f, AF.Ln)
                                            yield
                                            ktm = Lr.next()
                                            K.ts("pool", ktm, f, -1.0, 1.0, ALU.mult, ALU.add)
                                            psb_ = psum()
                                            for hi in range(4):
                                                K.mm(psb_[0:64, HS[hi]], lf[:, HS[hi]], U_, sig=(hi == 3))
                                            psw2 = psum()
                                            K.mm(psw2[0:64, 0:256], W2_, lf)
                                            bT = Lr.next()
                                            K.cp("dve", bT, psb_[0:64, 0:256])
                                            yield
                                            Eb = Lr.next()
                                            K.act(Eb, bT, AF.Exp)
                                            bp = Lr.next()
                                            K.tt("dve", r3(bp), r3(bT), r3(bT)[:, :, mid:mid + 1].bc([64, 4, 64]), ALU.subtract)
                                            Ebp, Ebn = Lr.next(), Lr.next()
                                            K.act(Ebp, bp, AF.Exp)
                                            K.act(Ebn, bp, AF.Exp, scale=-1.0)
                                            yield
                                            QiT, KiT, QdT = bfv(Lr.next()), bfv(Lr.next()), Lr.next()
                                            K.tt("pool", r3(QiT), qT[:, :, csl], r3(Ebp), ALU.mult)
                                            K.tt("pool", r3(KiT), kT[:, d_, :, csl], r3(Ebn), ALU.mult)
                                            K.tt("dve", r3(QdT), qT[:, :, csl], r3(Eb), ALU.mult)
                                            Ekd = Lr.next()
                                            K.act(Ekd, psw2[0:64, 0:256], AF.Exp)
                                            Kd = bfv(Lr.next())
                                            K.tt("dve", Kd, ktm, Ekd, ALU.mult)
                                            yield
                                            psa = psum()
                                            if d_ == 0:
                                                f_t, p_t, p_j, z_j = (32, 64), (0, 32), (0, 32), (32, 64)
                                            else:
                                                f_t, p_t, p_j, z_j = (0, 32), (32, 64), (32, 64), (0, 32)
                                            zer = cstb[0:64, C_ZERO:C_ZERO + 32]
                                            for hi in range(4):
                                                b0 = hi * 64
                                                K.mm(psa[0:64, b0 + f_t[0]:b0 + f_t[1]], KiT[:, HS[hi]], QiT[:, b0 + f_t[0]:b0 + f_t[1]], sig=False)
                                                K.mm(psa[p_j[0]:p_j[1], b0 + p_t[0]:b0 + p_t[1]], KiT[:, b0 + p_j[0]:b0 + p_j[1]],
                                                     QiT[:, b0 + p_t[0]:b0 + p_t[1]], sig=False)
                                                K.mm(psa[z_j[0]:z_j[1], b0 + p_t[0]:b0 + p_t[1]], zer, QiT[:, b0 + p_t[0]:b0 + p_t[1]], sig=(hi == 3))
                                            att = bfv(Lr.next())
                                            K.tt("dve", r3(att), r3(psa[0:64, 0:256]), MT_.un(1).bc([64, 4, 64]), ALU.mult)
                                            yield
                                            pso = psum()
                                            for hi in range(4):
                                                K.mm(pso[0:64, HS[hi]], Vt[:, HS[hi]], att[:, HS[hi]], start=True, stop=False, sig=False)
                                                K.mm(pso[0:64, HS[hi]], S[:, HS[hi]], QdT[:, HS[hi]], start=False, stop=True, sig=(hi == 3))
                                            ov = oT[:, :, csl]
                                            if c not in visited:
                                                K.cp("act", ov, r3(pso[0:64, 0:256]))
                                                visited.add(c)
                                            else:
                                                K.tt("dve", ov, ov, r3(pso[0:64, 0:256]), ALU.add)
                                            psm = psum()
                                            for hi in range(4):
                                                K.mm(psm[0:64, HS[hi]], Kd[:, HS[hi]], Vt[:, HS[hi]], sig=(hi == 3))
                                            K.tt("dve", r3(S), r3(S), r3(Eb)[:, :, last:last + 1].bc([64, 4, 64]), ALU.mult)
                                            K.tt("dve", S, S, psm[0:64, 0:256], ALU.add)
                                            yield
                                            yield
                                        for i in range(ncs):
                                            gens = [chunk_gen(i, 0), chunk_gen(i, 1)]
                                            if sample:
                                                for g_ in gens:
                                                    for _ in g_:
                                                        pass
                                            else:
                                                live = list(gens)
                                                while live:
                                                    for g_ in list(live):
                                                        try:
                                                            next(g_)
                                                        except StopIteration:
                                                            live.remove(g_)
                                        if not sample:
                                            for d_ in range(2):
                                                K.dma("sp", nsh[si, l, d_, hg * 4:hg * 4 + 4].re("h k v -> k h v"), r3(Sst[d_]), io_slot)
                                with K.scope():
                                    f64 = Ring(K, 4, [64, 512], F32, "f64g")
                                    wcg = wload(w_in[l, :, O_CG + hg * 256:O_CG + hg * 256 + 256], 8, 256)
                                    for hi in range(4):
                                        for tt_i in range(NT[g]):
                                            tsl = slice(tt_i * 512, (tt_i + 1) * 512)
                                            psg = psum()
                                            proj_fm64(wcg, hi * 64, hT[g][tt_i], psg[0:64, :])
                                            sg = f64.next()
                                            K.act(sg, psg[0:64, :], AF.Sigmoid)
                                            K.tt("dve", sg, oT[:, hi, tsl], sg, ALU.mult)
                                            sq = f64.next()
                                            K.act(sq, sg, AF.Square)
                                            ps = psum()
                                            K.mm(ps[0:64, :], onesf[0:64, 0:64], sq)
                                            rn = f64.next()
                                            K.act(rn, ps[0:64, :], AF.Sqrt, bias=epsc[0:64, 0:1], scale=1.0 / 64)
                                            K.recip(rn, rn)
                                            K.stt("dve", yT[:, hg * 4 + hi, tsl], sg, hnw[:, 0:1], rn, ALU.mult, ALU.mult)
                        branch_merge(l, g, 2, yT, g1[g])
                        chk("C" + g)
                    rms_mod(g, wm2[g], sh2[g])
                    with K.scope():
                        uT = K.tile([128, 4, 512], BF16, "uT")
                        rr = Ring(K, 2, [128, 512], F32, "relu")
                        for hb in range(8):
                            w1 = wload(mlp_w1[l, :, hb * 512:(hb + 1) * 512], 8, 512)
                            w2 = [wload(mlp_w2[l, hb * 512:(hb + 1) * 512, half * 512:(half + 1) * 512], 4, 512) for half in range(2)]
                            for tt_i in range(NT[g]):
                                for hc in range(4):
                                    ps = psum()
                                    for kc in range(8):
                                        K.mm(ps, w1[:, kc, hc * 128:(hc + 1) * 128], hT[g][tt_i][:, kc, :],
                                             start=(kc == 0), stop=(kc == 7), sig=(kc == 7))
                                    r = rr.next()
                                    K.act(r, ps, AF.Relu)
                                    K.tt("pool", uT[:, hc, :], r, r, ALU.mult)
                                for oc in range(8):
                                    ps = psum()
                                    for hc in range(4):
                                        K.mm(ps, w2[oc // 4][:, hc, (oc % 4) * 128:(oc % 4 + 1) * 128], uT[:, hc, :],
                                             start=(hc == 0), stop=(hc == 3), sig=(hc == 3))
                                    xv = xT[g][tt_i][:, oc, :]
                                    K.stt("dve", xv, ps, g2[g][:, oc:oc + 1], xv, ALU.mult, ALU.add)

        chk("layers")
        with K.scope():
            fw = K.tile([128, 8], F32, "fw")
            K.dma("sp", fw, final_norm_w.re("(kc p) -> p kc", p=128), misc_slot)
            zero8 = K.tile([128, 8], F32, "z8")
            K.memset("dve", zero8, 0.0)
            yst = Ring(K, 2, [128, D], F32, "yst")
            hf = {"P": [K.tile([128, 8, 512], F32, "hfP")], "S": [K.tile([128, 8, 512], F32, "hfS%d" % i) for i in range(2)]}
            for g, dst in (("P", y_p), ("S", y_s)):
                for tt_i in range(NT[g]):
                    x = xT[g][tt_i]
                    ps = psum()
                    for kc in range(8):
                        sq = sqring.next()
                        K.act(sq, x[:, kc, :], AF.Square)
                        K.mm(ps, onesb, sq, start=(kc == 0), stop=(kc == 7))
                    rstd = rsring.next()
                    K.act(rstd, ps, AF.Sqrt, bias=epsc[:, 0:1], scale=1.0 / D)
                    K.recip(rstd, rstd)
                    for kc in range(8):
                        K.stt("dve", hf[g][tt_i][:, kc, :], x[:, kc, :], fw[:, kc:kc + 1], rstd, ALU.mult, ALU.mult)
                    for j in range(4):
                        yt = yst.next()
                        for half in range(2):
                            ps = psum()
                            for kc4 in range(4):
                                kc = half * 4 + kc4
                                K.tr(ps[:, kc4 * 128:(kc4 + 1) * 128], hf[g][tt_i][:, kc, j * 128:(j + 1) * 128], identf)
                            K.cp("act" if half else "dve", yt[:, half * 512:(half + 1) * 512], ps)
                        r0 = tt_i * 512 + j * 128
                        K.dma("sp", dst[r0:r0 + 128, :], yt, io_slot)
        K.finish()
    return nc


_CACHE = {}


def kernel(**inp):
    f = lambda k: np.ascontiguousarray(np.asarray(inp[k], dtype=np.float32))
    if "nc" not in _CACHE:
        _CACHE["nc"] = build_program()
    nc = _CACHE["nc"]
    rpb = f("na_rpb")
    kc = np.arange(64)[:, None, None]
    e = np.arange(15)[None, :, None]
    qc = np.arange(64)[None, None, :]
    dc = np.clip(kc - qc + 15, 0, 30) + 0 * e
    rr = (14 - e) + 0 * dc
    rpbx = np.ascontiguousarray(rpb[:, :, rr, dc]).reshape(DEPTH, 8, 64, 15 * 64)
    shared = {
        "norm_w": f("norm_w"), "ada_w": f("ada_w"), "ada_b": f("ada_b"), "w_in": f("w_in"),
        "attn_sink": f("attn_sink"), "delta_conv": f("delta_conv"),
        "delta_a_log": f("delta_a_log").reshape(DEPTH, 16), "delta_dt_bias": f("delta_dt_bias").reshape(DEPTH, 16),
        "delta_norm_w": f("delta_norm_w"), "hgrn_lb": f("hgrn_lb").reshape(DEPTH, 1024),
        "hgrn_norm_w": f("hgrn_norm_w"), "rpbx": rpbx, "w_branch": f("w_branch"), "w_out": f("w_out"),
        "mlp_w1": f("mlp_w1"), "mlp_w2": f("mlp_w2"), "final_norm_w": f("final_norm_w"),
        "consts": make_consts(), "ropetab": make_rope_tab(),
    }
    xp, xs = f("x_prompt"), f("x_sample")
    cak, cav, cnk, cnv = f("cache_attn_k"), f("cache_attn_v"), f("cache_na_k"), f("cache_na_v")
    sd, sh, c, cctx = f("state_delta"), f("state_hgrn"), f("c"), f("c_ctx")
    in_maps = []
    for i in range(NCORES):
        m = dict(shared)
        m["x_p"] = xp[2 * i:2 * i + 2].reshape(NP_, D)
        m["x_s"] = xs[i]
        m["cak"] = cak[i].reshape(DEPTH, PAST, 128)
        m["cav"] = cav[i].reshape(DEPTH, PAST, 128)
        m["cnk"] = cnk[i].reshape(DEPTH, PAST, 512)
        m["cnv"] = cnv[i].reshape(DEPTH, PAST, 512)
        m["sd"] = sd[i]
        m["sh"] = sh[i]
        m["cvec"] = np.stack([cctx, c[i]], 0)
        in_maps.append(m)
    res = run_bass_kernel_spmd(nc, in_maps, core_ids=list(range(NCORES)))
    R = res.results
    cat = lambda k: np.concatenate([np.asarray(r[k]) for r in R], 0)
    y_prompt = cat("y_p").reshape(16, 256, D)
    y_sample = cat("y_s").reshape(8, 1024, D)
    nak = cat("nak").reshape(16, DEPTH, 256, 2, 64)
    nav = cat("nav").reshape(16, DEPTH, 256, 2, 64)
    nnk = cat("nnk").reshape(16, DEPTH, 256, 8, 64)
    nnv = cat("nnv").reshape(16, DEPTH, 256, 8, 64)
    nsd = cat("nsd").reshape(16, DEPTH, 2, 8, 64, 64)
    nsh = cat("nsh").reshape(16, DEPTH, 2, 8, 64, 64)
    return tuple(np.ascontiguousarray(a, dtype=np.float32) for a in (y_prompt, y_sample, nak, nav, nnk, nnv, nsd, nsh))
```

```python
import math
from contextlib import ExitStack, contextmanager
import numpy as np
import ml_dtypes
import concourse.bass as bass
import concourse.mybir as mybir
from concourse.bass_utils import run_bass_kernel_spmd

F32 = mybir.dt.float32
BF16 = mybir.dt.bfloat16
ALU = mybir.AluOpType
AF = mybir.ActivationFunctionType

D = 1024
DEPTH = 4
NCORES = 8
NP_ = 512
NS_ = 1024
PAST = 512
EPS = 1e-6
NEGM = -30000.0
O_AQ, O_AK, O_AV = 0, 512, 640
O_BQ, O_BK, O_BV, O_BZ, O_BA, O_BB = 768, 1280, 1792, 2304, 2816, 2832
O_CQ, O_CF, O_CI, O_CG = 2848, 3360, 4384, 4896
O_DQ, O_DK, O_DV, O_G = 5408, 5920, 6432, 6944
N_IN = 11040

C_ID, C_ONE = 0, 128
C_U = (256, 320)
C_W2 = (384, 448)
C_NM = (512, 576)
C_NMT = (640, 704)
C_SL = (768, 832)
C_MT = (896, 960)
C_ROPE, C_COLM, C_MPREV, C_MNEXT = 1024, 1088, 1152, 1280
C_ZERO = 1408
NCST = 1472
MID = (32, 31)


def make_consts():
    c = np.zeros((128, NCST), np.float32)
    c[:, C_ID:C_ID + 128] = np.eye(128)
    c[:, C_ONE:C_ONE + 128] = 1.0
    t = np.arange(64)
    for d in range(2):
        if d == 0:
            U = (t[:, None] <= t[None, :]).astype(np.float32)
        else:
            U = (t[:, None] >= t[None, :]).astype(np.float32)
        c[:64, C_U[d]:C_U[d] + 64] = U
        c[:64, C_W2[d]:C_W2[d] + 64] = 1.0 - U
        incl = U.T
        c[:64, C_NM[d]:C_NM[d] + 64] = np.where(incl > 0, 0.0, NEGM)
        c[:64, C_NMT[d]:C_NMT[d] + 64] = np.where(incl.T > 0, 0.0, NEGM)
        c[:64, C_SL[d]:C_SL[d] + 64] = incl - np.eye(64)
        c[:64, C_MT[d]:C_MT[d] + 64] = incl.T
    P = np.zeros((64, 64), np.float32)
    for half in (0, 32):
        for i in range(16):
            P[half + i, half + i + 16] = -1.0
            P[half + 16 + i, half + i] = 1.0
    c[:64, C_ROPE:C_ROPE + 64] = P.T
    qc = np.arange(64)
    ws = np.clip(qc - 8, 0, 48)
    kc = np.arange(64)
    c[:64, C_COLM:C_COLM + 64] = ((kc[:, None] >= ws[None, :]) & (kc[:, None] < ws[None, :] + 16)).astype(np.float32)
    k = np.arange(128)
    c[:, C_MPREV:C_MPREV + 128] = (k[:, None] >= k[None, :]).astype(np.float32)
    c[:, C_MNEXT:C_MNEXT + 128] = (k[:, None] <= k[None, :]).astype(np.float32)
    return c


def make_rope_tab():
    tt = np.arange(NS_)
    inv = (10000.0 ** (-np.arange(16, dtype=np.float32) / 16)).astype(np.float32)
    tab = np.zeros((64, 2, NS_), np.float32)
    for half, pos in ((0, tt // 64), (32, tt % 64)):
        ang = pos.astype(np.float32)[None, :] * inv[:, None]
        cs, sn = np.cos(ang).astype(np.float32), np.sin(ang).astype(np.float32)
        tab[half:half + 16, 0], tab[half + 16:half + 32, 0] = cs, cs
        tab[half:half + 16, 1], tab[half + 16:half + 32, 1] = sn, sn
    return tab


class V:
    __slots__ = ("ap", "toks")

    def __init__(self, ap, toks=()):
        self.ap = ap
        self.toks = toks

    def __getitem__(self, idx):
        return V(self.ap[idx], self.toks)

    def bc(self, shape):
        return V(self.ap.broadcast_to(list(shape)), self.toks)

    def un(self, axis):
        return V(self.ap.unsqueeze(axis), self.toks)

    def re(self, pat, **kw):
        return V(self.ap.rearrange(pat, **kw), self.toks)

    def bitcast(self, dt):
        return V(self.ap.bitcast(dt), self.toks)


class Slot:
    def __init__(self, key, sem):
        self.key, self.sem, self.cnt = key, sem, 0


class KB:
    def __init__(self, nc, es):
        self.nc = nc
        self.es = es
        self.eng = {"pe": nc.tensor, "act": nc.scalar, "dve": nc.vector, "pool": nc.gpsimd, "sp": nc.sync}
        self.semh = {}
        self.cnt = {}
        for e in ("pe", "act", "dve", "pool"):
            self.semh[e] = es.enter_context(nc.semaphore("s_" + e))
            self.cnt[e] = 0
        self.seen = {e: {} for e in self.eng}
        self.lastw = {}
        self.readers = {}
        self.slots = []
        self.ntile = 0
        self.scopes = []

    def tile(self, shape, dt, name=None):
        self.ntile += 1
        nm = "%s_%d" % (name or "t", self.ntile)
        st = self.scopes[-1] if self.scopes else self.es
        h = st.enter_context(self.nc.sbuf_tensor(nm, list(shape), dt))
        v = V(h.ap(), (nm,))
        return v

    def slot(self):
        s = Slot("d%d" % len(self.slots), self.es.enter_context(self.nc.semaphore("sd%d" % len(self.slots))))
        self.semh[s.key] = s.sem
        self.slots.append(s)
        return s

    @contextmanager
    def scope(self):
        st = ExitStack()
        self.scopes.append(st)
        try:
            yield
        finally:
            self.barrier()
            self.scopes.pop()
            st.close()

    def barrier(self):
        marks = [(e, c) for e, c in self.cnt.items() if c > 0] + [(s.key, s.cnt) for s in self.slots if s.cnt > 0]
        for e in ("pe", "act", "dve", "pool", "sp"):
            self._wait(e, marks, True)

    def _wait(self, e, marks, full=False):
        need = {}
        for (k, v) in marks:
            if k == e and e == "pe":
                continue
            if need.get(k, 0) < v:
                need[k] = v
        for k, v in need.items():
            if self.seen[e].get(k, 0) < v:
                self.eng[e].wait_ge(self.semh[k], v)
                self.seen[e][k] = v

    def _deps(self, reads, writes):
        marks = []
        for t in reads:
            if t in self.lastw:
                marks.append(self.lastw[t])
        for t in writes:
            if t in self.lastw:
                marks.append(self.lastw[t])
            marks.extend(self.readers.get(t, {}).items())
        return marks

    def _record(self, mark, reads, writes):
        for t in reads:
            r = self.readers.setdefault(t, {})
            if r.get(mark[0], 0) < mark[1]:
                r[mark[0]] = mark[1]
        for t in writes:
            self.lastw[t] = mark
            self.readers[t] = {}

    def op(self, e, fn, ins, outs, sig=True):
        reads = [t for v in ins for t in v.toks]
        writes = [t for v in outs for t in v.toks]
        writes = writes + [t for t in reads if t.startswith("ps") and t not in writes]
        self._wait(e, self._deps(reads, writes))
        inst = fn()
        if sig:
            self.cnt[e] += 1
            inst.then_inc(self.semh[e], 1)
            mark = (e, self.cnt[e])
        else:
            mark = (e, self.cnt[e] + 1)
        self._record(mark, reads, writes)

    def dma(self, q, out, in_, slot, **kw):
        reads, writes = list(in_.toks), list(out.toks)
        if isinstance(slot, list):
            slot.append(slot.pop(0))
            slot = slot[-1]
        marks = self._deps(reads, writes)
        if slot.cnt > 0:
            marks.append((slot.key, slot.cnt))
        self._wait(q, marks)
        inst = self.eng[q].dma_start(out=out.ap, in_=in_.ap, **kw)
        slot.cnt += 16
        inst.then_inc(slot.sem, 16)
        self._record((slot.key, slot.cnt), reads, writes)

    def mm(self, out, lhsT, rhs, start=True, stop=True, sig=True):
        self.op("pe", lambda: self.nc.tensor.matmul(out.ap, lhsT.ap, rhs.ap, start=start, stop=stop),
                [lhsT, rhs] + ([] if start else [out]), [out], sig)

    def tr(self, out, in_, ident, sig=True):
        self.mm(out, in_, ident, sig=sig)

    def act(self, out, in_, func, bias=0.0, scale=1.0):
        ins = [in_] + [x for x in (bias, scale) if isinstance(x, V)]
        b = bias.ap if isinstance(bias, V) else bias
        s = scale.ap if isinstance(scale, V) else scale
        self.op("act", lambda: self.nc.scalar.activation(out.ap, in_.ap, func, bias=b, scale=s), ins, [out])

    def _ve(self, e):
        return self.nc.vector if e == "dve" else self.nc.gpsimd

    def tt(self, e, out, in0, in1, op):
        self.op(e, lambda: self._ve(e).tensor_tensor(out.ap, in0.ap, in1.ap, op), [in0, in1], [out])

    def ts(self, e, out, in0, s1, s2=None, op0=ALU.mult, op1=None):
        ins = [in0] + [x for x in (s1, s2) if isinstance(x, V)]
        a = s1.ap if isinstance(s1, V) else s1
        b = s2.ap if isinstance(s2, V) else s2
        if op1 is None:
            self.op(e, lambda: self._ve(e).tensor_scalar(out.ap, in0.ap, a, None, op0), ins, [out])
        else:
            self.op(e, lambda: self._ve(e).tensor_scalar(out.ap, in0.ap, a, b, op0, op1), ins, [out])

    def stt(self, e, out, in0, sc, in1, op0, op1):
        ins = [in0, in1] + ([sc] if isinstance(sc, V) else [])
        a = sc.ap if isinstance(sc, V) else sc
        self.op(e, lambda: self._ve(e).scalar_tensor_tensor(out.ap, in0.ap, a, in1.ap, op0, op1), ins, [out])

    def cp(self, e, out, in_):
        if e == "act":
            self.op("act", lambda: self.nc.scalar.copy(out.ap, in_.ap), [in_], [out])
        else:
            self.op(e, lambda: self._ve(e).tensor_copy(out.ap, in_.ap), [in_], [out])

    def memset(self, e, out, val):
        self.op(e, lambda: self._ve(e).memset(out.ap, val), [], [out])

    def recip(self, out, in_):
        self.op("dve", lambda: self.nc.vector.reciprocal(out.ap, in_.ap), [in_], [out])

    def finish(self):
        marks = [(e, c) for e, c in self.cnt.items() if c > 0] + [(s.key, s.cnt) for s in self.slots if s.cnt > 0]
        self._wait("sp", marks, True)


class StopBuild(Exception):
    pass


class Ring:
    def __init__(self, K, n, shape, dt, name="r"):
        self.t = [K.tile(shape, dt, name) for _ in range(n)]
        self.i = 0

    def next(self):
        v = self.t[self.i % len(self.t)]
        self.i += 1
        return v


def build_program(nlayers=DEPTH, stage=99):
    holder = {}
    try:
        return _build_program(nlayers, stage, holder)
    except StopBuild:
        holder["K"].finish()
        return holder["nc"]


def _build_program(nlayers, stage, holder):
    nc = bass.Bass("TRN2", target_bir_lowering=False)

    def din(name, shape, dt=F32):
        return V(nc.dram_tensor(name, list(shape), dt, kind="ExternalInput").ap())

    def dout(name, shape, dt=F32):
        return V(nc.dram_tensor(name, list(shape), dt, kind="ExternalOutput").ap())

    x_p = din("x_p", [NP_, D])
    x_s = din("x_s", [NS_, D])
    cak = din("cak", [DEPTH, PAST, 128])
    cav = din("cav", [DEPTH, PAST, 128])
    cnk = din("cnk", [DEPTH, PAST, 512])
    cnv = din("cnv", [DEPTH, PAST, 512])
    sd_in = din("sd", [DEPTH, 2, 8, 64, 64])
    sh_in = din("sh", [DEPTH, 2, 8, 64, 64])
    cvec = din("cvec", [2, D])
    norm_w = din("norm_w", [DEPTH, 2, D])
    ada_w = din("ada_w", [DEPTH, D, 6 * D])
    ada_b = din("ada_b", [DEPTH, 6 * D])
    w_in = din("w_in", [DEPTH, D, N_IN])
    attn_sink = din("attn_sink", [DEPTH, 8])
    delta_conv = din("delta_conv", [DEPTH, 5, 1536])
    delta_a_log = din("delta_a_log", [DEPTH, 16])
    delta_dt_bias = din("delta_dt_bias", [DEPTH, 16])
    delta_norm_w = din("delta_norm_w", [DEPTH, 64])
    hgrn_lb = din("hgrn_lb", [DEPTH, 1024])
    hgrn_norm_w = din("hgrn_norm_w", [DEPTH, 64])
    rpbx = din("rpbx", [DEPTH, 8, 64, 15 * 64])
    w_branch = din("w_branch", [DEPTH, 4, 512, D])
    w_out = din("w_out", [DEPTH, D, D])
    mlp_w1 = din("mlp_w1", [DEPTH, D, 4 * D])
    mlp_w2 = din("mlp_w2", [DEPTH, 4 * D, D])
    final_norm_w = din("final_norm_w", [D])
    consts = din("consts", [128, NCST])
    ropetab = din("ropetab", [64, 2, NS_])

    y_p = dout("y_p", [NP_, D])
    y_s = dout("y_s", [NS_, D])
    nak = dout("nak", [2, DEPTH, 256, 128])
    nav = dout("nav", [2, DEPTH, 256, 128])
    nnk = dout("nnk", [2, DEPTH, 256, 512])
    nnv = dout("nnv", [2, DEPTH, 256, 512])
    nsd = dout("nsd", [2, DEPTH, 2, 8, 64, 64])
    nsh = dout("nsh", [2, DEPTH, 2, 8, 64, 64])

    es = ExitStack()
    with es:
        nc_np = es.enter_context(nc.allow_non_contiguous_dma(reason="small param layouts"))
        K = KB(nc, es)
        holder["K"], holder["nc"] = K, nc
        chk_i = [0]

        def chk(name):
            chk_i[0] += 1
            holder.setdefault("chk", []).append((chk_i[0], name, nc.n_instructions()))
            if chk_i[0] == stage - 100:
                print("STOP at checkpoint", chk_i[0], name)
                raise StopBuild()
        banks = []
        for i in range(8):
            h = es.enter_context(nc.psum_tensor("ps%d" % i, [128, 512], F32))
            banks.append(V(h.ap(), ("ps%d" % i,)))
        bank_i = [0]

        def psum():
            v = banks[bank_i[0] % 6]
            bank_i[0] += 1
            return v

        cst = K.tile([128, NCST], F32, "cst")
        cstb = K.tile([128, NCST], BF16, "cstb")
        s_c = K.slot()
        K.dma("sp", cst, consts, s_c)
        K.cp("dve", cstb, cst)
        if stage == 1:
            K.finish()
            return nc
        identf = cst[:, C_ID:C_ID + 128]
        identb = cstb[:, C_ID:C_ID + 128]
        onesb = cstb[:, C_ONE:C_ONE + 128]
        onesf = cst[:, C_ONE:C_ONE + 128]

        xT = {"P": [K.tile([128, 8, 512], F32, "xP")], "S": [K.tile([128, 8, 512], F32, "xS%d" % i) for i in range(2)]}
        _hS = [K.tile([128, 8, 512], BF16, "hS%d" % i) for i in range(2)]
        hT = {"P": [_hS[0]], "S": _hS}
        NT = {"P": 1, "S": 2}
        TT = {"P": NP_, "S": NS_}
        SEQS = {"P": [(0, 256), (256, 256)], "S": [(0, 1024)]}

        WN = 4
        wbuf = [K.tile([128, 4096], BF16, "w") for _ in range(WN)]
        wslot = [K.slot() for _ in range(WN)]
        w_i = [0]

        def wload(src2d, kc, cols, pat="(kc p) c -> p kc c", q="pool"):
            i = w_i[0] % WN
            w_i[0] += 1
            assert kc * cols <= 4096
            dst = wbuf[i][:, 0:kc * cols].re("p (k c) -> p k c", c=cols)
            rows = src2d.ap.shape[0] // kc
            K.dma(q, dst[0:rows], src2d.re(pat, kc=kc), wslot[i])
            return dst[0:rows]

        stage_slot = [K.slot() for _ in range(4)]
        misc_slot = [K.slot() for _ in range(3)]
        io_slot = [K.slot() for _ in range(6)]

        csf = K.tile([128, 8, 2], F32, "csf")
        csT = K.tile([128, 8, 2], BF16, "csT")
        for gi_ in range(2):
            K.dma("sp", csf[:, :, gi_], cvec[gi_].re("(kc p) -> p kc", p=128), misc_slot)
        K.act(csT, csf, AF.Silu)

        if stage == 2:
            K.finish()
            return nc
        with K.scope():
            xin = Ring(K, 2, [128, D], F32, "xin")
            xs_ = [K.slot(), K.slot()] if False else [stage_slot[0], stage_slot[1]]
            n = 0
            for gname, src in (("P", x_p), ("S", x_s)):
                for tile_i in range(TT[gname] // 128):
                    xt = xin.next()
                    K.dma("sp", xt, src[tile_i * 128:(tile_i + 1) * 128, :], xs_[n % 2])
                    n += 1
                    for half in range(2):
                        ps = psum()
                        for kc4 in range(4):
                            kc = half * 4 + kc4
                            K.tr(ps[:, kc4 * 128:(kc4 + 1) * 128], xt[:, kc * 128:(kc + 1) * 128], identf)
                        tt_i, off = divmod(tile_i * 128, 512)
                        K.cp("act" if half else "dve",
                             xT[gname][tt_i][:, half * 4:half * 4 + 4, off:off + 128],
                             ps.re("p (k c) -> p k c", c=128))

        if stage == 3:
            K.finish()
            return nc
        def rms_mod(g, wm, sh):
            for tt_i in range(NT[g]):
                x = xT[g][tt_i]
                ps = psum()
                for kc in range(8):
                    sq = sqring.next()
                    K.act(sq, x[:, kc, :], AF.Square)
                    K.mm(ps, onesb, sq, start=(kc == 0), stop=(kc == 7))
                rstd = rsring.next()
                K.act(rstd, ps, AF.Sqrt, bias=epsc[:, 0:1], scale=1.0 / D)
                K.recip(rstd, rstd)
                for kc in range(8):
                    tmp = tmpring.next()
                    K.tt("dve" if kc % 2 else "pool", tmp, x[:, kc, :], rstd, ALU.mult)
                    K.act(hT[g][tt_i][:, kc, :], tmp, AF.Identity, bias=sh[:, kc:kc + 1], scale=wm[:, kc:kc + 1])

        epsc = K.tile([128, 1], F32, "eps")
        K.memset("dve", epsc, EPS)
        onec = K.tile([128, 1], F32, "onec")
        K.memset("dve", onec, 1.0)
        sqring = Ring(K, 2, [128, 512], BF16, "sq")
        rsring = Ring(K, 2, [128, 512], F32, "rstd")
        tmpring = Ring(K, 3, [128, 512], F32, "tmp")
        pring = Ring(K, 3, [128, 512], BF16, "pT")

        def proj_fm64(w, c0, rhs_h, out_ps):
            for kc in range(8):
                K.mm(out_ps, w[:, kc, c0:c0 + 64], rhs_h[:, kc, :], start=(kc == 0), stop=(kc == 7), sig=(kc == 7))

        def attention(qv, N, keytiles, outv, esink=None, b3=None):
            def r(v):
                return v if b3 is None else v.re("p (a b) -> p a b", b=b3)
            psn, psd = banks[6], banks[7]
            nkt = len(keytiles)
            for i, (kt, vv, mask, rng, nk) in enumerate(keytiles):
                c0, c1 = rng if rng is not None else (0, N)
                pss = psum()
                qs = qv if rng is None else qv[:, c0:c1]
                K.mm(r(pss[0:nk, c0:c1]), kt, qs)
                pT = pring.next()
                K.act(pT[0:nk, c0:c1], pss[0:nk, c0:c1], AF.Exp, scale=0.125)
                if mask is not None:
                    K.tt("pool", r(pT[0:nk, c0:c1]), r(pT[0:nk, c0:c1]), mask, ALU.mult)
                K.mm(psn[0:64, c0:c1], vv, pT[0:nk, c0:c1], start=(i == 0), stop=(i == nkt - 1))
                K.mm(psd[0:64, c0:c1], onesb[0:nk, 0:64], pT[0:nk, c0:c1], start=(i == 0), stop=(i == nkt - 1))
            den = tmpring.next()
            if esink is not None:
                K.tt("dve", r(den[0:64, 0:N]), r(psd[0:64, 0:N]), esink, ALU.add)
                K.recip(den[0:64, 0:N], den[0:64, 0:N])
            else:
                K.recip(den[0:64, 0:N], psd[0:64, 0:N])
            K.tt("dve", outv, r(psn[0:64, 0:N]), r(den[0:64, 0:N]), ALU.mult)

        def branch_merge(l, g, k, yT, gate1):
            with K.scope():
                mk = K.tile([128, 8, 512], BF16, "mk")
                gt = Ring(K, 2, [128, 512], F32, "gt")
                for tt_i in range(NT[g]):
                    wb = [wload(w_branch[l, k, :, half * 512:(half + 1) * 512], 8, 512) for half in range(2)]
                    wg = [wload(w_in[l, :, O_G + k * D + half * 512:O_G + k * D + (half + 1) * 512], 8, 512) for half in range(2)]
                    for oc in range(8):
                        psy, psg = psum(), psum()
                        for h in range(8):
                            K.mm(psy, wb[oc // 4][:, h, (oc % 4) * 128:(oc % 4 + 1) * 128], yT[:, h, tt_i * 512:(tt_i + 1) * 512],
                                 start=(h == 0), stop=(h == 7), sig=(h == 7))
                        for kc in range(8):
                            K.mm(psg, wg[oc // 4][:, kc, (oc % 4) * 128:(oc % 4 + 1) * 128], hT[g][tt_i][:, kc, :],
                                 start=(kc == 0), stop=(kc == 7), sig=(kc == 7))
                        gg = gt.next()
                        K.act(gg, psg, AF.Sigmoid)
                        K.tt("dve", mk[:, oc, :], psy, gg, ALU.mult)
                    wo = [wload(w_out[l, :, half * 512:(half + 1) * 512], 8, 512) for half in range(2)]
                    for oc2 in range(8):
                        ps = psum()
                        for oc in range(8):
                            K.mm(ps, wo[oc2 // 4][:, oc, (oc2 % 4) * 128:(oc2 % 4 + 1) * 128], mk[:, oc, :],
                                 start=(oc == 0), stop=(oc == 7), sig=(oc == 7))
                        xv = xTn[g][tt_i][:, oc2, :]
                        K.stt("dve", xv, ps, gate1[:, oc2:oc2 + 1], xv, ALU.mult, ALU.add)

        xTn = xT

        for l in range(nlayers):
            with K.scope():
                nw = K.tile([128, 2, 8], F32, "nw")
                for j_ in range(2):
                    K.dma("sp", nw[:, j_, :], norm_w[l, j_].re("(kc p) -> p kc", p=128), misc_slot)
                adab = K.tile([128, 48], F32, "adab")
                K.dma("sp", adab, ada_b[l].re("(c p) -> p c", p=128), misc_slot)
                esk = K.tile([64, 8], F32, "esk")
                K.dma("sp", esk, V(attn_sink.ap[l].partition_broadcast(64)), misc_slot)
                K.act(esk, esk, AF.Exp)
                convw = K.tile([64, 5, 24], F32, "convw")
                for j_ in range(5):
                    K.dma("sp", convw[:, j_, :], delta_conv[l, j_].re("(n d) -> d n", d=64), misc_slot)
                nea = K.tile([64, 16], F32, "nea")
                K.dma("sp", nea, V(delta_a_log.ap[l].partition_broadcast(64)), misc_slot)
                K.act(nea, nea, AF.Exp)
                K.ts("dve", nea, nea, -1.0)
                dtb = K.tile([64, 16], F32, "dtb")
                K.dma("sp", dtb, V(delta_dt_bias.ap[l].partition_broadcast(64)), misc_slot)
                dnw = K.tile([64, 1], F32, "dnw")
                K.dma("sp", dnw, delta_norm_w[l].re("(d o) -> d o", o=1), misc_slot)
                hnw = K.tile([64, 1], F32, "hnw")
                K.dma("sp", hnw, hgrn_norm_w[l].re("(d o) -> d o", o=1), misc_slot)
                def compute_lb(dst, rowbc):
                    shp = [64, 4, 1024] if rowbc else [64, 4, 16]
                    with K.scope():
                        raw = K.tile(shp, F32, "lbraw")
                        for ll in range(4):
                            if not rowbc:
                                K.dma("sp", raw[:, ll, :], hgrn_lb[ll].re("(j d) -> d j", d=64), misc_slot)
                            else:
                                K.dma("sp", raw[:, ll, :], V(hgrn_lb.ap[ll].partition_broadcast(64)), misc_slot)
                        K.act(raw, raw, AF.Exp)
                        tot = K.tile(shp[0:1] + shp[2:], F32, "lbtot")
                        K.tt("dve", tot, raw[:, 0, :], raw[:, 1, :], ALU.add)
                        K.tt("dve", tot, tot, raw[:, 2, :], ALU.add)
                        K.tt("dve", tot, tot, raw[:, 3, :], ALU.add)
                        K.recip(tot, tot)
                        if l == 0:
                            K.memset("dve", dst, 0.0)
                        else:
                            acc = K.tile(shp[0:1] + shp[2:], F32, "lbacc")
                            K.cp("dve", acc, raw[:, 1, :])
                            for ll in range(2, l + 1):
                                K.tt("dve", acc, acc, raw[:, ll, :], ALU.add)
                            K.tt("dve", dst, acc, tot, ALU.mult)

                lbT = K.tile([64, 16], F32, "lbT")
                compute_lb(lbT, False)
                omlbT = K.tile([64, 16], F32, "omlbT")
                K.ts("dve", omlbT, lbT, -1.0, 1.0, ALU.mult, ALU.add)
                chk("params")
                mod = K.tile([128, 48, 2], F32, "mod")
                for cb in range(12):
                    w = wload(ada_w[l, :, cb * 512:(cb + 1) * 512], 8, 512)
                    ps = psum()
                    for cc in range(4):
                        for kc in range(8):
                            K.mm(ps[:, cc * 2:cc * 2 + 2], w[:, kc, cc * 128:(cc + 1) * 128], csT[:, kc, :],
                                 start=(kc == 0), stop=(kc == 7), sig=(kc == 7 and cc == 3))
                    K.tt("dve", mod[:, cb * 4:cb * 4 + 4, :], ps[:, 0:8].re("p (c g) -> p c g", g=2),
                         adab[:, cb * 4:cb * 4 + 4].un(2).bc([128, 4, 2]), ALU.add)
                wm1, wm2, sh1, sh2, g1, g2 = {}, {}, {}, {}, {}, {}
                for gi, g in enumerate(("P", "S")):
                    for (dst, sc_i, nwi) in ((wm1, 1, 0), (wm2, 4, 1)):
                        t = K.tile([128, 8], F32, "wm")
                        K.stt("dve", t, mod[:, sc_i * 8:sc_i * 8 + 8, gi], 1.0, nw[:, nwi, :], ALU.add, ALU.mult)
                        dst[g] = t
                    sh1[g] = mod[:, 0:8, gi]
                    g1[g] = mod[:, 16:24, gi]
                    sh2[g] = mod[:, 24:32, gi]
                    g2[g] = mod[:, 40:48, gi]

                chk("adaln")
                for g in ("P", "S"):
                    T = TT[g]
                    sample = (g == "S")
                    rms_mod(g, wm1[g], sh1[g])
                    chk("rms" + g)
                    with K.scope():
                        qT = K.tile([64, 8, T], BF16, "aq")
                        kT = K.tile([64, 2, T], BF16, "ak")
                        vtm = K.tile([128, T // 128, 128], BF16, "av")
                        yT = K.tile([64, 8, T], BF16, "ya")
                        wq = wload(w_in[l, :, O_AQ:O_AQ + 512], 8, 512)
                        wkv = wload(w_in[l, :, O_AK:O_AK + 256], 8, 256)
                        stg = Ring(K, 2, [128, 256], F32, "stg")
                        if sample:
                            rtab = K.tile([64, 2, NS_], F32, "rtab")
                            K.dma("sp", rtab, ropetab, misc_slot)
                            xbr = Ring(K, 2, [64, 512], BF16, "xb")
                        for tt_i in range(NT[g]):
                            tsl = slice(tt_i * 512, (tt_i + 1) * 512)
                            for hh in range(10):
                                ps = psum()
                                if hh < 8:
                                    proj_fm64(wq, hh * 64, hT[g][tt_i], ps[0:64, :])
                                    dst = qT[:, hh, tsl]
                                else:
                                    proj_fm64(wkv, (hh - 8) * 64, hT[g][tt_i], ps[0:64, :])
                                    dst = kT[:, hh - 8, tsl]
                                if not sample:
                                    K.cp("act", dst, ps[0:64, :])
                                else:
                                    xb = xbr.next()
                                    K.cp("act", xb, ps[0:64, :])
                                    ps2 = psum()
                                    K.mm(ps2[0:64, :], cstb[0:64, C_ROPE:C_ROPE + 64], xb)
                                    t1, t2 = tmpring.next(), tmpring.next()
                                    K.tt("dve", t1[0:64, :], ps2[0:64, :], rtab[:, 1, tsl], ALU.mult)
                                    K.tt("pool", t2[0:64, :], xb, rtab[:, 0, tsl], ALU.mult)
                                    K.tt("dve", dst, t1[0:64, :], t2[0:64, :], ALU.add)
                            chk("Afm" + g)
                            for j in range(4):
                                ps = psum()
                                for kc in range(8):
                                    K.mm(ps[:, 0:256], hT[g][tt_i][:, kc, j * 128:(j + 1) * 128], wkv[:, kc, :],
                                         start=(kc == 0), stop=(kc == 7), sig=(kc == 7))
                                tile_i = tt_i * 4 + j
                                K.cp("dve", vtm[:, tile_i, :], ps[:, 128:256])
                                chk("Atm_mm" + g)
                                if not sample:
                                    st = stg.next()
                                    K.cp("dve", st, ps[:, 0:256])
                                    chk("Atm_cp" + g)
                                    s_, r0 = divmod(tile_i * 128, 256)
                                    K.dma("sp", nak[s_, l, r0:r0 + 128, :], st[:, 0:128], io_slot)
                                    K.dma("sp", nav[s_, l, r0:r0 + 128, :], st[:, 128:256], io_slot)
                        chk("Aproj" + g)
                        if sample:
                            ctm = K.tile([128, 4, 128], BF16, "ctk")
                            cvv = K.tile([128, 4, 128], BF16, "ctv")
                            ckT = K.tile([64, 2, PAST], BF16, "ckT")
                            K.dma("pool", ctm, cak[l].re("(j p) c -> p j c", p=128), stage_slot[2])
                            K.dma("pool", cvv, cav[l].re("(j p) c -> p j c", p=128), stage_slot[3])
                            for kv in range(2):
                                ps = psum()
                                psb = ps
                                for j in range(4):
                                    K.tr(psb[0:64, j * 128:(j + 1) * 128], ctm[:, j, kv * 64:(kv + 1) * 64], identb)
                                K.cp("dve", ckT[:, kv, :], psb[0:64, 0:512])
                        for (t0, L) in SEQS[g]:
                            nblk = L // 128
                            for kv in range(2):
                                for bi in range(nblk):
                                    kts = []
                                    if sample:
                                        for j in range(4):
                                            kts.append((ckT[:, kv, j * 128:(j + 1) * 128], cvv[:, j, kv * 64:(kv + 1) * 64], None, None, 128))
                                        blks = [(bi - 1, C_MPREV), (bi, None), (bi + 1, C_MNEXT)]
                                    else:
                                        blks = [(b, None) for b in range(nblk)]
                                    for (b, mc) in blks:
                                        if b < 0 or b >= nblk:
                                            continue
                                        k0 = t0 + b * 128
                                        m = None if mc is None else cstb[:, mc:mc + 128].un(1).bc([128, 4, 128])
                                        kts.append((kT[:, kv, k0:k0 + 128], vtm[:, k0 // 128, kv * 64:(kv + 1) * 64], m, None, 128))
                                    q0 = t0 + bi * 128
                                    attention(qT[:, 4 * kv:4 * kv + 4, q0:q0 + 128], 512, kts,
                                              yT[:, 4 * kv:4 * kv + 4, q0:q0 + 128],
                                              esk[:, 4 * kv:4 * kv + 4].un(2).bc([64, 4, 128]), b3=128)
                        chk("Aattn" + g)
                        branch_merge(l, g, 0, yT, g1[g])
                        chk("Amerge" + g)
                    with K.scope():
                        yT = K.tile([64, 8, T], BF16, "yd")
                        wdq = wload(w_in[l, :, O_DQ:O_DQ + 512], 8, 512)
                        wdk = wload(w_in[l, :, O_DK:O_DK + 512], 8, 512)
                        wdv = wload(w_in[l, :, O_DV:O_DV + 512], 8, 512)
                        chk("Dw" + g)
                        if sample:
                            vrow = K.tile([64, 16, 512], BF16, "vrow")
                            for r_ in range(16):
                                ps = psum()
                                for kc in range(8):
                                    K.mm(ps[0:64, :], hT[g][r_ // 8][:, kc, (r_ % 8) * 64:(r_ % 8 + 1) * 64], wdv[:, kc, :],
                                         start=(kc == 0), stop=(kc == 7), sig=(kc == 7))
                                K.cp("act", vrow[:, r_, :], ps[0:64, :])
                            cktm = K.tile([128, 4, 512], BF16, "cktm")
                            cvtm = K.tile([128, 4, 512], BF16, "cvtm")
                            K.dma("pool", cktm, cnk[l].re("(j p) c -> p j c", p=128), stage_slot[2])
                            K.dma("pool", cvtm, cnv[l].re("(j p) c -> p j c", p=128), stage_slot[3])
                            ckr = Ring(K, 2, [64, 512], BF16, "ckr")
                            Er = Ring(K, 2, [64, 960], BF16, "Er")
                            Ef = Ring(K, 2, [64, 960], F32, "Ef")
                            colm = cst[0:64, C_COLM:C_COLM + 64]
                        else:
                            stg = Ring(K, 2, [128, 512], F32, "stgd")
                            vtm = K.tile([128, T // 128, 512], BF16, "dvtm")
                            for tile_i in range(T // 128):
                                tt_i, j = divmod(tile_i, 4)
                                for which, w in ((0, wdk), (1, wdv)):
                                    ps = psum()
                                    for kc in range(8):
                                        K.mm(ps, hT[g][tt_i][:, kc, j * 128:(j + 1) * 128], w[:, kc, :],
                                             start=(kc == 0), stop=(kc == 7), sig=(kc == 7))
                                    chk("Dmm" + g)
                                    st = stg.next()
                                    K.cp("act", st, ps)
                                    chk("Dcp" + g)
                                    s_, r0 = divmod(tile_i * 128, 256)
                                    K.dma("sp", (nnk if which == 0 else nnv)[s_, l, r0:r0 + 128, :], st, io_slot)
                                    chk("Ddma" + g)
                                    if which == 1:
                                        K.cp("dve", vtm[:, tile_i, :], ps)
                        chk("Dtm" + g)
                        qh = Ring(K, 2, [64, T], BF16, "dq")
                        kh = Ring(K, 2, [64, T], BF16, "dk")
                        for h in range(8):
                            q_h, k_h = qh.next(), kh.next()
                            for tt_i in range(NT[g]):
                                for (w, dst) in ((wdq, q_h), (wdk, k_h)):
                                    ps = psum()
                                    proj_fm64(w, h * 64, hT[g][tt_i], ps[0:64, :])
                                    K.cp("act", dst[:, tt_i * 512:(tt_i + 1) * 512], ps[0:64, :])
                            chk("Dproj%d" % h + g)
                            if sample:
                                ck = ckr.next()
                                ps = psum()
                                psb = ps
                                for j in range(4):
                                    K.tr(psb[0:64, j * 128:(j + 1) * 128], cktm[:, j, h * 64:(h + 1) * 64], identb)
                                K.cp("dve", ck, psb[0:64, 0:512])
                                ef = Ef.next()
                                K.dma("sp", ef, rpbx[l, h], stage_slot[h % 2])
                                K.act(ef, ef, AF.Exp)
                                E = Er.next()
                                K.tt("dve", E.re("p (e c) -> p e c", c=64), ef.re("p (e c) -> p e c", c=64),
                                     colm.un(1).bc([64, 15, 64]), ALU.mult)
                                for hf_ in range(2):
                                    kts = [(ck[:, j * 128:(j + 1) * 128], cvtm[:, j, h * 64:(h + 1) * 64], None, None, 128) for j in range(4)]
                                    for kr in range(16):
                                        qs = [r_ for r_ in range(8 * hf_, 8 * hf_ + 8)
                                              if min(max(r_ - 4, 0), 8) <= kr < min(max(r_ - 4, 0), 8) + 8]
                                        if not qs:
                                            continue
                                        c0, c1 = (qs[0] - 8 * hf_) * 64, (qs[-1] + 1 - 8 * hf_) * 64
                                        e0 = qs[0] - kr + 7
                                        kts.append((k_h[:, kr * 64:(kr + 1) * 64], vrow[:, kr, h * 64:(h + 1) * 64],
                                                    E[:, e0 * 64:(e0 + len(qs)) * 64], (c0, c1), 64))
                                    attention(q_h[:, hf_ * 512:(hf_ + 1) * 512], 512, kts, yT[:, h, hf_ * 512:(hf_ + 1) * 512])
                            else:
                                for (t0, L) in SEQS[g]:
                                    kts = [(k_h[:, t0 + j * 128:t0 + (j + 1) * 128], vtm[:, (t0 + j * 128) // 128, h * 64:(h + 1) * 64],
                                            None, None, 128) for j in range(L // 128)]
                                    attention(q_h[:, t0:t0 + L], L, kts, yT[:, h, t0:t0 + L])
                                    chk("Dattn%d" % h + g)
                        branch_merge(l, g, 3, yT, g1[g])
                        chk("D" + g)

                    def r3(v):
                        return v.re("p (h c) -> p h c", c=64)

                    def bfv(v):
                        return v.bitcast(BF16)[:, 0:256]

                    nch = T // 64
                    with K.scope():
                        yT = K.tile([64, 8, T], BF16, "yb")
                        gb = K.tile([64, nch, 32], F32, "gb")
                        wab = wload(w_in[l, :, O_BA:O_BA + 32], 8, 32)
                        for c in range(nch):
                            tt_i, off = divmod(c * 64, 512)
                            ps = psum()
                            for kc in range(8):
                                K.mm(ps[0:64, 0:32], hT[g][tt_i][:, kc, off:off + 64], wab[:, kc, :],
                                     start=(kc == 0), stop=(kc == 7), sig=(kc == 7))
                            K.cp("act", gb[:, c, :], ps[0:64, 0:32])
                        gv, bv = gb[:, :, 0:16], gb[:, :, 16:32]
                        K.tt("dve", gv, gv, dtb.un(1).bc([64, nch, 16]), ALU.add)
                        K.act(gv, gv, AF.Exp)
                        K.act(gv, gv, AF.Ln, bias=onec[0:64, 0:1])
                        K.tt("dve", gv, gv, nea.un(1).bc([64, nch, 16]), ALU.mult)
                        K.act(bv, bv, AF.Sigmoid)
                        for hg in range(2):
                            with K.scope():
                                qkv = K.tile([64, 12, T], BF16, "qkv")
                                oT = K.tile([64, 4, T], F32, "oTb")
                                w3 = [wload(w_in[l, :, o_ + hg * 256:o_ + hg * 256 + 256], 8, 256) for o_ in (O_BQ, O_BK, O_BV)]
                                nsq = len(SEQS[g])
                                Ls = SEQS[g][0][1]
                                with K.scope():
                                    rawr = Ring(K, 2, [64, nsq, Ls + 4], F32, "raw")
                                    for rt in rawr.t:
                                        K.memset("pool", rt, 0.0)
                                    cvr = Ring(K, 2, [64, T], F32, "cv")
                                    f64 = Ring(K, 4, [64, 512], F32, "f64")
                                    for which in range(3):
                                        for hi in range(4):
                                            h = hg * 4 + hi
                                            raw = rawr.next()
                                            for tt_i in range(NT[g]):
                                                ps = psum()
                                                proj_fm64(w3[which], hi * 64, hT[g][tt_i], ps[0:64, :])
                                                if sample:
                                                    K.cp("act", raw[:, 0, 2 + tt_i * 512:2 + (tt_i + 1) * 512], ps[0:64, :])
                                                else:
                                                    for s_ in range(2):
                                                        K.cp("act", raw[:, s_, 2:2 + 256], ps[0:64, s_ * 256:(s_ + 1) * 256])
                                            cv = cvr.next()
                                            ch = which * 8 + h
                                            for s_, (t0, L) in enumerate(SEQS[g]):
                                                K.ts("dve", cv[:, t0:t0 + L], raw[:, s_, 0:L], convw[:, 0, ch:ch + 1])
                                                for j in range(1, 5):
                                                    K.stt("dve", cv[:, t0:t0 + L], raw[:, s_, j:j + L],
                                                          convw[:, j, ch:ch + 1], cv[:, t0:t0 + L], ALU.mult, ALU.add)
                                            K.act(cv, cv, AF.Silu)
                                            if which < 2:
                                                for tt_i in range(NT[g]):
                                                    tsl = slice(tt_i * 512, (tt_i + 1) * 512)
                                                    sq = f64.next()
                                                    K.act(sq, cv[:, tsl], AF.Square)
                                                    ps = psum()
                                                    K.mm(ps[0:64, :], onesf[0:64, 0:64], sq)
                                                    rn = f64.next()
                                                    K.act(rn, ps[0:64, :], AF.Sqrt, bias=epsc[0:64, 0:1], scale=1.0)
                                                    K.recip(rn, rn)
                                                    K.stt("dve", qkv[:, which * 4 + hi, tsl], cv[:, tsl],
                                                          (0.125 if which == 0 else 1.0), rn, ALU.mult, ALU.mult)
                                            else:
                                                K.cp("pool", qkv[:, 8 + hi, :], cv)
                                with K.scope():
                                    Lr = Ring(K, 20 if sample else 40, [64, 256], F32, "bL")
                                    Nr = Ring(K, 6 if sample else 14, [64, 256], F32, "bN")
                                    Sst = [K.tile([64, 256], F32, "Sb%d" % d_) for d_ in range(2)]
                                    idf = cst[0:64, C_ID:C_ID + 64]
                                    idb = cstb[0:64, C_ID:C_ID + 64]
                                    one64 = onesf[0:64, 0:64]
                                    HS = [slice(hi * 64, (hi + 1) * 64) for hi in range(4)]
                                    for si, (t0, L) in enumerate(SEQS[g]):
                                        ncs = L // 64
                                        for d_ in range(2):
                                            if sample:
                                                K.dma("sp", r3(Sst[d_]), sd_in[l, d_, hg * 4:hg * 4 + 4].re("h k v -> k h v"), stage_slot[d_])
                                            else:
                                                K.memset("pool", Sst[d_], 0.0)
                                        visited = set()
                                        def chunk_gen(i, d_):
                                            c = i if d_ == 0 else ncs - 1 - i
                                            tok0 = t0 + c * 64
                                            cg = tok0 // 64
                                            csl = slice(tok0, tok0 + 64)
                                            U_ = cst[0:64, C_U[d_]:C_U[d_] + 64]
                                            nm_ = cst[0:64, C_NM[d_]:C_NM[d_] + 64]
                                            nmT_ = cst[0:64, C_NMT[d_]:C_NMT[d_] + 64]
                                            SL_ = cst[0:64, C_SL[d_]:C_SL[d_] + 64]
                                            gcol = gb[:, cg, d_ * 8 + hg * 4:d_ * 8 + hg * 4 + 4]
                                            bcol = gb[:, cg, 16 + d_ * 8 + hg * 4:16 + d_ * 8 + hg * 4 + 4]
                                            S = Sst[d_]
                                            ps = psum()
                                            psb = ps
                                            for hi in range(4):
                                                K.tr(psb[0:64, hi * 64:(hi + 1) * 64], qkv[:, 4 + hi, csl], idb, sig=False)
                                                K.tr(psb[0:64, 256 + hi * 64:256 + (hi + 1) * 64], qkv[:, 8 + hi, csl], idb, sig=(hi == 3))
                                            Ktm, Vtm = Lr.next(), Lr.next()
                                            K.cp("act", Ktm, psb[0:64, 0:256])
                                            yield
                                            K.cp("dve", Vtm, psb[0:64, 256:512])
                                            gU = Lr.next()
                                            K.tt("pool", r3(gU), U_.un(1).bc([64, 4, 64]), gcol.un(2).bc([64, 4, 64]), ALU.mult)
                                            ps1 = psum()
                                            K.mm(ps1[0:64, 0:256], one64, gU, sig=False)
                                            K.mm(ps1[0:64, 256:260], U_, gcol, sig=False)
                                            K.mm(ps1[0:64, 260:264], one64, gcol)
                                            sm = Lr.next()
                                            K.cp("dve", sm[:, 0:8], ps1[0:64, 256:264])
                                            gc, gl = sm[:, 0:4], sm[:, 4:8]
                                            K.act(sm[:, 8:12], gc, AF.Exp)
                                            K.tt("dve", sm[:, 12:16], gl, gc, ALU.subtract)
                                            K.act(sm[:, 12:16], sm[:, 12:16], AF.Exp)
                                            K.act(sm[:, 16:20], gl, AF.Exp)
                                            K.tt("dve", sm[:, 20:24], sm[:, 8:12], bcol, ALU.mult)
                                            be, ekd, cd = sm[:, 20:24], sm[:, 12:16], sm[:, 16:20]
                                            gcrow = Lr.next()
                                            K.cp("act", gcrow, ps1[0:64, 0:256])
                                            yield
                                            G = Lr.next()
                                            K.tt("dve", r3(G), nm_.un(1).bc([64, 4, 64]), r3(gcrow), ALU.subtract)
                                            K.tt("dve", r3(G), r3(G), gc.un(2).bc([64, 4, 64]), ALU.add)
                                            K.act(G, G, AF.Exp)
                                            GT = Lr.next()
                                            K.tt("pool", r3(GT), r3(gcrow), nmT_.un(1).bc([64, 4, 64]), ALU.add)
                                            K.tt("pool", r3(GT), r3(GT), gc.un(2).bc([64, 4, 64]), ALU.subtract)
                                            K.act(GT, GT, AF.Exp)
                                            Erow = Lr.next()
                                            K.act(Erow, gcrow, AF.Exp)
                                            yield
                                            psk = psum()
                                            for hi in range(4):
                                                K.mm(psk[0:64, HS[hi]], qkv[:, 4 + hi, csl], qkv[:, 4 + hi, csl], sig=(hi == 3))
                                            bsl = Lr.next()
                                            K.tt("pool", r3(bsl), SL_.un(1).bc([64, 4, 64]), bcol.un(2).bc([64, 4, 64]), ALU.mult)
                                            A = Lr.next()
                                            K.tt("dve", A, psk[0:64, 0:256], G, ALU.mult)
                                            K.tt("dve", A, A, bsl, ALU.mult)
                                            pst = psum()
                                            for hi in range(4):
                                                K.tr(pst[0:64, HS[hi]], A[:, HS[hi]], idf, sig=(hi == 3))
                                            AT = Lr.next()
                                            K.cp("act", AT, pst[0:64, 0:256])
                                            yield
                                            X = Nr.next()
                                            K.tt("dve", r3(X), idf.un(1).bc([64, 4, 64]), r3(pst[0:64, 0:256]), ALU.subtract)
                                            P_, PT = A, AT
                                            for kk in range(1, 6):
                                                psp = psum()
                                                for hi in range(4):
                                                    K.mm(psp[0:64, HS[hi]], PT[:, HS[hi]], P_[:, HS[hi]], sig=(hi == 3))
                                                Pn = Nr.next()
                                                K.cp("act", Pn, psp[0:64, 0:256])
                                                PTn = None
                                                if kk < 5:
                                                    pspt = psum()
                                                    for hi in range(4):
                                                        K.mm(pspt[0:64, HS[hi]], P_[:, HS[hi]], PT[:, HS[hi]], sig=(hi == 3))
                                                    PTn = Nr.next()
                                                    K.cp("pool", PTn, pspt[0:64, 0:256]) if False else K.cp("dve", PTn, pspt[0:64, 0:256])
                                                psx = psum()
                                                for hi in range(4):
                                                    K.mm(psx[0:64, HS[hi]], Pn[:, HS[hi]], X[:, HS[hi]], sig=(hi == 3))
                                                Xn = Nr.next() if kk < 5 else Lr.next()
                                                yield
                                                K.tt("dve", Xn, psx[0:64, 0:256], X, ALU.add)
                                                P_, PT, X = Pn, PTn, Xn
                                            TT_ = X
                                            Rk, Rv, Kd = Lr.next(), Lr.next(), Lr.next()
                                            K.tt("pool", r3(Rk), r3(Ktm), be.un(2).bc([64, 4, 64]), ALU.mult)
                                            K.tt("pool", r3(Rv), r3(Vtm), bcol.un(2).bc([64, 4, 64]), ALU.mult)
                                            K.tt("pool", r3(Kd), r3(Ktm), ekd.un(2).bc([64, 4, 64]), ALU.mult)
                                            psw = psum()
                                            for hi in range(4):
                                                K.mm(psw[0:64, HS[hi]], Rk[:, HS[hi]], TT_[:, HS[hi]], sig=(hi == 3))
                                            nWkT = Lr.next()
                                            K.ts("dve", nWkT, psw[0:64, 0:256], -1.0)
                                            yield
                                            psq = psum()
                                            for hi in range(4):
                                                K.mm(psq[0:64, HS[hi]], qkv[:, 4 + hi, csl], qkv[:, hi, csl], sig=(hi == 3))
                                            PqkT = Lr.next()
                                            K.tt("dve", PqkT, psq[0:64, 0:256], GT, ALU.mult)
                                            QdT = Lr.next()
                                            K.tt("pool", r3(QdT), qkv[:, 0:4, csl], r3(Erow), ALU.mult)
                                            yield
                                            psu = psum()
                                            for hi in range(4):
                                                K.mm(psu[0:64, HS[hi]], TT_[:, HS[hi]], Rv[:, HS[hi]], start=True, stop=False, sig=False)
                                                K.mm(psu[0:64, HS[hi]], nWkT[:, HS[hi]], S[:, HS[hi]], start=False, stop=True, sig=(hi == 3))
                                            Uc = Lr.next()
                                            yield
                                            K.cp("act", Uc, psu[0:64, 0:256])
                                            pso = psum()
                                            for hi in range(4):
                                                K.mm(pso[0:64, HS[hi]], S[:, HS[hi]], QdT[:, HS[hi]], start=True, stop=False, sig=False)
                                                K.mm(pso[0:64, HS[hi]], Uc[:, HS[hi]], PqkT[:, HS[hi]], start=False, stop=True, sig=(hi == 3))
                                            ov = oT[:, :, csl]
                                            if c not in visited:
                                                K.cp("act", ov, r3(pso[0:64, 0:256]))
                                                visited.add(c)
                                            else:
                                                K.tt("dve", ov, ov, r3(pso[0:64, 0:256]), ALU.add)
                                            psm = psum()
                                            for hi in range(4):
                                                K.mm(psm[0:64, HS[hi]], Kd[:, HS[hi]], Uc[:, HS[hi]], sig=(hi == 3))
                                            K.tt("dve", r3(S), r3(S), cd.un(2).bc([64, 4, 64]), ALU.mult)
                                            K.tt("dve", S, S, psm[0:64, 0:256], ALU.add)
                                            yield
                                            yield
                                        for i in range(ncs):
                                            gens = [chunk_gen(i, 0), chunk_gen(i, 1)]
                                            if sample:
                                                for g_ in gens:
                                                    for _ in g_:
                                                        pass
                                            else:
                                                live = list(gens)
                                                while live:
                                                    for g_ in list(live):
                                                        try:
                                                            next(g_)
                                                        except StopIteration:
                                                            live.remove(g_)
                                        if not sample:
                                            for d_ in range(2):
                                                K.dma("sp", nsd[si, l, d_, hg * 4:hg * 4 + 4].re("h k v -> k h v"), r3(Sst[d_]), io_slot)
                                with K.scope():
                                    f64 = Ring(K, 4, [64, 512], F32, "f64z")
                                    wz = wload(w_in[l, :, O_BZ + hg * 256:O_BZ + hg * 256 + 256], 8, 256)
                                    for hi in range(4):
                                        for tt_i in range(NT[g]):
                                            tsl = slice(tt_i * 512, (tt_i + 1) * 512)
                                            sq = f64.next()
                                            K.act(sq, oT[:, hi, tsl], AF.Square)
                                            ps = psum()
                                            K.mm(ps[0:64, :], one64, sq)
                                            rn = f64.next()
                                            K.act(rn, ps[0:64, :], AF.Sqrt, bias=epsc[0:64, 0:1], scale=1.0 / 64)
                                            K.recip(rn, rn)
                                            psz = psum()
                                            proj_fm64(wz, hi * 64, hT[g][tt_i], psz[0:64, :])
                                            sz = f64.next()
                                            K.act(sz, psz[0:64, :], AF.Silu)
                                            K.tt("dve", rn, oT[:, hi, tsl], rn, ALU.mult)
                                            K.stt("dve", yT[:, hg * 4 + hi, tsl], rn, dnw[:, 0:1], sz, ALU.mult, ALU.mult)
                        branch_merge(l, g, 1, yT, g1[g])
                        chk("B" + g)

                    with K.scope():
                        yT = K.tile([64, 8, T], BF16, "yc")
                        lbR = K.tile([64, 1024], F32, "lbR")
                        omlbR = K.tile([64, 1024], F32, "omlbR")
                        compute_lb(lbR, True)
                        K.ts("dve", omlbR, lbR, -1.0, 1.0, ALU.mult, ALU.add)
                        for hg in range(2):
                            with K.scope():
                                qT = K.tile([64, 4, T], BF16, "cq")
                                kT = K.tile([64, 2, 4, T], BF16, "ckk")
                                oT = K.tile([64, 4, T], F32, "oTc")
                                wcq = wload(w_in[l, :, O_CQ + hg * 256:O_CQ + hg * 256 + 256], 8, 256)
                                wcf = [wload(w_in[l, :, O_CF + d_ * 512 + hg * 256:O_CF + d_ * 512 + hg * 256 + 256], 8, 256) for d_ in range(2)]
                                wci = wload(w_in[l, :, O_CI + hg * 256:O_CI + hg * 256 + 256], 8, 256)
                                with K.scope():
                                    f64 = Ring(K, 4, [64, 512], F32, "f64c")
                                    for hi in range(4):
                                        for tt_i in range(NT[g]):
                                            tsl = slice(tt_i * 512, (tt_i + 1) * 512)
                                            ps = psum()
                                            proj_fm64(wcq, hi * 64, hT[g][tt_i], ps[0:64, :])
                                            K.act(qT[:, hi, tsl], ps[0:64, :], AF.Silu)
                                            for d_ in range(2):
                                                ps = psum()
                                                proj_fm64(wcf[d_], hi * 64, hT[g][tt_i], ps[0:64, :])
                                                sg = f64.next()
                                                K.act(sg, ps[0:64, :], AF.Sigmoid, scale=-1.0)
                                                col = d_ * 8 + hg * 4 + hi
                                                K.ts("dve", kT[:, d_, hi, tsl], sg, omlbT[:, col:col + 1])
                                with K.scope():
                                    Lr = Ring(K, 18 if sample else 36, [64, 256], F32, "cL")
                                    Sst = [K.tile([64, 256], F32, "Sc%d" % d_) for d_ in range(2)]
                                    HS = [slice(hi * 64, (hi + 1) * 64) for hi in range(4)]
                                    for si, (t0, L) in enumerate(SEQS[g]):
                                        ncs = L // 64
                                        for d_ in range(2):
                                            if sample:
                                                K.dma("sp", r3(Sst[d_]), sh_in[l, d_, hg * 4:hg * 4 + 4].re("h k v -> k h v"), stage_slot[d_])
                                            else:
                                                K.memset("pool", Sst[d_], 0.0)
                                        visited = set()
                                        def chunk_gen(i, d_):
                                            c = i if d_ == 0 else ncs - 1 - i
                                            tok0 = t0 + c * 64
                                            csl = slice(tok0, tok0 + 64)
                                            tt_i, off = divmod(tok0, 512)
                                            U_ = cst[0:64, C_U[d_]:C_U[d_] + 64]
                                            W2_ = cst[0:64, C_W2[d_]:C_W2[d_] + 64]
                                            MT_ = cst[0:64, C_MT[d_]:C_MT[d_] + 64]
                                            mid = MID[d_]
                                            last = 63 if d_ == 0 else 0
                                            S = Sst[d_]
                                            psf, psv = psum(), psum()
                                            for kc in range(8):
                                                K.mm(psf[0:64, 0:256], hT[g][tt_i][:, kc, off:off + 64], wcf[d_][:, kc, :],
                                                     start=(kc == 0), stop=(kc == 7), sig=(kc == 7))
                                            for kc in range(8):
                                                K.mm(psv[0:64, 0:256], hT[g][tt_i][:, kc, off:off + 64], wci[:, kc, :],
                                                     start=(kc == 0), stop=(kc == 7), sig=(kc == 7))
                                            Vt = bfv(Lr.next())
                                            K.cp("act", Vt, psv[0:64, 0:256])
                                            yield
                                            f = Lr.next()
                                            K.act(f, psf[0:64, 0:256], AF.Sigmoid)
                                            cs_ = slice(d_ * 512 + hg * 256, d_ * 512 + hg * 256 + 256)
                                            K.tt("dve", f, f, omlbR[:, cs_], ALU.mult)
                                            K.tt("dve", f, f, lbR[:, cs_], ALU.add)
                                            lf = Lr.next()
                                            K.act(lf, f, AF.Ln)
                                            yield
                                            ktm = Lr.next()
                                            K.ts("pool", ktm, f, -1.0, 1.0, ALU.mult, ALU.add)
                                            psb_ = psum()
                                            for hi in range(4):
                                                K.mm(psb_[0:64, HS[hi]], lf[:, HS[hi]], U_, sig=(hi == 3))
                                            psw2 = psum()
                                            K.mm(psw2[0:64, 0:256], W2_, lf)
                                            bT = Lr.next()
                                            K.cp("dve", bT, psb_[0:64, 0:256])
                                            yield
                                            Eb = Lr.next()
                                            K.act(Eb, bT, AF.Exp)
                                            bp = Lr.next()
                                            K.tt("dve", r3(bp), r3(bT), r3(bT)[:, :, mid:mid + 1].bc([64, 4, 64]), ALU.subtract)
                                            Ebp, Ebn = Lr.next(), Lr.next()
                                            K.act(Ebp, bp, AF.Exp)
                                            K.act(Ebn, bp, AF.Exp, scale=-1.0)
                                            yield
                                            QiT, KiT, QdT = bfv(Lr.next()), bfv(Lr.next()), Lr.next()
                                            K.tt("pool", r3(QiT), qT[:, :, csl], r3(Ebp), ALU.mult)
                                            K.tt("pool", r3(KiT), kT[:, d_, :, csl], r3(Ebn), ALU.mult)
                                            K.tt("dve", r3(QdT), qT[:, :, csl], r3(Eb), ALU.mult)
                                            Ekd = Lr.next()
                                            K.act(Ekd, psw2[0:64, 0:256], AF.Exp)
                                            Kd = bfv(Lr.next())
                                            K.tt("dve", Kd, ktm, Ekd, ALU.mult)
                                            yield
                                            psa = psum()
                                            if d_ == 0:
                                                f_t, p_t, p_j, z_j = (32, 64), (0, 32), (0, 32), (32, 64)
                                            else:
                                                f_t, p_t, p_j, z_j = (0, 32), (32, 64), (32, 64), (0, 32)
                                            zer = cstb[0:64, C_ZERO:C_ZERO + 32]
                                            for hi in range(4):
                                                b0 = hi * 64
                                                K.mm(psa[0:64, b0 + f_t[0]:b0 + f_t[1]], KiT[:, HS[hi]], QiT[:, b0 + f_t[0]:b0 + f_t[1]], sig=False)
                                                K.mm(psa[p_j[0]:p_j[1], b0 + p_t[0]:b0 + p_t[1]], KiT[:, b0 + p_j[0]:b0 + p_j[1]],
                                                     QiT[:, b0 + p_t[0]:b0 + p_t[1]], sig=False)
                                                K.mm(psa[z_j[0]:z_j[1], b0 + p_t[0]:b0 + p_t[1]], zer, QiT[:, b0 + p_t[0]:b0 + p_t[1]], sig=(hi == 3))
                                            att = bfv(Lr.next())
                                            K.tt("dve", r3(att), r3(psa[0:64, 0:256]), MT_.un(1).bc([64, 4, 64]), ALU.mult)
                                            yield
                                            pso = psum()
                                            for hi in range(4):
                                                K.mm(pso[0:64, HS[hi]], Vt[:, HS[hi]], att[:, HS[hi]], start=True, stop=False, sig=False)
                                                K.mm(pso[0:64, HS[hi]], S[:, HS[hi]], QdT[:, HS[hi]], start=False, stop=True, sig=(hi == 3))
                                            ov = oT[:, :, csl]
                                            if c not in visited:
                                                K.cp("act", ov, r3(pso[0:64, 0:256]))
                                                visited.add(c)
                                            else:
                                                K.tt("dve", ov, ov, r3(pso[0:64, 0:256]), ALU.add)
                                            psm = psum()
                                            for hi in range(4):
                                                K.mm(psm[0:64, HS[hi]], Kd[:, HS[hi]], Vt[:, HS[hi]], sig=(hi == 3))
                                            K.tt("dve", r3(S), r3(S), r3(Eb)[:, :, last:last + 1].bc([64, 4, 64]), ALU.mult)
                                            K.tt("dve", S, S, psm[0:64, 0:256], ALU.add)
                                            yield
                                            yield
                                        for i in range(ncs):
                                            gens = [chunk_gen(i, 0), chunk_gen(i, 1)]
                                            if sample:
                                                for g_ in gens:
                                                    for _ in g_:
                                                        pass
                                            else:
                                                live = list(gens)
                                                while live:
                                                    for g_ in list(live):
                                                        try:
                                                            next(g_)
                                                        except StopIteration:
                                                            live.remove(g_)
                                        if not sample:
                                            for d_ in range(2):
                                                K.dma("sp", nsh[si, l, d_, hg * 4:hg * 4 + 4].re("h k v -> k h v"), r3(Sst[d_]), io_slot)
                                with K.scope():
                                    f64 = Ring(K, 4, [64, 512], F32, "f64g")
                                    wcg = wload(w_in[l, :, O_CG + hg * 256:O_CG + hg * 256 + 256], 8, 256)
                                    for hi in range(4):
                                        for tt_i in range(NT[g]):
                                            tsl = slice(tt_i * 512, (tt_i + 1) * 512)
                                            psg = psum()
                                            proj_fm64(wcg, hi * 64, hT[g][tt_i], psg[0:64, :])
                                            sg = f64.next()
                                            K.act(sg, psg[0:64, :], AF.Sigmoid)
                                            K.tt("dve", sg, oT[:, hi, tsl], sg, ALU.mult)
                                            sq = f64.next()
                                            K.act(sq, sg, AF.Square)
                                            ps = psum()
                                            K.mm(ps[0:64, :], onesf[0:64, 0:64], sq)
                                            rn = f64.next()
                                            K.act(rn, ps[0:64, :], AF.Sqrt, bias=epsc[0:64, 0:1], scale=1.0 / 64)
                                            K.recip(rn, rn)
                                            K.stt("dve", yT[:, hg * 4 + hi, tsl], sg, hnw[:, 0:1], rn, ALU.mult, ALU.mult)
                        branch_merge(l, g, 2, yT, g1[g])
                        chk("C" + g)
                    rms_mod(g, wm2[g], sh2[g])
                    with K.scope():
                        uT = K.tile([128, 4, 512], BF16, "uT")
                        rr = Ring(K, 2, [128, 512], F32, "relu")
                        for hb in range(8):
                            w1 = wload(mlp_w1[l, :, hb * 512:(hb + 1) * 512], 8, 512)
                            w2 = [wload(mlp_w2[l, hb * 512:(hb + 1) * 512, half * 512:(half + 1) * 512], 4, 512) for half in range(2)]
                            for tt_i in range(NT[g]):
                                for hc in range(4):
                                    ps = psum()
                                    for kc in range(8):
                                        K.mm(ps, w1[:, kc, hc * 128:(hc + 1) * 128], hT[g][tt_i][:, kc, :],
                                             start=(kc == 0), stop=(kc == 7), sig=(kc == 7))
                                    r = rr.next()
                                    K.act(r, ps, AF.Relu)
                                    K.tt("pool", uT[:, hc, :], r, r, ALU.mult)
                                for oc in range(8):
                                    ps = psum()
                                    for hc in range(4):
                                        K.mm(ps, w2[oc // 4][:, hc, (oc % 4) * 128:(oc % 4 + 1) * 128], uT[:, hc, :],
                                             start=(hc == 0), stop=(hc == 3), sig=(hc == 3))
                                    xv = xT[g][tt_i][:, oc, :]
                                    K.stt("dve", xv, ps, g2[g][:, oc:oc + 1], xv, ALU.mult, ALU.add)

        chk("layers")
        with K.scope():
            fw = K.tile([128, 8], F32, "fw")
            K.dma("sp", fw, final_norm_w.re("(kc p) -> p kc", p=128), misc_slot)
            zero8 = K.tile([128, 8], F32, "z8")
            K.memset("dve", zero8, 0.0)
            yst = Ring(K, 2, [128, D], F32, "yst")
            hf = {"P": [K.tile([128, 8, 512], F32, "hfP")], "S": [K.tile([128, 8, 512], F32, "hfS%d" % i) for i in range(2)]}
            for g, dst in (("P", y_p), ("S", y_s)):
                for tt_i in range(NT[g]):
                    x = xT[g][tt_i]
                    ps = psum()
                    for kc in range(8):
                        sq = sqring.next()
                        K.act(sq, x[:, kc, :], AF.Square)
                        K.mm(ps, onesb, sq, start=(kc == 0), stop=(kc == 7))
                    rstd = rsring.next()
                    K.act(rstd, ps, AF.Sqrt, bias=epsc[:, 0:1], scale=1.0 / D)
                    K.recip(rstd, rstd)
                    for kc in range(8):
                        K.stt("dve", hf[g][tt_i][:, kc, :], x[:, kc, :], fw[:, kc:kc + 1], rstd, ALU.mult, ALU.mult)
                    for j in range(4):
                        yt = yst.next()
                        for half in range(2):
                            ps = psum()
                            for kc4 in range(4):
                                kc = half * 4 + kc4
                                K.tr(ps[:, kc4 * 128:(kc4 + 1) * 128], hf[g][tt_i][:, kc, j * 128:(j + 1) * 128], identf)
                            K.cp("act" if half else "dve", yt[:, half * 512:(half + 1) * 512], ps)
                        r0 = tt_i * 512 + j * 128
                        K.dma("sp", dst[r0:r0 + 128, :], yt, io_slot)
        K.finish()
    return nc


_CACHE = {}


def kernel(**inp):
    f = lambda k: np.ascontiguousarray(np.asarray(inp[k], dtype=np.float32))
    if "nc" not in _CACHE:
        _CACHE["nc"] = build_program()
    nc = _CACHE["nc"]
    rpb = f("na_rpb")
    kc = np.arange(64)[:, None, None]
    e = np.arange(15)[None, :, None]
    qc = np.arange(64)[None, None, :]
    dc = np.clip(kc - qc + 15, 0, 30) + 0 * e
    rr = (14 - e) + 0 * dc
    rpbx = np.ascontiguousarray(rpb[:, :, rr, dc]).reshape(DEPTH, 8, 64, 15 * 64)
    shared = {
        "norm_w": f("norm_w"), "ada_w": f("ada_w"), "ada_b": f("ada_b"), "w_in": f("w_in"),
        "attn_sink": f("attn_sink"), "delta_conv": f("delta_conv"),
        "delta_a_log": f("delta_a_log").reshape(DEPTH, 16), "delta_dt_bias": f("delta_dt_bias").reshape(DEPTH, 16),
        "delta_norm_w": f("delta_norm_w"), "hgrn_lb": f("hgrn_lb").reshape(DEPTH, 1024),
        "hgrn_norm_w": f("hgrn_norm_w"), "rpbx": rpbx, "w_branch": f("w_branch"), "w_out": f("w_out"),
        "mlp_w1": f("mlp_w1"), "mlp_w2": f("mlp_w2"), "final_norm_w": f("final_norm_w"),
        "consts": make_consts(), "ropetab": make_rope_tab(),
    }
    xp, xs = f("x_prompt"), f("x_sample")
    cak, cav, cnk, cnv = f("cache_attn_k"), f("cache_attn_v"), f("cache_na_k"), f("cache_na_v")
    sd, sh, c, cctx = f("state_delta"), f("state_hgrn"), f("c"), f("c_ctx")
    in_maps = []
    for i in range(NCORES):
        m = dict(shared)
        m["x_p"] = xp[2 * i:2 * i + 2].reshape(NP_, D)
        m["x_s"] = xs[i]
        m["cak"] = cak[i].reshape(DEPTH, PAST, 128)
        m["cav"] = cav[i].reshape(DEPTH, PAST, 128)
        m["cnk"] = cnk[i].reshape(DEPTH, PAST, 512)
        m["cnv"] = cnv[i].reshape(DEPTH, PAST, 512)
        m["sd"] = sd[i]
        m["sh"] = sh[i]
        m["cvec"] = np.stack([cctx, c[i]], 0)
        in_maps.append(m)
    res = run_bass_kernel_spmd(nc, in_maps, core_ids=list(range(NCORES)))
    R = res.results
    cat = lambda k: np.concatenate([np.asarray(r[k]) for r in R], 0)
    y_prompt = cat("y_p").reshape(16, 256, D)
    y_sample = cat("y_s").reshape(8, 1024, D)
    nak = cat("nak").reshape(16, DEPTH, 256, 2, 64)
    nav = cat("nav").reshape(16, DEPTH, 256, 2, 64)
    nnk = cat("nnk").reshape(16, DEPTH, 256, 8, 64)
    nnv = cat("nnv").reshape(16, DEPTH, 256, 8, 64)
    nsd = cat("nsd").reshape(16, DEPTH, 2, 8, 64, 64)
    nsh = cat("nsh").reshape(16, DEPTH, 2, 8, 64, 64)
    return tuple(np.ascontiguousarray(a, dtype=np.float32) for a in (y_prompt, y_sample, nak, nav, nnk, nnv, nsd, nsh))
```

```python
import math
from contextlib import ExitStack, contextmanager
import numpy as np
import ml_dtypes
import concourse.bass as bass
import concourse.mybir as mybir
from concourse.bass_utils import run_bass_kernel_spmd

F32 = mybir.dt.float32
BF16 = mybir.dt.bfloat16
ALU = mybir.AluOpType
AF = mybir.ActivationFunctionType

D = 1024
DEPTH = 4
NCORES = 8
NP_ = 512
NS_ = 1024
PAST = 512
EPS = 1e-6
NEGM = -30000.0
O_AQ, O_AK, O_AV = 0, 512, 640
O_BQ, O_BK, O_BV, O_BZ, O_BA, O_BB = 768, 1280, 1792, 2304, 2816, 2832
O_CQ, O_CF, O_CI, O_CG = 2848, 3360, 4384, 4896
O_DQ, O_DK, O_DV, O_G = 5408, 5920, 6432, 6944
N_IN = 11040

C_ID, C_ONE = 0, 128
C_U = (256, 320)
C_W2 = (384, 448)
C_NM = (512, 576)
C_NMT = (640, 704)
C_SL = (768, 832)
C_MT = (896, 960)
C_ROPE, C_COLM, C_MPREV, C_MNEXT = 1024, 1088, 1152, 1280
C_ZERO = 1408
NCST = 1472
MID = (32, 31)


def make_consts():
    c = np.zeros((128, NCST), np.float32)
    c[:, C_ID:C_ID + 128] = np.eye(128)
    c[:, C_ONE:C_ONE + 128] = 1.0
    t = np.arange(64)
    for d in range(2):
        if d == 0:
            U = (t[:, None] <= t[None, :]).astype(np.float32)
        else:
            U = (t[:, None] >= t[None, :]).astype(np.float32)
        c[:64, C_U[d]:C_U[d] + 64] = U
        c[:64, C_W2[d]:C_W2[d] + 64] = 1.0 - U
        incl = U.T
        c[:64, C_NM[d]:C_NM[d] + 64] = np.where(incl > 0, 0.0, NEGM)
        c[:64, C_NMT[d]:C_NMT[d] + 64] = np.where(incl.T > 0, 0.0, NEGM)
        c[:64, C_SL[d]:C_SL[d] + 64] = incl - np.eye(64)
        c[:64, C_MT[d]:C_MT[d] + 64] = incl.T
    P = np.zeros((64, 64), np.float32)
    for half in (0, 32):
        for i in range(16):
            P[half + i, half + i + 16] = -1.0
            P[half + 16 + i, half + i] = 1.0
    c[:64, C_ROPE:C_ROPE + 64] = P.T
    qc = np.arange(64)
    ws = np.clip(qc - 8, 0, 48)
    kc = np.arange(64)
    c[:64, C_COLM:C_COLM + 64] = ((kc[:, None] >= ws[None, :]) & (kc[:, None] < ws[None, :] + 16)).astype(np.float32)
    k = np.arange(128)
    c[:, C_MPREV:C_MPREV + 128] = (k[:, None] >= k[None, :]).astype(np.float32)
    c[:, C_MNEXT:C_MNEXT + 128] = (k[:, None] <= k[None, :]).astype(np.float32)
    return c


def make_rope_tab():
    tt = np.arange(NS_)
    inv = (10000.0 ** (-np.arange(16, dtype=np.float32) / 16)).astype(np.float32)
    tab = np.zeros((64, 2, NS_), np.float32)
    for half, pos in ((0, tt // 64), (32, tt % 64)):
        ang = pos.astype(np.float32)[None, :] * inv[:, None]
        cs, sn = np.cos(ang).astype(np.float32), np.sin(ang).astype(np.float32)
        tab[half:half + 16, 0], tab[half + 16:half + 32, 0] = cs, cs
        tab[half:half + 16, 1], tab[half + 16:half + 32, 1] = sn, sn
    return tab


class V:
    __slots__ = ("ap", "toks")

    def __init__(self, ap, toks=()):
        self.ap = ap
        self.toks = toks

    def __getitem__(self, idx):
        return V(self.ap[idx], self.toks)

    def bc(self, shape):
        return V(self.ap.broadcast_to(list(shape)), self.toks)

    def un(self, axis):
        return V(self.ap.unsqueeze(axis), self.toks)

    def re(self, pat, **kw):
        return V(self.ap.rearrange(pat, **kw), self.toks)

    def bitcast(self, dt):
        return V(self.ap.bitcast(dt), self.toks)


class Slot:
    def __init__(self, key, sem):
        self.key, self.sem, self.cnt = key, sem, 0


class KB:
    def __init__(self, nc, es):
        self.nc = nc
        self.es = es
        self.eng = {"pe": nc.tensor, "act": nc.scalar, "dve": nc.vector, "pool": nc.gpsimd, "sp": nc.sync}
        self.semh = {}
        self.cnt = {}
        for e in ("pe", "act", "dve", "pool"):
            self.semh[e] = es.enter_context(nc.semaphore("s_" + e))
            self.cnt[e] = 0
        self.seen = {e: {} for e in self.eng}
        self.lastw = {}
        self.readers = {}
        self.slots = []
        self.ntile = 0
        self.scopes = []

    def tile(self, shape, dt, name=None):
        self.ntile += 1
        nm = "%s_%d" % (name or "t", self.ntile)
        st = self.scopes[-1] if self.scopes else self.es
        h = st.enter_context(self.nc.sbuf_tensor(nm, list(shape), dt))
        v = V(h.ap(), (nm,))
        return v

    def slot(self):
        s = Slot("d%d" % len(self.slots), self.es.enter_context(self.nc.semaphore("sd%d" % len(self.slots))))
        self.semh[s.key] = s.sem
        self.slots.append(s)
        return s

    @contextmanager
    def scope(self):
        st = ExitStack()
        self.scopes.append(st)
        try:
            yield
        finally:
            self.barrier()
            self.scopes.pop()
            st.close()

    def barrier(self):
        marks = [(e, c) for e, c in self.cnt.items() if c > 0] + [(s.key, s.cnt) for s in self.slots if s.cnt > 0]
        for e in ("pe", "act", "dve", "pool", "sp"):
            self._wait(e, marks, True)

    def _wait(self, e, marks, full=False):
        need = {}
        for (k, v) in marks:
            if k == e and e == "pe":
                continue
            if need.get(k, 0) < v:
                need[k] = v
        for k, v in need.items():
            if self.seen[e].get(k, 0) < v:
                self.eng[e].wait_ge(self.semh[k], v)
                self.seen[e][k] = v

    def _deps(self, reads, writes, eng=None):
        marks = []
        for t in reads:
            if t in self.lastw:
                marks.append(self.lastw[t])
        skip = eng if eng in ("act", "dve", "pool") else None
        for t in writes:
            if t in reads:
                if t in self.lastw:
                    marks.append(self.lastw[t])
            elif t in self.lastw and self.lastw[t][0] != skip:
                marks.append(self.lastw[t])
            marks.extend(m for m in self.readers.get(t, {}).items() if m[0] != skip)
        return marks

    def _record(self, mark, reads, writes):
        for t in reads:
            r = self.readers.setdefault(t, {})
            if r.get(mark[0], 0) < mark[1]:
                r[mark[0]] = mark[1]
        for t in writes:
            self.lastw[t] = mark
            self.readers[t] = {}

    def op(self, e, fn, ins, outs, sig=True):
        reads = [t for v in ins for t in v.toks]
        writes = [t for v in outs for t in v.toks]
        writes = writes + [t for t in reads if t.startswith("ps") and t not in writes]
        self._wait(e, self._deps(reads, writes, e))
        inst = fn()
        if sig:
            self.cnt[e] += 1
            inst.then_inc(self.semh[e], 1)
            mark = (e, self.cnt[e])
        else:
            mark = (e, self.cnt[e] + 1)
        self._record(mark, reads, writes)

    def dma(self, q, out, in_, slot, **kw):
        reads, writes = list(in_.toks), list(out.toks)
        if isinstance(slot, list):
            slot.append(slot.pop(0))
            slot = slot[-1]
        marks = self._deps(reads, writes)
        if slot.cnt > 0:
            marks.append((slot.key, slot.cnt))
        self._wait(q, marks)
        inst = self.eng[q].dma_start(out=out.ap, in_=in_.ap, **kw)
        slot.cnt += 16
        inst.then_inc(slot.sem, 16)
        self._record((slot.key, slot.cnt), reads, writes)

    def mm(self, out, lhsT, rhs, start=True, stop=True, sig=True):
        self.op("pe", lambda: self.nc.tensor.matmul(out.ap, lhsT.ap, rhs.ap, start=start, stop=stop),
                [lhsT, rhs] + ([] if start else [out]), [out], sig)

    def tr(self, out, in_, ident, sig=True):
        self.mm(out, in_, ident, sig=sig)

    def act(self, out, in_, func, bias=0.0, scale=1.0):
        ins = [in_] + [x for x in (bias, scale) if isinstance(x, V)]
        b = bias.ap if isinstance(bias, V) else bias
        s = scale.ap if isinstance(scale, V) else scale
        self.op("act", lambda: self.nc.scalar.activation(out.ap, in_.ap, func, bias=b, scale=s), ins, [out])

    def _ve(self, e):
        return self.nc.vector if e == "dve" else self.nc.gpsimd

    def tt(self, e, out, in0, in1, op):
        self.op(e, lambda: self._ve(e).tensor_tensor(out.ap, in0.ap, in1.ap, op), [in0, in1], [out])

    def ts(self, e, out, in0, s1, s2=None, op0=ALU.mult, op1=None):
        ins = [in0] + [x for x in (s1, s2) if isinstance(x, V)]
        a = s1.ap if isinstance(s1, V) else s1
        b = s2.ap if isinstance(s2, V) else s2
        if op1 is None:
            self.op(e, lambda: self._ve(e).tensor_scalar(out.ap, in0.ap, a, None, op0), ins, [out])
        else:
            self.op(e, lambda: self._ve(e).tensor_scalar(out.ap, in0.ap, a, b, op0, op1), ins, [out])

    def stt(self, e, out, in0, sc, in1, op0, op1):
        ins = [in0, in1] + ([sc] if isinstance(sc, V) else [])
        a = sc.ap if isinstance(sc, V) else sc
        self.op(e, lambda: self._ve(e).scalar_tensor_tensor(out.ap, in0.ap, a, in1.ap, op0, op1), ins, [out])

    def cp(self, e, out, in_):
        if e == "act":
            self.op("act", lambda: self.nc.scalar.copy(out.ap, in_.ap), [in_], [out])
        else:
            self.op(e, lambda: self._ve(e).tensor_copy(out.ap, in_.ap), [in_], [out])

    def memset(self, e, out, val):
        self.op(e, lambda: self._ve(e).memset(out.ap, val), [], [out])

    def recip(self, out, in_):
        self.op("dve", lambda: self.nc.vector.reciprocal(out.ap, in_.ap), [in_], [out])

    def finish(self):
        marks = [(e, c) for e, c in self.cnt.items() if c > 0] + [(s.key, s.cnt) for s in self.slots if s.cnt > 0]
        self._wait("sp", marks, True)


class StopBuild(Exception):
    pass


class Ring:
    def __init__(self, K, n, shape, dt, name="r"):
        self.t = [K.tile(shape, dt, name) for _ in range(n)]
        self.i = 0

    def next(self):
        v = self.t[self.i % len(self.t)]
        self.i += 1
        return v


def build_program(nlayers=DEPTH, stage=99):
    holder = {}
    try:
        return _build_program(nlayers, stage, holder)
    except StopBuild:
        holder["K"].finish()
        return holder["nc"]


def _build_program(nlayers, stage, holder):
    nc = bass.Bass("TRN2", target_bir_lowering=False)

    def din(name, shape, dt=F32):
        return V(nc.dram_tensor(name, list(shape), dt, kind="ExternalInput").ap())

    def dout(name, shape, dt=F32):
        return V(nc.dram_tensor(name, list(shape), dt, kind="ExternalOutput").ap())

    x_p = din("x_p", [NP_, D])
    x_s = din("x_s", [NS_, D])
    cak = din("cak", [DEPTH, PAST, 128])
    cav = din("cav", [DEPTH, PAST, 128])
    cnk = din("cnk", [DEPTH, PAST, 512])
    cnv = din("cnv", [DEPTH, PAST, 512])
    sd_in = din("sd", [DEPTH, 2, 8, 64, 64])
    sh_in = din("sh", [DEPTH, 2, 8, 64, 64])
    cvec = din("cvec", [2, D])
    norm_w = din("norm_w", [DEPTH, 2, D])
    ada_w = din("ada_w", [DEPTH, D, 6 * D])
    ada_b = din("ada_b", [DEPTH, 6 * D])
    w_in = din("w_in", [DEPTH, D, N_IN])
    attn_sink = din("attn_sink", [DEPTH, 8])
    delta_conv = din("delta_conv", [DEPTH, 5, 1536])
    delta_a_log = din("delta_a_log", [DEPTH, 16])
    delta_dt_bias = din("delta_dt_bias", [DEPTH, 16])
    delta_norm_w = din("delta_norm_w", [DEPTH, 64])
    hgrn_lb = din("hgrn_lb", [DEPTH, 1024])
    hgrn_norm_w = din("hgrn_norm_w", [DEPTH, 64])
    rpbx = din("rpbx", [DEPTH, 8, 64, 15 * 64])
    w_branch = din("w_branch", [DEPTH, 4, 512, D])
    w_out = din("w_out", [DEPTH, D, D])
    mlp_w1 = din("mlp_w1", [DEPTH, D, 4 * D])
    mlp_w2 = din("mlp_w2", [DEPTH, 4 * D, D])
    final_norm_w = din("final_norm_w", [D])
    consts = din("consts", [128, NCST])
    ropetab = din("ropetab", [64, 2, NS_])

    y_p = dout("y_p", [NP_, D])
    y_s = dout("y_s", [NS_, D])
    nak = dout("nak", [2, DEPTH, 256, 128])
    nav = dout("nav", [2, DEPTH, 256, 128])
    nnk = dout("nnk", [2, DEPTH, 256, 512])
    nnv = dout("nnv", [2, DEPTH, 256, 512])
    nsd = dout("nsd", [2, DEPTH, 2, 8, 64, 64])
    nsh = dout("nsh", [2, DEPTH, 2, 8, 64, 64])

    es = ExitStack()
    with es:
        nc_np = es.enter_context(nc.allow_non_contiguous_dma(reason="small param layouts"))
        K = KB(nc, es)
        holder["K"], holder["nc"] = K, nc
        chk_i = [0]

        def chk(name):
            chk_i[0] += 1
            holder.setdefault("chk", []).append((chk_i[0], name, nc.n_instructions()))
            if chk_i[0] == stage - 100:
                print("STOP at checkpoint", chk_i[0], name)
                raise StopBuild()
        banks = []
        for i in range(8):
            h = es.enter_context(nc.psum_tensor("ps%d" % i, [128, 512], F32))
            banks.append(V(h.ap(), ("ps%d" % i,)))
        bank_i = [0]

        def psum():
            v = banks[bank_i[0] % 6]
            bank_i[0] += 1
            return v

        cst = K.tile([128, NCST], F32, "cst")
        cstb = K.tile([128, NCST], BF16, "cstb")
        s_c = K.slot()
        K.dma("sp", cst, consts, s_c)
        K.cp("dve", cstb, cst)
        if stage == 1:
            K.finish()
            return nc
        identf = cst[:, C_ID:C_ID + 128]
        identb = cstb[:, C_ID:C_ID + 128]
        onesb = cstb[:, C_ONE:C_ONE + 128]
        onesf = cst[:, C_ONE:C_ONE + 128]

        xT = {"P": [K.tile([128, 8, 512], F32, "xP")], "S": [K.tile([128, 8, 512], F32, "xS%d" % i) for i in range(2)]}
        _hS = [K.tile([128, 8, 512], BF16, "hS%d" % i) for i in range(2)]
        hT = {"P": [_hS[0]], "S": _hS}
        NT = {"P": 1, "S": 2}
        TT = {"P": NP_, "S": NS_}
        SEQS = {"P": [(0, 256), (256, 256)], "S": [(0, 1024)]}

        WN = 4
        wbuf = [K.tile([128, 4096], BF16, "w") for _ in range(WN)]
        wslot = [K.slot() for _ in range(WN)]
        w_i = [0]

        def wload(src2d, kc, cols, pat="(kc p) c -> p kc c", q="pool"):
            i = w_i[0] % WN
            w_i[0] += 1
            assert kc * cols <= 4096
            dst = wbuf[i][:, 0:kc * cols].re("p (k c) -> p k c", c=cols)
            rows = src2d.ap.shape[0] // kc
            K.dma(q, dst[0:rows], src2d.re(pat, kc=kc), wslot[i])
            return dst[0:rows]

        stage_slot = [K.slot() for _ in range(4)]
        misc_slot = [K.slot() for _ in range(3)]
        io_slot = [K.slot() for _ in range(6)]

        csf = K.tile([128, 8, 2], F32, "csf")
        csT = K.tile([128, 8, 2], BF16, "csT")
        for gi_ in range(2):
            K.dma("sp", csf[:, :, gi_], cvec[gi_].re("(kc p) -> p kc", p=128), misc_slot)
        K.act(csT, csf, AF.Silu)

        if stage == 2:
            K.finish()
            return nc
        with K.scope():
            xin = Ring(K, 2, [128, D], F32, "xin")
            xs_ = [K.slot(), K.slot()] if False else [stage_slot[0], stage_slot[1]]
            n = 0
            for gname, src in (("P", x_p), ("S", x_s)):
                for tile_i in range(TT[gname] // 128):
                    xt = xin.next()
                    K.dma("sp", xt, src[tile_i * 128:(tile_i + 1) * 128, :], xs_[n % 2])
                    n += 1
                    for half in range(2):
                        ps = psum()
                        for kc4 in range(4):
                            kc = half * 4 + kc4
                            K.tr(ps[:, kc4 * 128:(kc4 + 1) * 128], xt[:, kc * 128:(kc + 1) * 128], identf)
                        tt_i, off = divmod(tile_i * 128, 512)
                        K.cp("act" if half else "dve",
                             xT[gname][tt_i][:, half * 4:half * 4 + 4, off:off + 128],
                             ps.re("p (k c) -> p k c", c=128))

        if stage == 3:
            K.finish()
            return nc
        def rms_mod(g, wm, sh):
            for tt_i in range(NT[g]):
                x = xT[g][tt_i]
                ps = psum()
                for kc in range(8):
                    sq = sqring.next()
                    K.act(sq, x[:, kc, :], AF.Square)
                    K.mm(ps, onesb, sq, start=(kc == 0), stop=(kc == 7))
                rstd = rsring.next()
                K.act(rstd, ps, AF.Sqrt, bias=epsc[:, 0:1], scale=1.0 / D)
                K.recip(rstd, rstd)
                for kc in range(8):
                    tmp = tmpring.next()
                    K.tt("dve" if kc % 2 else "pool", tmp, x[:, kc, :], rstd, ALU.mult)
                    K.act(hT[g][tt_i][:, kc, :], tmp, AF.Identity, bias=sh[:, kc:kc + 1], scale=wm[:, kc:kc + 1])

        epsc = K.tile([128, 1], F32, "eps")
        K.memset("dve", epsc, EPS)
        onec = K.tile([128, 1], F32, "onec")
        K.memset("dve", onec, 1.0)
        sqring = Ring(K, 2, [128, 512], BF16, "sq")
        rsring = Ring(K, 2, [128, 512], F32, "rstd")
        tmpring = Ring(K, 3, [128, 512], F32, "tmp")
        pring = Ring(K, 3, [128, 512], BF16, "pT")

        def proj_fm64(w, c0, rhs_h, out_ps):
            for kc in range(8):
                K.mm(out_ps, w[:, kc, c0:c0 + 64], rhs_h[:, kc, :], start=(kc == 0), stop=(kc == 7), sig=(kc == 7))

        def attention(qv, N, keytiles, outv, esink=None, b3=None):
            def r(v):
                return v if b3 is None else v.re("p (a b) -> p a b", b=b3)
            psn, psd = banks[6], banks[7]
            nkt = len(keytiles)
            for i, (kt, vv, mask, rng, nk) in enumerate(keytiles):
                c0, c1 = rng if rng is not None else (0, N)
                pss = psum()
                qs = qv if rng is None else qv[:, c0:c1]
                K.mm(r(pss[0:nk, c0:c1]), kt, qs)
                pT = pring.next()
                K.act(pT[0:nk, c0:c1], pss[0:nk, c0:c1], AF.Exp, scale=0.125)
                if mask is not None:
                    K.tt("pool", r(pT[0:nk, c0:c1]), r(pT[0:nk, c0:c1]), mask, ALU.mult)
                K.mm(psn[0:64, c0:c1], vv, pT[0:nk, c0:c1], start=(i == 0), stop=(i == nkt - 1))
                K.mm(psd[0:64, c0:c1], onesb[0:nk, 0:64], pT[0:nk, c0:c1], start=(i == 0), stop=(i == nkt - 1))
            den = tmpring.next()
            if esink is not None:
                K.tt("dve", r(den[0:64, 0:N]), r(psd[0:64, 0:N]), esink, ALU.add)
                K.recip(den[0:64, 0:N], den[0:64, 0:N])
            else:
                K.recip(den[0:64, 0:N], psd[0:64, 0:N])
            K.tt("dve", outv, r(psn[0:64, 0:N]), r(den[0:64, 0:N]), ALU.mult)

        def branch_merge(l, g, k, yT, gate1):
            with K.scope():
                mk = K.tile([128, 8, 512], BF16, "mk")
                gt = Ring(K, 2, [128, 512], F32, "gt")
                for tt_i in range(NT[g]):
                    wb = [wload(w_branch[l, k, :, half * 512:(half + 1) * 512], 8, 512) for half in range(2)]
                    wg = [wload(w_in[l, :, O_G + k * D + half * 512:O_G + k * D + (half + 1) * 512], 8, 512) for half in range(2)]
                    for oc in range(8):
                        psy, psg = psum(), psum()
                        for h in range(8):
                            K.mm(psy, wb[oc // 4][:, h, (oc % 4) * 128:(oc % 4 + 1) * 128], yT[:, h, tt_i * 512:(tt_i + 1) * 512],
                                 start=(h == 0), stop=(h == 7), sig=(h == 7))
                        for kc in range(8):
                            K.mm(psg, wg[oc // 4][:, kc, (oc % 4) * 128:(oc % 4 + 1) * 128], hT[g][tt_i][:, kc, :],
                                 start=(kc == 0), stop=(kc == 7), sig=(kc == 7))
                        gg = gt.next()
                        K.act(gg, psg, AF.Sigmoid)
                        K.tt("dve", mk[:, oc, :], psy, gg, ALU.mult)
                    wo = [wload(w_out[l, :, half * 512:(half + 1) * 512], 8, 512) for half in range(2)]
                    for oc2 in range(8):
                        ps = psum()
                        for oc in range(8):
                            K.mm(ps, wo[oc2 // 4][:, oc, (oc2 % 4) * 128:(oc2 % 4 + 1) * 128], mk[:, oc, :],
                                 start=(oc == 0), stop=(oc == 7), sig=(oc == 7))
                        xv = xTn[g][tt_i][:, oc2, :]
                        K.stt("dve", xv, ps, gate1[:, oc2:oc2 + 1], xv, ALU.mult, ALU.add)

        xTn = xT

        for l in range(nlayers):
            with K.scope():
                nw = K.tile([128, 2, 8], F32, "nw")
                for j_ in range(2):
                    K.dma("sp", nw[:, j_, :], norm_w[l, j_].re("(kc p) -> p kc", p=128), misc_slot)
                adab = K.tile([128, 48], F32, "adab")
                K.dma("sp", adab, ada_b[l].re("(c p) -> p c", p=128), misc_slot)
                esk = K.tile([64, 8], F32, "esk")
                K.dma("sp", esk, V(attn_sink.ap[l].partition_broadcast(64)), misc_slot)
                K.act(esk, esk, AF.Exp)
                convw = K.tile([64, 5, 24], F32, "convw")
                for j_ in range(5):
                    K.dma("sp", convw[:, j_, :], delta_conv[l, j_].re("(n d) -> d n", d=64), misc_slot)
                nea = K.tile([64, 16], F32, "nea")
                K.dma("sp", nea, V(delta_a_log.ap[l].partition_broadcast(64)), misc_slot)
                K.act(nea, nea, AF.Exp)
                K.ts("dve", nea, nea, -1.0)
                dtb = K.tile([64, 16], F32, "dtb")
                K.dma("sp", dtb, V(delta_dt_bias.ap[l].partition_broadcast(64)), misc_slot)
                dnw = K.tile([64, 1], F32, "dnw")
                K.dma("sp", dnw, delta_norm_w[l].re("(d o) -> d o", o=1), misc_slot)
                hnw = K.tile([64, 1], F32, "hnw")
                K.dma("sp", hnw, hgrn_norm_w[l].re("(d o) -> d o", o=1), misc_slot)
                def compute_lb(dst, rowbc):
                    shp = [64, 4, 1024] if rowbc else [64, 4, 16]
                    with K.scope():
                        raw = K.tile(shp, F32, "lbraw")
                        for ll in range(4):
                            if not rowbc:
                                K.dma("sp", raw[:, ll, :], hgrn_lb[ll].re("(j d) -> d j", d=64), misc_slot)
                            else:
                                K.dma("sp", raw[:, ll, :], V(hgrn_lb.ap[ll].partition_broadcast(64)), misc_slot)
                        K.act(raw, raw, AF.Exp)
                        tot = K.tile(shp[0:1] + shp[2:], F32, "lbtot")
                        K.tt("dve", tot, raw[:, 0, :], raw[:, 1, :], ALU.add)
                        K.tt("dve", tot, tot, raw[:, 2, :], ALU.add)
                        K.tt("dve", tot, tot, raw[:, 3, :], ALU.add)
                        K.recip(tot, tot)
                        if l == 0:
                            K.memset("dve", dst, 0.0)
                        else:
                            acc = K.tile(shp[0:1] + shp[2:], F32, "lbacc")
                            K.cp("dve", acc, raw[:, 1, :])
                            for ll in range(2, l + 1):
                                K.tt("dve", acc, acc, raw[:, ll, :], ALU.add)
                            K.tt("dve", dst, acc, tot, ALU.mult)

                lbT = K.tile([64, 16], F32, "lbT")
                compute_lb(lbT, False)
                omlbT = K.tile([64, 16], F32, "omlbT")
                K.ts("dve", omlbT, lbT, -1.0, 1.0, ALU.mult, ALU.add)
                chk("params")
                mod = K.tile([128, 48, 2], F32, "mod")
                for cb in range(12):
                    w = wload(ada_w[l, :, cb * 512:(cb + 1) * 512], 8, 512)
                    ps = psum()
                    for cc in range(4):
                        for kc in range(8):
                            K.mm(ps[:, cc * 2:cc * 2 + 2], w[:, kc, cc * 128:(cc + 1) * 128], csT[:, kc, :],
                                 start=(kc == 0), stop=(kc == 7), sig=(kc == 7 and cc == 3))
                    K.tt("dve", mod[:, cb * 4:cb * 4 + 4, :], ps[:, 0:8].re("p (c g) -> p c g", g=2),
                         adab[:, cb * 4:cb * 4 + 4].un(2).bc([128, 4, 2]), ALU.add)
                wm1, wm2, sh1, sh2, g1, g2 = {}, {}, {}, {}, {}, {}
                for gi, g in enumerate(("P", "S")):
                    for (dst, sc_i, nwi) in ((wm1, 1, 0), (wm2, 4, 1)):
                        t = K.tile([128, 8], F32, "wm")
                        K.stt("dve", t, mod[:, sc_i * 8:sc_i * 8 + 8, gi], 1.0, nw[:, nwi, :], ALU.add, ALU.mult)
                        dst[g] = t
                    sh1[g] = mod[:, 0:8, gi]
                    g1[g] = mod[:, 16:24, gi]
                    sh2[g] = mod[:, 24:32, gi]
                    g2[g] = mod[:, 40:48, gi]

                chk("adaln")
                for g in ("P", "S"):
                    T = TT[g]
                    sample = (g == "S")
                    rms_mod(g, wm1[g], sh1[g])
                    chk("rms" + g)
                    with K.scope():
                        qT = K.tile([64, 8, T], BF16, "aq")
                        kT = K.tile([64, 2, T], BF16, "ak")
                        vtm = K.tile([128, T // 128, 128], BF16, "av")
                        yT = K.tile([64, 8, T], BF16, "ya")
                        wq = wload(w_in[l, :, O_AQ:O_AQ + 512], 8, 512)
                        wkv = wload(w_in[l, :, O_AK:O_AK + 256], 8, 256)
                        stg = Ring(K, 2, [128, 256], F32, "stg")
                        if sample:
                            rtab = K.tile([64, 2, NS_], F32, "rtab")
                            K.dma("sp", rtab, ropetab, misc_slot)
                            xbr = Ring(K, 2, [64, 512], BF16, "xb")
                        for tt_i in range(NT[g]):
                            tsl = slice(tt_i * 512, (tt_i + 1) * 512)
                            for hh in range(10):
                                ps = psum()
                                if hh < 8:
                                    proj_fm64(wq, hh * 64, hT[g][tt_i], ps[0:64, :])
                                    dst = qT[:, hh, tsl]
                                else:
                                    proj_fm64(wkv, (hh - 8) * 64, hT[g][tt_i], ps[0:64, :])
                                    dst = kT[:, hh - 8, tsl]
                                if not sample:
                                    K.cp("act", dst, ps[0:64, :])
                                else:
                                    xb = xbr.next()
                                    K.cp("act", xb, ps[0:64, :])
                                    ps2 = psum()
                                    K.mm(ps2[0:64, :], cstb[0:64, C_ROPE:C_ROPE + 64], xb)
                                    t1, t2 = tmpring.next(), tmpring.next()
                                    K.tt("dve", t1[0:64, :], ps2[0:64, :], rtab[:, 1, tsl], ALU.mult)
                                    K.tt("pool", t2[0:64, :], xb, rtab[:, 0, tsl], ALU.mult)
                                    K.tt("dve", dst, t1[0:64, :], t2[0:64, :], ALU.add)
                            chk("Afm" + g)
                            for j in range(4):
                                ps = psum()
                                for kc in range(8):
                                    K.mm(ps[:, 0:256], hT[g][tt_i][:, kc, j * 128:(j + 1) * 128], wkv[:, kc, :],
                                         start=(kc == 0), stop=(kc == 7), sig=(kc == 7))
                                tile_i = tt_i * 4 + j
                                K.cp("dve", vtm[:, tile_i, :], ps[:, 128:256])
                                chk("Atm_mm" + g)
                                if not sample:
                                    st = stg.next()
                                    K.cp("dve", st, ps[:, 0:256])
                                    chk("Atm_cp" + g)
                                    s_, r0 = divmod(tile_i * 128, 256)
                                    K.dma("sp", nak[s_, l, r0:r0 + 128, :], st[:, 0:128], io_slot)
                                    K.dma("sp", nav[s_, l, r0:r0 + 128, :], st[:, 128:256], io_slot)
                        chk("Aproj" + g)
                        if sample:
                            ctm = K.tile([128, 4, 128], BF16, "ctk")
                            cvv = K.tile([128, 4, 128], BF16, "ctv")
                            ckT = K.tile([64, 2, PAST], BF16, "ckT")
                            K.dma("pool", ctm, cak[l].re("(j p) c -> p j c", p=128), stage_slot[2])
                            K.dma("pool", cvv, cav[l].re("(j p) c -> p j c", p=128), stage_slot[3])
                            for kv in range(2):
                                ps = psum()
                                psb = ps
                                for j in range(4):
                                    K.tr(psb[0:64, j * 128:(j + 1) * 128], ctm[:, j, kv * 64:(kv + 1) * 64], identb)
                                K.cp("dve", ckT[:, kv, :], psb[0:64, 0:512])
                        for (t0, L) in SEQS[g]:
                            nblk = L // 128
                            for kv in range(2):
                                for bi in range(nblk):
                                    kts = []
                                    if sample:
                                        for j in range(4):
                                            kts.append((ckT[:, kv, j * 128:(j + 1) * 128], cvv[:, j, kv * 64:(kv + 1) * 64], None, None, 128))
                                        blks = [(bi - 1, C_MPREV), (bi, None), (bi + 1, C_MNEXT)]
                                    else:
                                        blks = [(b, None) for b in range(nblk)]
                                    for (b, mc) in blks:
                                        if b < 0 or b >= nblk:
                                            continue
                                        k0 = t0 + b * 128
                                        m = None if mc is None else cstb[:, mc:mc + 128].un(1).bc([128, 4, 128])
                                        kts.append((kT[:, kv, k0:k0 + 128], vtm[:, k0 // 128, kv * 64:(kv + 1) * 64], m, None, 128))
                                    q0 = t0 + bi * 128
                                    attention(qT[:, 4 * kv:4 * kv + 4, q0:q0 + 128], 512, kts,
                                              yT[:, 4 * kv:4 * kv + 4, q0:q0 + 128],
                                              esk[:, 4 * kv:4 * kv + 4].un(2).bc([64, 4, 128]), b3=128)
                        chk("Aattn" + g)
                        branch_merge(l, g, 0, yT, g1[g])
                        chk("Amerge" + g)
                    with K.scope():
                        yT = K.tile([64, 8, T], BF16, "yd")
                        wdq = wload(w_in[l, :, O_DQ:O_DQ + 512], 8, 512)
                        wdk = wload(w_in[l, :, O_DK:O_DK + 512], 8, 512)
                        wdv = wload(w_in[l, :, O_DV:O_DV + 512], 8, 512)
                        chk("Dw" + g)
                        if sample:
                            vrow = K.tile([64, 16, 512], BF16, "vrow")
                            for r_ in range(16):
                                ps = psum()
                                for kc in range(8):
                                    K.mm(ps[0:64, :], hT[g][r_ // 8][:, kc, (r_ % 8) * 64:(r_ % 8 + 1) * 64], wdv[:, kc, :],
                                         start=(kc == 0), stop=(kc == 7), sig=(kc == 7))
                                K.cp("act", vrow[:, r_, :], ps[0:64, :])
                            cktm = K.tile([128, 4, 512], BF16, "cktm")
                            cvtm = K.tile([128, 4, 512], BF16, "cvtm")
                            K.dma("pool", cktm, cnk[l].re("(j p) c -> p j c", p=128), stage_slot[2])
                            K.dma("pool", cvtm, cnv[l].re("(j p) c -> p j c", p=128), stage_slot[3])
                            ckr = Ring(K, 2, [64, 512], BF16, "ckr")
                            Er = Ring(K, 2, [64, 960], BF16, "Er")
                            Ef = Ring(K, 2, [64, 960], F32, "Ef")
                            colm = cst[0:64, C_COLM:C_COLM + 64]
                        else:
                            stg = Ring(K, 2, [128, 512], F32, "stgd")
                            vtm = K.tile([128, T // 128, 512], BF16, "dvtm")
                            for tile_i in range(T // 128):
                                tt_i, j = divmod(tile_i, 4)
                                for which, w in ((0, wdk), (1, wdv)):
                                    ps = psum()
                                    for kc in range(8):
                                        K.mm(ps, hT[g][tt_i][:, kc, j * 128:(j + 1) * 128], w[:, kc, :],
                                             start=(kc == 0), stop=(kc == 7), sig=(kc == 7))
                                    chk("Dmm" + g)
                                    st = stg.next()
                                    K.cp("act", st, ps)
                                    chk("Dcp" + g)
                                    s_, r0 = divmod(tile_i * 128, 256)
                                    K.dma("sp", (nnk if which == 0 else nnv)[s_, l, r0:r0 + 128, :], st, io_slot)
                                    chk("Ddma" + g)
                                    if which == 1:
                                        K.cp("dve", vtm[:, tile_i, :], ps)
                        chk("Dtm" + g)
                        qh = Ring(K, 2, [64, T], BF16, "dq")
                        kh = Ring(K, 2, [64, T], BF16, "dk")
                        for h in range(8):
                            q_h, k_h = qh.next(), kh.next()
                            for tt_i in range(NT[g]):
                                for (w, dst) in ((wdq, q_h), (wdk, k_h)):
                                    ps = psum()
                                    proj_fm64(w, h * 64, hT[g][tt_i], ps[0:64, :])
                                    K.cp("act", dst[:, tt_i * 512:(tt_i + 1) * 512], ps[0:64, :])
                            chk("Dproj%d" % h + g)
                            if sample:
                                ck = ckr.next()
                                ps = psum()
                                psb = ps
                                for j in range(4):
                                    K.tr(psb[0:64, j * 128:(j + 1) * 128], cktm[:, j, h * 64:(h + 1) * 64], identb)
                                K.cp("dve", ck, psb[0:64, 0:512])
                                ef = Ef.next()
                                K.dma("sp", ef, rpbx[l, h], stage_slot[h % 2])
                                K.act(ef, ef, AF.Exp)
                                E = Er.next()
                                K.tt("dve", E.re("p (e c) -> p e c", c=64), ef.re("p (e c) -> p e c", c=64),
                                     colm.un(1).bc([64, 15, 64]), ALU.mult)
                                for hf_ in range(2):
                                    kts = [(ck[:, j * 128:(j + 1) * 128], cvtm[:, j, h * 64:(h + 1) * 64], None, None, 128) for j in range(4)]
                                    for kr in range(16):
                                        qs = [r_ for r_ in range(8 * hf_, 8 * hf_ + 8)
                                              if min(max(r_ - 4, 0), 8) <= kr < min(max(r_ - 4, 0), 8) + 8]
                                        if not qs:
                                            continue
                                        c0, c1 = (qs[0] - 8 * hf_) * 64, (qs[-1] + 1 - 8 * hf_) * 64
                                        e0 = qs[0] - kr + 7
                                        kts.append((k_h[:, kr * 64:(kr + 1) * 64], vrow[:, kr, h * 64:(h + 1) * 64],
                                                    E[:, e0 * 64:(e0 + len(qs)) * 64], (c0, c1), 64))
                                    attention(q_h[:, hf_ * 512:(hf_ + 1) * 512], 512, kts, yT[:, h, hf_ * 512:(hf_ + 1) * 512])
                            else:
                                for (t0, L) in SEQS[g]:
                                    kts = [(k_h[:, t0 + j * 128:t0 + (j + 1) * 128], vtm[:, (t0 + j * 128) // 128, h * 64:(h + 1) * 64],
                                            None, None, 128) for j in range(L // 128)]
                                    attention(q_h[:, t0:t0 + L], L, kts, yT[:, h, t0:t0 + L])
                                    chk("Dattn%d" % h + g)
                        branch_merge(l, g, 3, yT, g1[g])
                        chk("D" + g)

                    def r3(v):
                        return v.re("p (h c) -> p h c", c=64)

                    def bfv(v):
                        return v.bitcast(BF16)[:, 0:256]

                    nch = T // 64
                    with K.scope():
                        yT = K.tile([64, 8, T], BF16, "yb")
                        gb = K.tile([64, nch, 32], F32, "gb")
                        wab = wload(w_in[l, :, O_BA:O_BA + 32], 8, 32)
                        for c in range(nch):
                            tt_i, off = divmod(c * 64, 512)
                            ps = psum()
                            for kc in range(8):
                                K.mm(ps[0:64, 0:32], hT[g][tt_i][:, kc, off:off + 64], wab[:, kc, :],
                                     start=(kc == 0), stop=(kc == 7), sig=(kc == 7))
                            K.cp("act", gb[:, c, :], ps[0:64, 0:32])
                        gv, bv = gb[:, :, 0:16], gb[:, :, 16:32]
                        K.tt("dve", gv, gv, dtb.un(1).bc([64, nch, 16]), ALU.add)
                        K.act(gv, gv, AF.Exp)
                        K.act(gv, gv, AF.Ln, bias=onec[0:64, 0:1])
                        K.tt("dve", gv, gv, nea.un(1).bc([64, nch, 16]), ALU.mult)
                        K.act(bv, bv, AF.Sigmoid)
                        for hg in range(2):
                            with K.scope():
                                qkv = K.tile([64, 12, T], BF16, "qkv")
                                oT = K.tile([64, 4, T], F32, "oTb")
                                w3 = [wload(w_in[l, :, o_ + hg * 256:o_ + hg * 256 + 256], 8, 256) for o_ in (O_BQ, O_BK, O_BV)]
                                nsq = len(SEQS[g])
                                Ls = SEQS[g][0][1]
                                with K.scope():
                                    rawr = Ring(K, 2, [64, nsq, Ls + 4], F32, "raw")
                                    for rt in rawr.t:
                                        K.memset("pool", rt, 0.0)
                                    cvr = Ring(K, 2, [64, T], F32, "cv")
                                    f64 = Ring(K, 4, [64, 512], F32, "f64")
                                    for which in range(3):
                                        for hi in range(4):
                                            h = hg * 4 + hi
                                            raw = rawr.next()
                                            for tt_i in range(NT[g]):
                                                ps = psum()
                                                proj_fm64(w3[which], hi * 64, hT[g][tt_i], ps[0:64, :])
                                                if sample:
                                                    K.cp("act", raw[:, 0, 2 + tt_i * 512:2 + (tt_i + 1) * 512], ps[0:64, :])
                                                else:
                                                    for s_ in range(2):
                                                        K.cp("act", raw[:, s_, 2:2 + 256], ps[0:64, s_ * 256:(s_ + 1) * 256])
                                            cv = cvr.next()
                                            ch = which * 8 + h
                                            for s_, (t0, L) in enumerate(SEQS[g]):
                                                K.ts("dve", cv[:, t0:t0 + L], raw[:, s_, 0:L], convw[:, 0, ch:ch + 1])
                                                for j in range(1, 5):
                                                    K.stt("dve", cv[:, t0:t0 + L], raw[:, s_, j:j + L],
                                                          convw[:, j, ch:ch + 1], cv[:, t0:t0 + L], ALU.mult, ALU.add)
                                            K.act(cv, cv, AF.Silu)
                                            if which < 2:
                                                for tt_i in range(NT[g]):
                                                    tsl = slice(tt_i * 512, (tt_i + 1) * 512)
                                                    sq = f64.next()
                                                    K.act(sq, cv[:, tsl], AF.Square)
                                                    ps = psum()
                                                    K.mm(ps[0:64, :], onesf[0:64, 0:64], sq)
                                                    rn = f64.next()
                                                    K.act(rn, ps[0:64, :], AF.Sqrt, bias=epsc[0:64, 0:1], scale=1.0)
                                                    K.recip(rn, rn)
                                                    K.stt("dve", qkv[:, which * 4 + hi, tsl], cv[:, tsl],
                                                          (0.125 if which == 0 else 1.0), rn, ALU.mult, ALU.mult)
                                            else:
                                                K.cp("pool", qkv[:, 8 + hi, :], cv)
                                with K.scope():
                                    Lr = Ring(K, 20 if sample else 40, [64, 256], F32, "bL")
                                    Nr = Ring(K, 6 if sample else 14, [64, 256], F32, "bN")
                                    Sst = [K.tile([64, 256], F32, "Sb%d" % d_) for d_ in range(2)]
                                    idf = cst[0:64, C_ID:C_ID + 64]
                                    idb = cstb[0:64, C_ID:C_ID + 64]
                                    one64 = onesf[0:64, 0:64]
                                    HS = [slice(hi * 64, (hi + 1) * 64) for hi in range(4)]
                                    for si, (t0, L) in enumerate(SEQS[g]):
                                        ncs = L // 64
                                        for d_ in range(2):
                                            if sample:
                                                K.dma("sp", r3(Sst[d_]), sd_in[l, d_, hg * 4:hg * 4 + 4].re("h k v -> k h v"), stage_slot[d_])
                                            else:
                                                K.memset("pool", Sst[d_], 0.0)
                                        visited = set()
                                        def chunk_gen(i, d_):
                                            c = i if d_ == 0 else ncs - 1 - i
                                            tok0 = t0 + c * 64
                                            cg = tok0 // 64
                                            csl = slice(tok0, tok0 + 64)
                                            U_ = cst[0:64, C_U[d_]:C_U[d_] + 64]
                                            nm_ = cst[0:64, C_NM[d_]:C_NM[d_] + 64]
                                            nmT_ = cst[0:64, C_NMT[d_]:C_NMT[d_] + 64]
                                            SL_ = cst[0:64, C_SL[d_]:C_SL[d_] + 64]
                                            gcol = gb[:, cg, d_ * 8 + hg * 4:d_ * 8 + hg * 4 + 4]
                                            bcol = gb[:, cg, 16 + d_ * 8 + hg * 4:16 + d_ * 8 + hg * 4 + 4]
                                            S = Sst[d_]
                                            ps = psum()
                                            psb = ps
                                            for hi in range(4):
                                                K.tr(psb[0:64, hi * 64:(hi + 1) * 64], qkv[:, 4 + hi, csl], idb, sig=False)
                                                K.tr(psb[0:64, 256 + hi * 64:256 + (hi + 1) * 64], qkv[:, 8 + hi, csl], idb, sig=(hi == 3))
                                            Ktm, Vtm = Lr.next(), Lr.next()
                                            K.cp("act", Ktm, psb[0:64, 0:256])
                                            yield
                                            K.cp("dve", Vtm, psb[0:64, 256:512])
                                            gU = Lr.next()
                                            K.tt("pool", r3(gU), U_.un(1).bc([64, 4, 64]), gcol.un(2).bc([64, 4, 64]), ALU.mult)
                                            ps1 = psum()
                                            K.mm(ps1[0:64, 0:256], one64, gU, sig=False)
                                            K.mm(ps1[0:64, 256:260], U_, gcol, sig=False)
                                            K.mm(ps1[0:64, 260:264], one64, gcol)
                                            sm = Lr.next()
                                            K.cp("dve", sm[:, 0:8], ps1[0:64, 256:264])
                                            gc, gl = sm[:, 0:4], sm[:, 4:8]
                                            K.act(sm[:, 8:12], gc, AF.Exp)
                                            K.tt("dve", sm[:, 12:16], gl, gc, ALU.subtract)
                                            K.act(sm[:, 12:16], sm[:, 12:16], AF.Exp)
                                            K.act(sm[:, 16:20], gl, AF.Exp)
                                            K.tt("dve", sm[:, 20:24], sm[:, 8:12], bcol, ALU.mult)
                                            be, ekd, cd = sm[:, 20:24], sm[:, 12:16], sm[:, 16:20]
                                            gcrow = Lr.next()
                                            K.cp("act", gcrow, ps1[0:64, 0:256])
                                            yield
                                            G = Lr.next()
                                            K.tt("dve", r3(G), nm_.un(1).bc([64, 4, 64]), r3(gcrow), ALU.subtract)
                                            K.tt("dve", r3(G), r3(G), gc.un(2).bc([64, 4, 64]), ALU.add)
                                            K.act(G, G, AF.Exp)
                                            GT = Lr.next()
                                            K.tt("pool", r3(GT), r3(gcrow), nmT_.un(1).bc([64, 4, 64]), ALU.add)
                                            K.tt("pool", r3(GT), r3(GT), gc.un(2).bc([64, 4, 64]), ALU.subtract)
                                            K.act(GT, GT, AF.Exp)
                                            Erow = Lr.next()
                                            K.act(Erow, gcrow, AF.Exp)
                                            yield
                                            psk = psum()
                                            for hi in range(4):
                                                K.mm(psk[0:64, HS[hi]], qkv[:, 4 + hi, csl], qkv[:, 4 + hi, csl], sig=(hi == 3))
                                            bsl = Lr.next()
                                            K.tt("pool", r3(bsl), SL_.un(1).bc([64, 4, 64]), bcol.un(2).bc([64, 4, 64]), ALU.mult)
                                            A = Lr.next()
                                            K.tt("dve", A, psk[0:64, 0:256], G, ALU.mult)
                                            K.tt("dve", A, A, bsl, ALU.mult)
                                            pst = psum()
                                            for hi in range(4):
                                                K.tr(pst[0:64, HS[hi]], A[:, HS[hi]], idf, sig=(hi == 3))
                                            AT = Lr.next()
                                            K.cp("act", AT, pst[0:64, 0:256])
                                            yield
                                            X = Nr.next()
                                            K.tt("dve", r3(X), idf.un(1).bc([64, 4, 64]), r3(pst[0:64, 0:256]), ALU.subtract)
                                            P_, PT = A, AT
                                            for kk in range(1, 6):
                                                psp = psum()
                                                for hi in range(4):
                                                    K.mm(psp[0:64, HS[hi]], PT[:, HS[hi]], P_[:, HS[hi]], sig=(hi == 3))
                                                Pn = Nr.next()
                                                K.cp("act", Pn, psp[0:64, 0:256])
                                                PTn = None
                                                if kk < 5:
                                                    pspt = psum()
                                                    for hi in range(4):
                                                        K.mm(pspt[0:64, HS[hi]], P_[:, HS[hi]], PT[:, HS[hi]], sig=(hi == 3))
                                                    PTn = Nr.next()
                                                    K.cp("pool", PTn, pspt[0:64, 0:256]) if False else K.cp("dve", PTn, pspt[0:64, 0:256])
                                                psx = psum()
                                                for hi in range(4):
                                                    K.mm(psx[0:64, HS[hi]], Pn[:, HS[hi]], X[:, HS[hi]], sig=(hi == 3))
                                                Xn = Nr.next() if kk < 5 else Lr.next()
                                                yield
                                                K.tt("dve", Xn, psx[0:64, 0:256], X, ALU.add)
                                                P_, PT, X = Pn, PTn, Xn
                                            TT_ = X
                                            Rk, Rv, Kd = Lr.next(), Lr.next(), Lr.next()
                                            K.tt("pool", r3(Rk), r3(Ktm), be.un(2).bc([64, 4, 64]), ALU.mult)
                                            K.tt("pool", r3(Rv), r3(Vtm), bcol.un(2).bc([64, 4, 64]), ALU.mult)
                                            K.tt("pool", r3(Kd), r3(Ktm), ekd.un(2).bc([64, 4, 64]), ALU.mult)
                                            psw = psum()
                                            for hi in range(4):
                                                K.mm(psw[0:64, HS[hi]], Rk[:, HS[hi]], TT_[:, HS[hi]], sig=(hi == 3))
                                            nWkT = Lr.next()
                                            K.ts("dve", nWkT, psw[0:64, 0:256], -1.0)
                                            yield
                                            psq = psum()
                                            for hi in range(4):
                                                K.mm(psq[0:64, HS[hi]], qkv[:, 4 + hi, csl], qkv[:, hi, csl], sig=(hi == 3))
                                            PqkT = Lr.next()
                                            K.tt("dve", PqkT, psq[0:64, 0:256], GT, ALU.mult)
                                            QdT = Lr.next()
                                            K.tt("pool", r3(QdT), qkv[:, 0:4, csl], r3(Erow), ALU.mult)
                                            yield
                                            psu = psum()
                                            for hi in range(4):
                                                K.mm(psu[0:64, HS[hi]], TT_[:, HS[hi]], Rv[:, HS[hi]], start=True, stop=False, sig=False)
                                                K.mm(psu[0:64, HS[hi]], nWkT[:, HS[hi]], S[:, HS[hi]], start=False, stop=True, sig=(hi == 3))
                                            Uc = Lr.next()
                                            yield
                                            K.cp("act", Uc, psu[0:64, 0:256])
                                            pso = psum()
                                            for hi in range(4):
                                                K.mm(pso[0:64, HS[hi]], S[:, HS[hi]], QdT[:, HS[hi]], start=True, stop=False, sig=False)
                                                K.mm(pso[0:64, HS[hi]], Uc[:, HS[hi]], PqkT[:, HS[hi]], start=False, stop=True, sig=(hi == 3))
                                            ov = oT[:, :, csl]
                                            if c not in visited:
                                                K.cp("act", ov, r3(pso[0:64, 0:256]))
                                                visited.add(c)
                                            else:
                                                K.tt("dve", ov, ov, r3(pso[0:64, 0:256]), ALU.add)
                                            psm = psum()
                                            for hi in range(4):
                                                K.mm(psm[0:64, HS[hi]], Kd[:, HS[hi]], Uc[:, HS[hi]], sig=(hi == 3))
                                            K.tt("dve", r3(S), r3(S), cd.un(2).bc([64, 4, 64]), ALU.mult)
                                            K.tt("dve", S, S, psm[0:64, 0:256], ALU.add)
                                            yield
                                            yield
                                        for i in range(ncs):
                                            gens = [chunk_gen(i, 0), chunk_gen(i, 1)]
                                            if sample:
                                                for g_ in gens:
                                                    for _ in g_:
                                                        pass
                                            else:
                                                live = list(gens)
                                                while live:
                                                    for g_ in list(live):
                                                        try:
                                                            next(g_)
                                                        except StopIteration:
                                                            live.remove(g_)
                                        if not sample:
                                            for d_ in range(2):
                                                K.dma("sp", nsd[si, l, d_, hg * 4:hg * 4 + 4].re("h k v -> k h v"), r3(Sst[d_]), io_slot)
                                with K.scope():
                                    f64 = Ring(K, 4, [64, 512], F32, "f64z")
                                    wz = wload(w_in[l, :, O_BZ + hg * 256:O_BZ + hg * 256 + 256], 8, 256)
                                    for hi in range(4):
                                        for tt_i in range(NT[g]):
                                            tsl = slice(tt_i * 512, (tt_i + 1) * 512)
                                            sq = f64.next()
                                            K.act(sq, oT[:, hi, tsl], AF.Square)
                                            ps = psum()
                                            K.mm(ps[0:64, :], one64, sq)
                                            rn = f64.next()
                                            K.act(rn, ps[0:64, :], AF.Sqrt, bias=epsc[0:64, 0:1], scale=1.0 / 64)
                                            K.recip(rn, rn)
                                            psz = psum()
                                            proj_fm64(wz, hi * 64, hT[g][tt_i], psz[0:64, :])
                                            sz = f64.next()
                                            K.act(sz, psz[0:64, :], AF.Silu)
                                            K.tt("dve", rn, oT[:, hi, tsl], rn, ALU.mult)
                                            K.stt("dve", yT[:, hg * 4 + hi, tsl], rn, dnw[:, 0:1], sz, ALU.mult, ALU.mult)
                        branch_merge(l, g, 1, yT, g1[g])
                        chk("B" + g)

                    with K.scope():
                        yT = K.tile([64, 8, T], BF16, "yc")
                        lbR = K.tile([64, 1024], F32, "lbR")
                        omlbR = K.tile([64, 1024], F32, "omlbR")
                        compute_lb(lbR, True)
                        K.ts("dve", omlbR, lbR, -1.0, 1.0, ALU.mult, ALU.add)
                        for hg in range(2):
                            with K.scope():
                                qT = K.tile([64, 4, T], BF16, "cq")
                                kT = K.tile([64, 2, 4, T], BF16, "ckk")
                                oT = K.tile([64, 4, T], F32, "oTc")
                                wcq = wload(w_in[l, :, O_CQ + hg * 256:O_CQ + hg * 256 + 256], 8, 256)
                                wcf = [wload(w_in[l, :, O_CF + d_ * 512 + hg * 256:O_CF + d_ * 512 + hg * 256 + 256], 8, 256) for d_ in range(2)]
                                wci = wload(w_in[l, :, O_CI + hg * 256:O_CI + hg * 256 + 256], 8, 256)
                                with K.scope():
                                    f64 = Ring(K, 4, [64, 512], F32, "f64c")
                                    for hi in range(4):
                                        for tt_i in range(NT[g]):
                                            tsl = slice(tt_i * 512, (tt_i + 1) * 512)
                                            ps = psum()
                                            proj_fm64(wcq, hi * 64, hT[g][tt_i], ps[0:64, :])
                                            K.act(qT[:, hi, tsl], ps[0:64, :], AF.Silu)
                                            for d_ in range(2):
                                                ps = psum()
                                                proj_fm64(wcf[d_], hi * 64, hT[g][tt_i], ps[0:64, :])
                                                sg = f64.next()
                                                K.act(sg, ps[0:64, :], AF.Sigmoid, scale=-1.0)
                                                col = d_ * 8 + hg * 4 + hi
                                                K.ts("dve", kT[:, d_, hi, tsl], sg, omlbT[:, col:col + 1])
                                with K.scope():
                                    Lr = Ring(K, 18 if sample else 36, [64, 256], F32, "cL")
                                    Sst = [K.tile([64, 256], F32, "Sc%d" % d_) for d_ in range(2)]
                                    HS = [slice(hi * 64, (hi + 1) * 64) for hi in range(4)]
                                    for si, (t0, L) in enumerate(SEQS[g]):
                                        ncs = L // 64
                                        for d_ in range(2):
                                            if sample:
                                                K.dma("sp", r3(Sst[d_]), sh_in[l, d_, hg * 4:hg * 4 + 4].re("h k v -> k h v"), stage_slot[d_])
                                            else:
                                                K.memset("pool", Sst[d_], 0.0)
                                        visited = set()
                                        def chunk_gen(i, d_):
                                            c = i if d_ == 0 else ncs - 1 - i
                                            tok0 = t0 + c * 64
                                            csl = slice(tok0, tok0 + 64)
                                            tt_i, off = divmod(tok0, 512)
                                            U_ = cst[0:64, C_U[d_]:C_U[d_] + 64]
                                            W2_ = cst[0:64, C_W2[d_]:C_W2[d_] + 64]
                                            MT_ = cst[0:64, C_MT[d_]:C_MT[d_] + 64]
                                            mid = MID[d_]
                                            last = 63 if d_ == 0 else 0
                                            S = Sst[d_]
                                            psf, psv = psum(), psum()
                                            for kc in range(8):
                                                K.mm(psf[0:64, 0:256], hT[g][tt_i][:, kc, off:off + 64], wcf[d_][:, kc, :],
                                                     start=(kc == 0), stop=(kc == 7), sig=(kc == 7))
                                            for kc in range(8):
                                                K.mm(psv[0:64, 0:256], hT[g][tt_i][:, kc, off:off + 64], wci[:, kc, :],
                                                     start=(kc == 0), stop=(kc == 7), sig=(kc == 7))
                                            Vt = bfv(Lr.next())
                                            K.cp("act", Vt, psv[0:64, 0:256])
                                            yield
                                            f = Lr.next()
                                            K.act(f, psf[0:64, 0:256], AF.Sigmoid)
                                            cs_ = slice(d_ * 512 + hg * 256, d_ * 512 + hg * 256 + 256)
                                            K.tt("dve", f, f, omlbR[:, cs_], ALU.mult)
                                            K.tt("dve", f, f, lbR[:, cs_], ALU.add)
                                            lf = Lr.next()
                                            K.act(lf, f, AF.Ln)
                                            yield
                                            ktm = Lr.next()
                                            K.ts("pool", ktm, f, -1.0, 1.0, ALU.mult, ALU.add)
                                            psb_ = psum()
                                            for hi in range(4):
                                                K.mm(psb_[0:64, HS[hi]], lf[:, HS[hi]], U_, sig=(hi == 3))
                                            psw2 = psum()
                                            K.mm(psw2[0:64, 0:256], W2_, lf)
                                            bT = Lr.next()
                                            K.cp("dve", bT, psb_[0:64, 0:256])
                                            yield
                                            Eb = Lr.next()
                                            K.act(Eb, bT, AF.Exp)
                                            bp = Lr.next()
                                            K.tt("dve", r3(bp), r3(bT), r3(bT)[:, :, mid:mid + 1].bc([64, 4, 64]), ALU.subtract)
                                            Ebp, Ebn = Lr.next(), Lr.next()
                                            K.act(Ebp, bp, AF.Exp)
                                            K.act(Ebn, bp, AF.Exp, scale=-1.0)
                                            yield
                                            QiT, KiT, QdT = bfv(Lr.next()), bfv(Lr.next()), Lr.next()
                                            K.tt("pool", r3(QiT), qT[:, :, csl], r3(Ebp), ALU.mult)
                                            K.tt("pool", r3(KiT), kT[:, d_, :, csl], r3(Ebn), ALU.mult)
                                            K.tt("dve", r3(QdT), qT[:, :, csl], r3(Eb), ALU.mult)
                                            Ekd = Lr.next()
                                            K.act(Ekd, psw2[0:64, 0:256], AF.Exp)
                                            Kd = bfv(Lr.next())
                                            K.tt("dve", Kd, ktm, Ekd, ALU.mult)
                                            yield
                                            psa = psum()
                                            if d_ == 0:
                                                f_t, p_t, p_j, z_j = (32, 64), (0, 32), (0, 32), (32, 64)
                                            else:
                                                f_t, p_t, p_j, z_j = (0, 32), (32, 64), (32, 64), (0, 32)
                                            zer = cstb[0:64, C_ZERO:C_ZERO + 32]
                                            for hi in range(4):
                                                b0 = hi * 64
                                                K.mm(psa[0:64, b0 + f_t[0]:b0 + f_t[1]], KiT[:, HS[hi]], QiT[:, b0 + f_t[0]:b0 + f_t[1]], sig=False)
                                                K.mm(psa[p_j[0]:p_j[1], b0 + p_t[0]:b0 + p_t[1]], KiT[:, b0 + p_j[0]:b0 + p_j[1]],
                                                     QiT[:, b0 + p_t[0]:b0 + p_t[1]], sig=False)
                                                K.mm(psa[z_j[0]:z_j[1], b0 + p_t[0]:b0 + p_t[1]], zer, QiT[:, b0 + p_t[0]:b0 + p_t[1]], sig=(hi == 3))
                                            att = bfv(Lr.next())
                                            K.tt("dve", r3(att), r3(psa[0:64, 0:256]), MT_.un(1).bc([64, 4, 64]), ALU.mult)
                                            yield
                                            pso = psum()
                                            for hi in range(4):
                                                K.mm(pso[0:64, HS[hi]], Vt[:, HS[hi]], att[:, HS[hi]], start=True, stop=False, sig=False)
                                                K.mm(pso[0:64, HS[hi]], S[:, HS[hi]], QdT[:, HS[hi]], start=False, stop=True, sig=(hi == 3))
                                            ov = oT[:, :, csl]
                                            if c not in visited:
                                                K.cp("act", ov, r3(pso[0:64, 0:256]))
                                                visited.add(c)
                                            else:
                                                K.tt("dve", ov, ov, r3(pso[0:64, 0:256]), ALU.add)
                                            psm = psum()
                                            for hi in range(4):
                                                K.mm(psm[0:64, HS[hi]], Kd[:, HS[hi]], Vt[:, HS[hi]], sig=(hi == 3))
                                            K.tt("dve", r3(S), r3(S), r3(Eb)[:, :, last:last + 1].bc([64, 4, 64]), ALU.mult)
                                            K.tt("dve", S, S, psm[0:64, 0:256], ALU.add)
                                            yield
                                            yield
                                        for i in range(ncs):
                                            gens = [chunk_gen(i, 0), chunk_gen(i, 1)]
                                            if sample:
                                                for g_ in gens:
                                                    for _ in g_:
                                                        pass
                                            else:
                                                live = list(gens)
                                                while live:
                                                    for g_ in list(live):
                                                        try:
                                                            next(g_)
                                                        except StopIteration:
                                                            live.remove(g_)
                                        if not sample:
                                            for d_ in range(2):
                                                K.dma("sp", nsh[si, l, d_, hg * 4:hg * 4 + 4].re("h k v -> k h v"), r3(Sst[d_]), io_slot)
                                with K.scope():
                                    f64 = Ring(K, 4, [64, 512], F32, "f64g")
                                    wcg = wload(w_in[l, :, O_CG + hg * 256:O_CG + hg * 256 + 256], 8, 256)
                                    for hi in range(4):
                                        for tt_i in range(NT[g]):
                                            tsl = slice(tt_i * 512, (tt_i + 1) * 512)
                                            psg = psum()
                                            proj_fm64(wcg, hi * 64, hT[g][tt_i], psg[0:64, :])
                                            sg = f64.next()
                                            K.act(sg, psg[0:64, :], AF.Sigmoid)
                                            K.tt("dve", sg, oT[:, hi, tsl], sg, ALU.mult)
                                            sq = f64.next()
                                            K.act(sq, sg, AF.Square)
                                            ps = psum()
                                            K.mm(ps[0:64, :], onesf[0:64, 0:64], sq)
                                            rn = f64.next()
                                            K.act(rn, ps[0:64, :], AF.Sqrt, bias=epsc[0:64, 0:1], scale=1.0 / 64)
                                            K.recip(rn, rn)
                                            K.stt("dve", yT[:, hg * 4 + hi, tsl], sg, hnw[:, 0:1], rn, ALU.mult, ALU.mult)
                        branch_merge(l, g, 2, yT, g1[g])
                        chk("C" + g)
                    rms_mod(g, wm2[g], sh2[g])
                    with K.scope():
                        uT = K.tile([128, 4, 512], BF16, "uT")
                        rr = Ring(K, 2, [128, 512], F32, "relu")
                        for hb in range(8):
                            w1 = wload(mlp_w1[l, :, hb * 512:(hb + 1) * 512], 8, 512)
                            w2 = [wload(mlp_w2[l, hb * 512:(hb + 1) * 512, half * 512:(half + 1) * 512], 4, 512) for half in range(2)]
                            for tt_i in range(NT[g]):
                                for hc in range(4):
                                    ps = psum()
                                    for kc in range(8):
                                        K.mm(ps, w1[:, kc, hc * 128:(hc + 1) * 128], hT[g][tt_i][:, kc, :],
                                             start=(kc == 0), stop=(kc == 7), sig=(kc == 7))
                                    r = rr.next()
                                    K.act(r, ps, AF.Relu)
                                    K.tt("pool", uT[:, hc, :], r, r, ALU.mult)
                                for oc in range(8):
                                    ps = psum()
                                    for hc in range(4):
                                        K.mm(ps, w2[oc // 4][:, hc, (oc % 4) * 128:(oc % 4 + 1) * 128], uT[:, hc, :],
                                             start=(hc == 0), stop=(hc == 3), sig=(hc == 3))
                                    xv = xT[g][tt_i][:, oc, :]
                                    K.stt("dve", xv, ps, g2[g][:, oc:oc + 1], xv, ALU.mult, ALU.add)

        chk("layers")
        with K.scope():
            fw = K.tile([128, 8], F32, "fw")
            K.dma("sp", fw, final_norm_w.re("(kc p) -> p kc", p=128), misc_slot)
            zero8 = K.tile([128, 8], F32, "z8")
            K.memset("dve", zero8, 0.0)
            yst = Ring(K, 2, [128, D], F32, "yst")
            hf = {"P": [K.tile([128, 8, 512], F32, "hfP")], "S": [K.tile([128, 8, 512], F32, "hfS%d" % i) for i in range(2)]}
            for g, dst in (("P", y_p), ("S", y_s)):
                for tt_i in range(NT[g]):
                    x = xT[g][tt_i]
                    ps = psum()
                    for kc in range(8):
                        sq = sqring.next()
                        K.act(sq, x[:, kc, :], AF.Square)
                        K.mm(ps, onesb, sq, start=(kc == 0), stop=(kc == 7))
                    rstd = rsring.next()
                    K.act(rstd, ps, AF.Sqrt, bias=epsc[:, 0:1], scale=1.0 / D)
                    K.recip(rstd, rstd)
                    for kc in range(8):
                        K.stt("dve", hf[g][tt_i][:, kc, :], x[:, kc, :], fw[:, kc:kc + 1], rstd, ALU.mult, ALU.mult)
                    for j in range(4):
                        yt = yst.next()
                        for half in range(2):
                            ps = psum()
                            for kc4 in range(4):
                                kc = half * 4 + kc4
                                K.tr(ps[:, kc4 * 128:(kc4 + 1) * 128], hf[g][tt_i][:, kc, j * 128:(j + 1) * 128], identf)
                            K.cp("act" if half else "dve", yt[:, half * 512:(half + 1) * 512], ps)
                        r0 = tt_i * 512 + j * 128
                        K.dma("sp", dst[r0:r0 + 128, :], yt, io_slot)
        K.finish()
    return nc


_CACHE = {}


def kernel(**inp):
    f = lambda k: np.ascontiguousarray(np.asarray(inp[k], dtype=np.float32))
    if "nc" not in _CACHE:
        _CACHE["nc"] = build_program()
    nc = _CACHE["nc"]
    rpb = f("na_rpb")
    kc = np.arange(64)[:, None, None]
    e = np.arange(15)[None, :, None]
    qc = np.arange(64)[None, None, :]
    dc = np.clip(kc - qc + 15, 0, 30) + 0 * e
    rr = (14 - e) + 0 * dc
    rpbx = np.ascontiguousarray(rpb[:, :, rr, dc]).reshape(DEPTH, 8, 64, 15 * 64)
    shared = {
        "norm_w": f("norm_w"), "ada_w": f("ada_w"), "ada_b": f("ada_b"), "w_in": f("w_in"),
        "attn_sink": f("attn_sink"), "delta_conv": f("delta_conv"),
        "delta_a_log": f("delta_a_log").reshape(DEPTH, 16), "delta_dt_bias": f("delta_dt_bias").reshape(DEPTH, 16),
        "delta_norm_w": f("delta_norm_w"), "hgrn_lb": f("hgrn_lb").reshape(DEPTH, 1024),
        "hgrn_norm_w": f("hgrn_norm_w"), "rpbx": rpbx, "w_branch": f("w_branch"), "w_out": f("w_out"),
        "mlp_w1": f("mlp_w1"), "mlp_w2": f("mlp_w2"), "final_norm_w": f("final_norm_w"),
        "consts": make_consts(), "ropetab": make_rope_tab(),
    }
    xp, xs = f("x_prompt"), f("x_sample")
    cak, cav, cnk, cnv = f("cache_attn_k"), f("cache_attn_v"), f("cache_na_k"), f("cache_na_v")
    sd, sh, c, cctx = f("state_delta"), f("state_hgrn"), f("c"), f("c_ctx")
    in_maps = []
    for i in range(NCORES):
        m = dict(shared)
        m["x_p"] = xp[2 * i:2 * i + 2].reshape(NP_, D)
        m["x_s"] = xs[i]
        m["cak"] = cak[i].reshape(DEPTH, PAST, 128)
        m["cav"] = cav[i].reshape(DEPTH, PAST, 128)
        m["cnk"] = cnk[i].reshape(DEPTH, PAST, 512)
        m["cnv"] = cnv[i].reshape(DEPTH, PAST, 512)
        m["sd"] = sd[i]
        m["sh"] = sh[i]
        m["cvec"] = np.stack([cctx, c[i]], 0)
        in_maps.append(m)
    res = run_bass_kernel_spmd(nc, in_maps, core_ids=list(range(NCORES)))
    R = res.results
    cat = lambda k: np.concatenate([np.asarray(r[k]) for r in R], 0)
    y_prompt = cat("y_p").reshape(16, 256, D)
    y_sample = cat("y_s").reshape(8, 1024, D)
    nak = cat("nak").reshape(16, DEPTH, 256, 2, 64)
    nav = cat("nav").reshape(16, DEPTH, 256, 2, 64)
    nnk = cat("nnk").reshape(16, DEPTH, 256, 8, 64)
    nnv = cat("nnv").reshape(16, DEPTH, 256, 8, 64)
    nsd = cat("nsd").reshape(16, DEPTH, 2, 8, 64, 64)
    nsh = cat("nsh").reshape(16, DEPTH, 2, 8, 64, 64)
    return tuple(np.ascontiguousarray(a, dtype=np.float32) for a in (y_prompt, y_sample, nak, nav, nnk, nnv, nsd, nsh))
```
